# Optimizing a Trainium2 kernel written in Bass

```python
import jax, jax.numpy as jnp
from jax import lax
import numpy as np

D_MODEL = 1024
BATCH = 8
SEQ = 2048
DEPTH = 2

PLE_DIM = 256
HEAD_DIM = 64
SB_WIDTH = D_MODEL // 4
SB_HEADS = SB_WIDTH // HEAD_DIM
MLA_WIDTH = D_MODEL // 2
MLA_HEADS = MLA_WIDTH // HEAD_DIM
MLA_NOPE_DIM = 64
MLA_ROPE_DIM = 32
MLA_V_DIM = HEAD_DIM
MLA_Q_RANK = 384
MLA_KV_RANK = 256
CONV_WIDTH = D_MODEL // 4
CONV_K = 3
MIX_WIDTH = SB_WIDTH + MLA_WIDTH + CONV_WIDTH
IN_SIZES = (SB_WIDTH, SB_WIDTH, SB_WIDTH,
            MLA_Q_RANK, MLA_KV_RANK, MLA_ROPE_DIM,
            CONV_WIDTH, CONV_WIDTH, CONV_WIDTH)
IN_WIDTH = sum(IN_SIZES)
D_FF = 2816
Q_BLOCK = 128
ROPE_BASE = 10000.0
EPS = 1e-6
NEG_INF = -1e30

kernel_name = "hybrid_sb_mla_shortconv_macaron"


def rmsnorm(x, g):
    xf = x.astype(jnp.float32)
    y = xf * lax.rsqrt(jnp.mean(xf * xf, axis=-1, keepdims=True) + EPS)
    return (y * g.astype(jnp.float32)).astype(x.dtype)


def swiglu(x, w_gate, w_up, w_down):
    return (jax.nn.silu(x @ w_gate) * (x @ w_up)) @ w_down


def rope(x, pos):
    half = x.shape[-1] // 2
    inv = ROPE_BASE ** (-jnp.arange(half, dtype=jnp.float32) / half)
    ang = pos.astype(jnp.float32)[..., None] * inv
    cos = jnp.cos(ang)[:, :, None, :]
    sin = jnp.sin(ang)[:, :, None, :]
    xf = x.astype(jnp.float32)
    x1, x2 = xf[..., :half], xf[..., half:]
    return jnp.concatenate([x1 * cos - x2 * sin, x2 * cos + x1 * sin], axis=-1).astype(x.dtype)


def stick_breaking_attention(q, k, v):
    S = q.shape[1]
    scale = q.shape[-1] ** -0.5
    outs = []
    for start in range(0, S, Q_BLOCK):
        end = start + Q_BLOCK
        z = jnp.einsum("bqhd,bkhd->bhqk", q[:, start:end], k[:, :end],
                       preferred_element_type=jnp.float32) * scale
        t_idx = start + jnp.arange(Q_BLOCK)[:, None]
        s_idx = jnp.arange(end)[None, :]
        mask = s_idx < t_idx
        log_not = jnp.where(mask, -jax.nn.softplus(z), 0.0)
        after = lax.cumsum(log_not, axis=3, reverse=True) - log_not
        w = jnp.where(mask, jnp.exp(jax.nn.log_sigmoid(z) + after), 0.0)
        outs.append(jnp.einsum("bhqk,bkhd->bqhd", w.astype(v.dtype), v[:, :end]))
    return jnp.concatenate(outs, axis=1)


def latent_attention(c_q, c_kv, k_rope_raw, pos, g_q, g_kv, w_uq, w_ukv):
    B, S = c_q.shape[0], c_q.shape[1]
    q = (rmsnorm(c_q, g_q) @ w_uq).reshape(B, S, MLA_HEADS, MLA_NOPE_DIM + MLA_ROPE_DIM)
    q_nope, q_rope = q[..., :MLA_NOPE_DIM], rope(q[..., MLA_NOPE_DIM:], pos)
    kv = (rmsnorm(c_kv, g_kv) @ w_ukv).reshape(B, S, MLA_HEADS, MLA_NOPE_DIM + MLA_V_DIM)
    k_nope, v = kv[..., :MLA_NOPE_DIM], kv[..., MLA_NOPE_DIM:]
    k_rope = rope(k_rope_raw[:, :, None, :], pos)[:, :, 0, :]
    scale = (MLA_NOPE_DIM + MLA_ROPE_DIM) ** -0.5
    outs = []
    for start in range(0, S, Q_BLOCK):
        end = start + Q_BLOCK
        sc = (jnp.einsum("bqhd,bkhd->bhqk", q_nope[:, start:end], k_nope[:, :end],
                         preferred_element_type=jnp.float32)
              + jnp.einsum("bqhr,bkr->bhqk", q_rope[:, start:end], k_rope[:, :end],
                           preferred_element_type=jnp.float32)) * scale
        mask = jnp.arange(end)[None, :] <= (start + jnp.arange(Q_BLOCK)[:, None])
        w = jax.nn.softmax(jnp.where(mask, sc, NEG_INF), axis=-1)
        outs.append(jnp.einsum("bhqk,bkhd->bqhd", w.astype(v.dtype), v[:, :end]))
    return jnp.concatenate(outs, axis=1)


def short_gated_conv(b_gate, c_gate, h, w_conv):
    u = c_gate * h
    y = lax.conv_general_dilated(u, w_conv[:, None, :].astype(u.dtype), window_strides=(1,),
                                 padding=[(CONV_K - 1, 0)],
                                 dimension_numbers=("NWC", "WIO", "NWC"),
                                 feature_group_count=u.shape[-1])
    return b_gate * y


def setup_inputs(seed: int = 0) -> dict:
    key = jax.random.key(seed)
    ks = iter(jax.random.split(key, 32))

    def dense(shape, fan_in):
        return jax.random.normal(next(ks), shape, jnp.float32) * fan_in ** -0.5

    def gain(dim):
        return 1.0 + 0.05 * jax.random.normal(next(ks), (DEPTH, dim), jnp.float32)

    x = jax.random.normal(next(ks), (BATCH, SEQ, D_MODEL), jnp.float32)
    p = jax.random.normal(next(ks), (DEPTH, BATCH, SEQ, PLE_DIM), jnp.float32)
    offsets = jax.random.randint(next(ks), (BATCH, 1), 0, 1024, dtype=jnp.int32)
    positions = offsets + jnp.arange(SEQ, dtype=jnp.int32)[None, :]
    return {
        "x": x, "p": p, "positions": positions,
        "g_ffn1_pre": gain(D_MODEL),
        "w1_gate": dense((DEPTH, D_MODEL, D_FF), D_MODEL),
        "w1_up": dense((DEPTH, D_MODEL, D_FF), D_MODEL),
        "w1_down": dense((DEPTH, D_FF, D_MODEL), D_FF),
        "g_ffn1_post": gain(D_MODEL),
        "g_mix_pre": gain(D_MODEL),
        "w_in": dense((DEPTH, D_MODEL, IN_WIDTH), D_MODEL),
        "g_mla_q": gain(MLA_Q_RANK),
        "w_mla_uq": dense((DEPTH, MLA_Q_RANK, MLA_HEADS * (MLA_NOPE_DIM + MLA_ROPE_DIM)), MLA_Q_RANK),
        "g_mla_kv": gain(MLA_KV_RANK),
        "w_mla_ukv": dense((DEPTH, MLA_KV_RANK, MLA_HEADS * (MLA_NOPE_DIM + MLA_V_DIM)), MLA_KV_RANK),
        "w_conv": dense((DEPTH, CONV_K, CONV_WIDTH), CONV_K),
        "w_out": dense((DEPTH, MIX_WIDTH, D_MODEL), MIX_WIDTH),
        "g_mix_post": gain(D_MODEL),
        "g_ffn2_pre": gain(D_MODEL),
        "w2_gate": dense((DEPTH, D_MODEL, D_FF), D_MODEL),
        "w2_up": dense((DEPTH, D_MODEL, D_FF), D_MODEL),
        "w2_down": dense((DEPTH, D_FF, D_MODEL), D_FF),
        "g_ffn2_post": gain(D_MODEL),
        "g_ple_pre": gain(D_MODEL),
        "w_ple_gate": dense((DEPTH, D_MODEL, D_MODEL), D_MODEL),
        "w_ple_proj": dense((DEPTH, PLE_DIM, D_MODEL), PLE_DIM),
        "g_ple_post": gain(D_MODEL),
    }


def reference(x, p, positions, g_ffn1_pre, w1_gate, w1_up, w1_down, g_ffn1_post,
              g_mix_pre, w_in, g_mla_q, w_mla_uq, g_mla_kv, w_mla_ukv, w_conv, w_out,
              g_mix_post, g_ffn2_pre, w2_gate, w2_up, w2_down, g_ffn2_post,
              g_ple_pre, w_ple_gate, w_ple_proj, g_ple_post):
    B, S = x.shape[0], x.shape[1]
    split_at = [int(i) for i in np.cumsum(IN_SIZES)[:-1]]
    for i in range(DEPTH):
        h = rmsnorm(x, g_ffn1_pre[i])
        x = x + 0.5 * rmsnorm(swiglu(h, w1_gate[i], w1_up[i], w1_down[i]), g_ffn1_post[i])

        h = rmsnorm(x, g_mix_pre[i])
        (sb_q, sb_k, sb_v, c_q, c_kv, k_rope_raw,
         cv_b, cv_c, cv_h) = jnp.split(h @ w_in[i], split_at, axis=-1)
        heads = lambda t: t.reshape(B, S, SB_HEADS, HEAD_DIM)
        y_sb = stick_breaking_attention(heads(sb_q), heads(sb_k), heads(sb_v))
        y_mla = latent_attention(c_q, c_kv, k_rope_raw, positions, g_mla_q[i], g_mla_kv[i],
                                 w_mla_uq[i], w_mla_ukv[i])
        y_cv = short_gated_conv(cv_b, cv_c, cv_h, w_conv[i])
        mixed = jnp.concatenate([y_sb.reshape(B, S, SB_WIDTH),
                                 y_mla.reshape(B, S, MLA_WIDTH), y_cv], axis=-1) @ w_out[i]
        x = x + rmsnorm(mixed, g_mix_post[i])

        h = rmsnorm(x, g_ffn2_pre[i])
        x = x + 0.5 * rmsnorm(swiglu(h, w2_gate[i], w2_up[i], w2_down[i]), g_ffn2_post[i])

        h = rmsnorm(x, g_ple_pre[i])
        e = jax.nn.sigmoid(h @ w_ple_gate[i]) * (p[i].astype(x.dtype) @ w_ple_proj[i])
        x = x + rmsnorm(e, g_ple_post[i])
    return x
```

```python
import math
from contextlib import ExitStack
import numpy as np
import ml_dtypes
import concourse.bass as bass
import concourse.mybir as mybir
from concourse.bass_utils import run_bass_kernel_spmd

F32 = mybir.dt.float32
BF16 = mybir.dt.bfloat16
I32 = mybir.dt.int32
AF = mybir.ActivationFunctionType
ALU = mybir.AluOpType

S = 2048
D = 1024
DFF = 2816
NJ = DFF // 128
NTILE = S // 128
NKC = D // 128
DEPTH = 2
EPS = 1e-6
BIG = 30000.0
IN_W = 2208
MLA_SCALE = 96 ** -0.5

SBUF_BASE = 16384
import os as _os
EPOCH = int(_os.environ.get('EPOCH', '30000'))
SBUF_LIMIT = 229376


import types as _types


def _freeze(fn):
    if fn is None or fn.__closure__ is None:
        return fn
    cells = []
    for c in fn.__closure__:
        try:
            cells.append(_types.CellType(c.cell_contents))
        except ValueError:
            cells.append(c)
    g = _types.FunctionType(fn.__code__, fn.__globals__, fn.__name__, fn.__defaults__, tuple(cells))
    g.__kwdefaults__ = fn.__kwdefaults__
    return g


class Res:
    __slots__ = ("w", "r")

    def __init__(self):
        self.w = None
        self.r = {}


class DSem:
    __slots__ = ("key", "count")

    def __init__(self, key):
        self.key = key
        self.count = 0


class Prog:
    def __init__(self, nc, stack):
        self.nc = nc
        self.stack = stack
        self.h = dict(pe=nc.tensor, act=nc.scalar, dve=nc.vector, pool=nc.gpsimd, sp=nc.sync)
        self.prog = {e: [] for e in self.h}
        self.semh = []
        self.cnt = {}
        self.key = {}
        for e in ("pe", "act", "dve", "pool", "sp"):
            self._new_eng_sem(e)
        self.waited = {e: {} for e in self.h}
        self.dsems = []
        self.resd = {}
        self.pools = {"sp": [self.dsem("sp%d" % i) for i in range(8)],
                      "pool": [self.dsem("pl%d" % i) for i in range(10)]}
        self.pool_i = {"sp": 0, "pool": 0}

    def nsem(self, eng):
        p = self.pools[eng]
        d = p[self.pool_i[eng] % len(p)]
        self.pool_i[eng] += 1
        return d

    def _new_sem(self, name):
        h = self.stack.enter_context(self.nc.semaphore(name))
        self.semh.append(h)
        return len(self.semh) - 1

    def _new_eng_sem(self, e):
        self.key[e] = self._new_sem("c_%s_%d" % (e, len(self.semh)))
        self.cnt[e] = 0

    def dsem(self, name):
        d = DSem(self._new_sem("d_%s_%d" % (name, len(self.semh))))
        self.dsems.append(d)
        return d

    def res(self, *key):
        r = self.resd.get(key)
        if r is None:
            r = Res()
            self.resd[key] = r
        return r

    def op(self, eng, fn, reads=(), writes=(), dsem=None):
        deps = {}
        is_pe = eng == "pe" and dsem is None

        def need(key, val, src):
            if is_pe and src == "pe":
                return
            if deps.get(key, 0) < val:
                deps[key] = val

        for r in reads:
            if r.w is not None:
                need(*r.w)
        for w in writes:
            if w.w is not None:
                need(*w.w)
            for k, (v, s) in w.r.items():
                need(k, v, s)
        if dsem is not None and dsem.count > 0:
            need(dsem.key, dsem.count, "dma")
        wd = self.waited[eng]
        wl = []
        for k, v in deps.items():
            if wd.get(k, 0) < v:
                wd[k] = v
                wl.append((k, v))
        if dsem is not None:
            dsem.count += 16
            tok = (dsem.key, dsem.count, "dma")
            inc = (dsem.key, 16)
        elif fn is None:
            tok = None
            inc = None
        else:
            if self.cnt[eng] >= EPOCH:
                self._new_eng_sem(eng)
            self.cnt[eng] += 1
            tok = (self.key[eng], self.cnt[eng], eng)
            inc = (self.key[eng], 1)
        self.prog[eng].append((wl, _freeze(fn), inc))
        if tok is not None:
            for r in reads:
                cur = r.r.get(tok[0])
                if cur is None or cur[0] < tok[1]:
                    r.r[tok[0]] = (tok[1], tok[2])
            for w in writes:
                w.w = tok
                w.r = {}

    def barrier(self):
        targets = []
        for e in ("pe", "act", "dve", "pool"):
            if self.cnt[e] > 0:
                targets.append((self.key[e], self.cnt[e]))
        for d in self.dsems:
            if d.count > 0:
                targets.append((d.key, d.count))
        for e in self.h:
            wd = self.waited[e]
            wl = []
            for k, v in targets:
                if wd.get(k, 0) < v:
                    wd[k] = v
                    wl.append((k, v))
            if wl:
                self.prog[e].append((wl, None, None))

    def emit(self):
        semh = self.semh
        with self.nc.Block() as block:
            for e, deco in (("sp", block.sync), ("act", block.scalar), ("dve", block.vector),
                            ("pool", block.gpsimd), ("pe", block.tensor)):
                items = self.prog[e]

                def body(h, items=items):
                    for wl, fn, inc in items:
                        for k, v in wl:
                            h.wait_ge(semh[k], v)
                        if fn is not None:
                            ins = fn(h)
                            ins.then_inc(semh[inc[0]], inc[1])
                deco(body)


class Mem:
    def __init__(self, nc):
        self.nc = nc
        self.off = SBUF_BASE
        self.n = 0
        self.peak = 0

    def alloc(self, name, shape, dtype):
        esz = 2 if dtype == BF16 else 4
        size = esz
        for s in shape[1:]:
            size *= s
        size = (size + 63) // 64 * 64
        self.n += 1
        if not hasattr(self, "names"):
            self.names = {}
        self.names[name] = "%s_%d" % (name, self.n)
        h = self.nc.alloc_sbuf_tensor_at("%s_%d" % (name, self.n), list(shape), dtype, offset=self.off)
        self.off += size
        assert self.off <= SBUF_LIMIT, (name, self.off)
        self.peak = max(self.peak, self.off)
        return h

    def mark(self):
        return self.off

    def release(self, m):
        self.off = m


class Builder:
    def __init__(self, stop_after=None, ffn_nt=1024, mix_stop=99, phases=None):
        self.stop_after = stop_after
        self.phases = phases
        self.mix_stop = mix_stop
        self.ffn_nt = ffn_nt
        self.stack = ExitStack()
        nc = bass.Bass("TRN2", target_bir_lowering=False)
        self.nc = nc
        self.P = Prog(nc, self.stack)
        self.M = Mem(nc)
        self.dram = {}

    def din(self, name, shape, dtype=F32):
        ap = self.nc.dram_tensor(name, list(shape), dtype, kind="ExternalInput").ap()
        self.dram[name] = ap
        return ap

    def setup(self):
        nc, P, M = self.nc, self.P, self.M
        L = DEPTH
        self.din("x", [S, D])
        self.din("pT", [L, 256, S])
        self.din("pos", [1, S], I32)
        self.din("wgu1", [L, NJ, 128, 2, NKC, 128])
        self.din("wgu2", [L, NJ, 128, 2, NKC, 128])
        self.din("wd1", [L, DFF, D])
        self.din("wd2", [L, DFF, D])
        self.din("w_in", [L, D, IN_W])
        self.din("w_uq", [L, 384, 768])
        self.din("w_ukv", [L, 256, 1024])
        self.din("w_out", [L, D, D])
        self.din("w_pg", [L, D, D])
        self.din("w_pp", [L, 256, D])
        self.din("w_convT", [L, 128, 2, 3])
        self.din("g_q_cols", [L, 128, 3])
        self.din("g_kv_cols", [L, 128, 2])
        for g in ("g_ffn1_pre", "g_ffn1_post", "g_mix_pre", "g_mix_post", "g_ffn2_pre",
                  "g_ffn2_post", "g_ple_pre", "g_ple_post"):
            self.din(g, [L, D])
        self.din("constb", [128, 6 * 128], BF16)
        self.din("constf", [128, 132])
        self.out = nc.dram_tensor("out", [S, D], F32, kind="ExternalOutput").ap()

        self.X = M.alloc("X", [128, NTILE, D], F32)
        self.CB = M.alloc("CB", [128, 6 * 128], BF16)
        self.CF = M.alloc("CF", [128, 132], F32)
        self.SS = M.alloc("SS", [128, 64], F32)
        self.RS = M.alloc("RS", [128, 64], F32)
        self.ident = self.CB[:, 0:128]
        self.tri = self.CB[:, 128:256]
        self.ones = self.CB[:, 256:384]
        self.mbig = self.CB[:, 384:512]
        self.mneg = self.CB[:, 512:640]
        self.zeros = self.CB[:, 640:768]
        self.mlt = self.CF[:, 0:128]
        self.invf = self.CF[:, 128:129]
        self.epsc = self.CF[:, 129:130]
        self.onec = self.CF[:, 130:131]
        self.sscol = 0
        self.bank = [self.stack.enter_context(nc.psum_tensor("bank%d" % i, [128, 512], F32))
                     for i in range(8)]
        self.bres = [P.res("bank", i) for i in range(8)]
        self.XR = [P.res("X", i) for i in range(NTILE)]

        cres = P.res("const")
        d = self.dram
        P.op("sp", lambda h: h.dma_start(out=self.CB[:, :], in_=d["constb"][:, :]), writes=[cres], dsem=P.nsem("sp"))
        P.op("sp", lambda h: h.dma_start(out=self.CF[:, :], in_=d["constf"][:, :]), writes=[cres], dsem=P.nsem("sp"))
        self.cres = cres
        xv = d["x"].rearrange("(n p) d -> p n d", p=128)
        for i in range(NTILE):
            P.op("sp", lambda h, i=i: h.dma_start(out=self.X[:, i, :], in_=xv[:, i, :]),
                 writes=[self.XR[i]], dsem=P.nsem("sp"))

    def next_col(self):
        c = self.sscol
        self.sscol = (self.sscol + 1) % 64
        return c, self.P.res("ss", c), self.P.res("rs", c)

    def load_gain(self, dst, name, l, scale=None, res=None):
        P = self.P
        src = self.dram[name][l:l + 1, :].partition_broadcast(128)
        P.op("sp", lambda h: h.dma_start(out=dst[:, :], in_=src), writes=[res], dsem=P.nsem("sp"))
        if scale is not None:
            P.op("dve", lambda h: h.tensor_scalar(out=dst[:, :], in0=dst[:, :], scalar1=float(scale),
                                                  scalar2=None, op0=ALU.mult),
                 reads=[res], writes=[res])

    def rstd(self, cs, cr, ssr, rsr, scale):
        P, SS, RS = self.P, self.SS, self.RS
        P.op("act", lambda h: h.activation(out=RS[:, cr:cr + 1], in_=SS[:, cs:cs + 1], func=AF.Sqrt,
                                           scale=float(scale), bias=self.epsc),
             reads=[ssr, self.cres], writes=[rsr])
        P.op("dve", lambda h: h.reciprocal(out=RS[:, cr:cr + 1], in_=RS[:, cr:cr + 1]), reads=[rsr], writes=[rsr])

    def norm_transpose(self, tiles, G, gres, HT, htres_fn, col0, hn_bufs, junk, banks):
        P = self.P
        X = self.X
        for idx, i in enumerate(tiles):
            c, ssr, rsr = self.next_col()
            hn, hnr = hn_bufs[idx % len(hn_bufs)]
            b = banks[idx % len(banks)]
            SS, RS = self.SS, self.RS
            P.op("act", lambda h, i=i, c=c: h.activation(out=junk[:, :], in_=X[:, i, :], func=AF.Square,
                                                          accum_out=SS[:, c:c + 1]),
                 reads=[self.XR[i]], writes=[ssr, P.res("junk", id(junk))])
            self.rstd(c, c, ssr, rsr, 1.0 / D)
            P.op("dve", lambda h, i=i, c=c, hn=hn: h.scalar_tensor_tensor(
                out=hn[:, :], in0=X[:, i, :], scalar=RS[:, c:c + 1], in1=G[:, :], op0=ALU.mult, op1=ALU.mult),
                reads=[self.XR[i], rsr, gres], writes=[hnr])
            pb = self.bank[b][:, :].bitcast(BF16)
            for k in range(NKC):
                P.op("pe", lambda h, k=k, hn=hn, pb=pb: h.transpose(
                    out=pb[:, k * 128:(k + 1) * 128], in_=hn[:, k * 128:(k + 1) * 128], identity=self.ident),
                    reads=[hnr, self.cres], writes=[self.bres[b]])
            cc = col0 + idx * 128
            P.op("act", lambda h, pb=pb, cc=cc: h.activation(
                out=HT[:, :, cc:cc + 128], in_=pb.rearrange("p (k t) -> p k t", k=NKC), func=AF.Copy),
                reads=[self.bres[b]], writes=[htres_fn(cc)])

    def post_residual(self, i, halves, Gs, gres, tmp, tmpr, junk, srcs=None, sq_scale=1.0):
        P = self.P
        X, SS, RS = self.X, self.SS, self.RS
        if srcs is None:
            srcs = [(self.bank[b][:, :], self.bres[b]) for b in halves]
        cols = []
        for hf, (sap, sres) in enumerate(srcs):
            c, ssr, rsr = self.next_col()
            cols.append((c, ssr, rsr))
            P.op("act", lambda h, sap=sap, c=c: h.activation(out=junk[:, 0:512], in_=sap, func=AF.Square,
                                                              scale=float(sq_scale), accum_out=SS[:, c:c + 1]),
                 reads=[sres], writes=[ssr, P.res("junk", id(junk))])
        (c0, s0, r0), (c1, s1, r1) = cols
        P.op("dve", lambda h: h.tensor_tensor(out=SS[:, c0:c0 + 1], in0=SS[:, c0:c0 + 1], in1=SS[:, c1:c1 + 1],
                                              op=ALU.add), reads=[s0, s1], writes=[s0])
        self.rstd(c0, c0, s0, r0, 1.0 / D)
        for hf, (sap, sres) in enumerate(srcs):
            P.op("dve", lambda h, hf=hf, sap=sap: h.scalar_tensor_tensor(
                out=tmp[:, hf * 512:(hf + 1) * 512], in0=sap, scalar=RS[:, c0:c0 + 1],
                in1=Gs[:, hf * 512:(hf + 1) * 512], op0=ALU.mult, op1=ALU.mult),
                reads=[sres, r0, gres], writes=[tmpr])
        P.op("pool", lambda h: h.tensor_tensor(out=X[:, i, :], in0=X[:, i, :], in1=tmp[:, :], op=ALU.add),
             reads=[tmpr, self.XR[i]], writes=[self.XR[i]])

    def ffn(self, l, which):
        nc, P, M = self.nc, self.P, self.M
        d = self.dram
        wgu = d["wgu%d" % which]
        wd = d["wd%d" % which]
        gpre = "g_ffn%d_pre" % which
        gpost = "g_ffn%d_post" % which
        NT = self.ffn_nt
        npass = S // NT
        ntb = NT // 512
        m0 = M.mark()
        HT = M.alloc("HT", [128, NKC, NT], BF16)
        ACTT = M.alloc("ACTT", [128, NJ, NT], BF16)
        NWG = 2
        WGU = [M.alloc("WGU", [128, 2, NKC, 128], BF16) for _ in range(NWG)]
        WDF = M.alloc("WDF", [128, NJ, D], BF16)
        wgur = [P.res("wgu", which, l, s) for s in range(NWG)]
        wdr = [P.res("wd", which, l, j) for j in range(NJ)]
        wds = None
        Gpre = M.alloc("Gpre", [128, D], F32)
        Gpost = M.alloc("Gpost", [128, D], F32)
        gprer, gpostr = P.res("gpre", which, l), P.res("gpost", which, l)
        hn_bufs = [(M.alloc("hn", [128, D], BF16), P.res("hn", which, l, s)) for s in range(2)]
        junk = M.alloc("junk", [128, D], BF16)
        SG = [(M.alloc("sg", [128, 512], F32), P.res("sg", which, l, s)) for s in range(2)]
        TMP = [(M.alloc("tmp", [128, D], F32), P.res("tmp", which, l, s)) for s in range(2)]
        self.load_gain(Gpre, gpre, l, None, gprer)
        self.load_gain(Gpost, gpost, l, 0.5, gpostr)
        htr = {}
        actr = {}

        def htres(cc):
            return htr.setdefault(cc // 512, P.res("HT", which, l, cc // 512))

        for ps in range(npass):
            tiles = list(range(ps * (NT // 128), (ps + 1) * (NT // 128)))
            self.norm_transpose(tiles, Gpre, gprer, HT, htres, 0, hn_bufs, junk, [6, 7])
            for j in range(NJ):
                s = j % NWG
                P.op("pool", lambda h, j=j, s=s: h.dma_start(out=WGU[s][:, :, :, :], in_=wgu[l, j]),
                     writes=[wgur[s]], dsem=P.nsem("pool"))
                if ps == 0:
                    P.op("pool", lambda h, j=j: h.dma_start(out=WDF[:, j, :], in_=wd[l, j * 128:(j + 1) * 128, :]),
                         writes=[wdr[j]], dsem=P.nsem("pool"))
                for tb in range(ntb):
                    bg, bu = (0, 1) if (j * ntb + tb) % 2 == 0 else (2, 3)
                    for gu, b in ((0, bg), (1, bu)):
                        for k in range(NKC):
                            P.op("pe", lambda h, s=s, gu=gu, k=k, b=b, tb=tb: h.matmul(
                                self.bank[b][:, :], lhsT=WGU[s][:, gu, k, :], rhs=HT[:, k, tb * 512:(tb + 1) * 512],
                                start=(k == 0), stop=(k == NKC - 1)),
                                reads=[wgur[s], htres(tb * 512)], writes=[self.bres[b]])
                    sg, sgr = SG[(j * ntb + tb) % 2]
                    P.op("act", lambda h, sg=sg, bg=bg: h.activation(out=sg[:, :], in_=self.bank[bg][:, :],
                                                                      func=AF.Silu),
                         reads=[self.bres[bg]], writes=[sgr])
                    ar = actr.setdefault((j, tb), P.res("ACTT", which, l, j, tb))
                    P.op("dve", lambda h, sg=sg, bu=bu, j=j, tb=tb: h.tensor_tensor(
                        out=ACTT[:, j, tb * 512:(tb + 1) * 512], in0=self.bank[bu][:, :], in1=sg[:, :], op=ALU.mult),
                        reads=[self.bres[bu], sgr], writes=[ar])
            self._ffn_down(l, which, ps, tiles, ACTT, actr, wd, WDF, wdr, wds, Gpost, gpostr, TMP, junk)
        M.release(m0)
        P.barrier()

    def _ffn_down(self, l, which, ps, tiles, ACTT, actr, wd, WDF, wdr, wds, Gpost, gpostr, TMP, junk):
        P = self.P
        for idx, i in enumerate(tiles):
            halves = (4, 5) if idx % 2 == 0 else (6, 7)
            il = i - tiles[0]
            tb = il // 4
            for hf, b in enumerate(halves):
                for j in range(NJ):
                    P.op("pe", lambda h, j=j, il=il, hf=hf, b=b: h.matmul(
                        self.bank[b][:, :], lhsT=ACTT[:, j, il * 128:(il + 1) * 128],
                        rhs=WDF[:, j, hf * 512:(hf + 1) * 512], start=(j == 0), stop=(j == NJ - 1)),
                        reads=[actr[(j, tb)], wdr[j]], writes=[self.bres[b]])
            tmp, tmpr = TMP[idx % 2]
            self.post_residual(i, halves, Gpost, gpostr, tmp, tmpr, junk)

    def store_out(self):
        P = self.P
        ov = self.out.rearrange("(n p) d -> p n d", p=128)
        for i in range(NTILE):
            P.op("sp", lambda h, i=i: h.dma_start(out=ov[:, i, :], in_=self.X[:, i, :]),
                 reads=[self.XR[i]], dsem=P.nsem("sp"))
        P.prog["sp"].append(([(dd.key, dd.count) for dd in P.pools["sp"] if dd.count > 0], None, None))

    def build(self):
        self.setup()
        phases = []
        for l in range(DEPTH):
            phases += [("ffn1", l), ("mix", l), ("ffn2", l), ("ple", l)]
        if self.phases is not None:
            phases = self.phases
        for n, (ph, l) in enumerate(phases):
            if self.stop_after is not None and n >= self.stop_after:
                break
            if ph == "ffn1":
                self.ffn(l, 1)
            elif ph == "ffn2":
                self.ffn(l, 2)
            elif ph == "mix":
                self.mix(l)
            elif ph == "ple":
                self.ple(l)
        self.store_out()
        self.P.emit()
        return self.nc

    def proj_fm(self, slot, sres, col, Mrows, HT, htres, banks, evac, lhs_fn=None):
        P = self.P
        for tb in range(4):
            b = banks[tb % len(banks)]
            for k in range(NKC):
                lhs = slot[:, k, col:col + Mrows] if lhs_fn is None else lhs_fn(k)
                P.op("pe", lambda h, k=k, b=b, tb=tb, lhs=lhs: h.matmul(
                    self.bank[b][0:Mrows, :], lhsT=lhs, rhs=HT[:, k, tb * 512:(tb + 1) * 512],
                    start=(k == 0), stop=(k == NKC - 1)),
                    reads=[sres, htres(tb * 512)], writes=[self.bres[b]])
            evac(b, tb)

    def load_win(self, l, slot, sres, dsem, c0, w):
        dsem = self.P.nsem("pool")
        P = self.P
        src = self.dram["w_in"][l].rearrange("(kc p) n -> p kc n", p=128)[:, :, c0:c0 + w]
        P.op("pool", lambda h: h.dma_start(out=slot[:, :, 0:w], in_=src), writes=[sres], dsem=dsem)

    def mix(self, l):
        P, M = self.P, self.M
        d = self.dram
        bank, bres = self.bank, self.bres
        m0 = M.mark()
        MIXT = M.alloc("MIXT", [128, 8, S], BF16)
        mixr = {(c, tb): P.res("MIXT", l, c, tb) for c in range(8) for tb in range(4)}
        regB = M.mark()
        HT = M.alloc("HT", [128, NKC, S], BF16)
        RING = [M.alloc("WIN", [128, NKC, 512], BF16) for _ in range(2)]
        regC = M.mark()
        ringr = [P.res("win", l, s) for s in range(2)]
        rings = [None, None]
        htr = {}

        def htres(cc):
            return htr.setdefault(cc // 512, P.res("HT", "mix", l, cc // 512))

        Gpre = M.alloc("Gpre", [128, D], F32)
        gprer = P.res("gpre", "mix", l)
        hn_bufs = [(M.alloc("hn", [128, D], BF16), P.res("hn", "mix", l, s)) for s in range(2)]
        junk = M.alloc("junk", [128, D], BF16)
        self.load_gain(Gpre, "g_mix_pre", l, None, gprer)
        self.load_win(l, RING[0], ringr[0], rings[0], 0, 512)
        self.load_win(l, RING[1], ringr[1], rings[1], 512, 256)
        self.norm_transpose(list(range(NTILE)), Gpre, gprer, HT, htres, 0, hn_bufs, junk, [6, 7])
        M.release(regC)
        P.barrier()
        if self.mix_stop <= 0:
            M.release(m0)
            return

        mS = M.mark()
        QT = M.alloc("QT", [128, 2, S], BF16)
        KT = M.alloc("KT", [128, 2, S], BF16)
        NKT = M.alloc("NKT", [128, 2, S], BF16)
        VS = M.alloc("VS", [128, NTILE, 256], BF16)
        qr = {(c, tb): P.res("sbq", l, c, tb) for c in range(2) for tb in range(4)}
        kr_ = {(c, tb): P.res("sbk", l, c, tb) for c in range(2) for tb in range(4)}
        vr = [P.res("sbv", l, i) for i in range(NTILE)]
        for c in range(2):
            def evq(b, tb, c=c):
                P.op("act", lambda h: h.activation(out=QT[:, c, tb * 512:(tb + 1) * 512], in_=bank[b][:, :],
                                                   func=AF.Copy, scale=0.125),
                     reads=[bres[b]], writes=[qr[(c, tb)]])
            self.proj_fm(RING[0], ringr[0], c * 128, 128, HT, htres, [0, 1], evq)

            def evk(b, tb, c=c):
                P.op("act", lambda h: h.activation(out=KT[:, c, tb * 512:(tb + 1) * 512], in_=bank[b][:, :],
                                                   func=AF.Copy),
                     reads=[bres[b]], writes=[kr_[(c, tb)]])
                P.op("dve", lambda h: h.tensor_scalar(out=NKT[:, c, tb * 512:(tb + 1) * 512], in0=bank[b][:, :],
                                                      scalar1=-1.0, scalar2=None, op0=ALU.mult),
                     reads=[bres[b]], writes=[kr_[(c, tb)]])
            self.proj_fm(RING[0], ringr[0], 256 + c * 128, 128, HT, htres, [2, 3], evk)
        for i in range(NTILE):
            b = 4 + (i % 2)
            for k in range(NKC):
                P.op("pe", lambda h, k=k, i=i, b=b: h.matmul(
                    bank[b][:, 0:256], lhsT=HT[:, k, i * 128:(i + 1) * 128], rhs=RING[1][:, k, 0:256],
                    start=(k == 0), stop=(k == NKC - 1)),
                    reads=[ringr[1], htres(i * 128)], writes=[bres[b]])
            P.op("act", lambda h, i=i, b=b: h.activation(out=VS[:, i, :], in_=bank[b][:, 0:256], func=AF.Copy),
                 reads=[bres[b]], writes=[vr[i]])
        if self.mix_stop <= 1:
            M.release(m0)
            P.barrier()
            return
        self.load_win(l, RING[0], ringr[0], rings[0], 1440, 512)
        self.load_win(l, RING[1], ringr[1], rings[1], 1952, 256)

        EW = [(M.alloc("sbE", [128, 512], F32), P.res("sbE", l, s)) for s in range(2)]
        SPW = [(M.alloc("sbSP", [128, 512], F32), P.res("sbSP", l, s)) for s in range(2)]
        SPB = [(M.alloc("sbSPb", [128, 512], BF16), P.res("sbSPb", l, s)) for s in range(3)]
        RACC = [(M.alloc("sbRA", [128, 512], BF16), P.res("sbRA", l, s)) for s in range(3)]
        ATW = [(M.alloc("sbAT", [128, 512], BF16), P.res("sbAT", l, s)) for s in range(3)]
        zb = [0, 1, 6]
        cb_ = [2, 3]
        ob = [4, 5]
        gstep = 0
        for h_ in range(4):
            c, r0 = h_ // 2, (h_ % 2) * 64
            for tb in range(4):
                steps = list(range(4 * tb + 3, -1, -1))
                nst = len(steps)
                obk = ob[(h_ * 4 + tb) % 2]
                cols = slice(tb * 512, (tb + 1) * 512)
                P.op("pe", lambda h, obk=obk, c=c, cols=cols: h.matmul(
                    bank[obk][0:64, :], lhsT=self.zeros[:, 0:64], rhs=QT[:, c, cols], start=True, stop=False),
                    reads=[self.cres, qr[(c, tb)]], writes=[bres[obk]])
                for ra, rar in RACC:
                    P.op("pool", lambda h, ra=ra: h.memset(ra[:, :], 0.0), writes=[rar])

                def c0_of(st):
                    return max(0, st - 4 * tb) * 128

                def emit_z(n):
                    st = steps[n]
                    c0 = c0_of(st)
                    b = zb[(gstep + n) % 3]
                    P.op("pe", lambda h: h.matmul(
                        bank[b][:, c0:512], lhsT=KT[r0:r0 + 64, c, st * 128:(st + 1) * 128],
                        rhs=QT[r0:r0 + 64, c, tb * 512 + c0:(tb + 1) * 512], start=True, stop=True),
                        reads=[kr_[(c, st // 4)], qr[(c, tb)]], writes=[bres[b]])

                def emit_sp(n):
                    st = steps[n]
                    c0 = c0_of(st)
                    b = zb[(gstep + n) % 3]
                    e, er = EW[(gstep + n) % 2]
                    sp, spr = SPW[(gstep + n) % 2]
                    spb, spbr = SPB[(gstep + n) % 3]
                    P.op("act", lambda h: h.activation(out=e[:, c0:512], in_=bank[b][:, c0:512], func=AF.Exp),
                         reads=[bres[b]], writes=[er])
                    P.op("act", lambda h: h.activation(out=sp[:, c0:512], in_=e[:, c0:512], func=AF.Ln, bias=self.onec),
                         reads=[er, self.cres], writes=[spr])
                    diag = st >= 4 * tb
                    if diag:
                        P.op("dve", lambda h: h.tensor_tensor(out=spb[:, c0:c0 + 128], in0=sp[:, c0:c0 + 128],
                                                              in1=self.mlt, op=ALU.mult),
                             reads=[spr, self.cres], writes=[spbr])
                        if c0 + 128 < 512:
                            P.op("dve", lambda h: h.tensor_copy(out=spb[:, c0 + 128:512], in_=sp[:, c0 + 128:512]),
                                 reads=[spr], writes=[spbr])
                    else:
                        P.op("dve", lambda h: h.tensor_copy(out=spb[:, :], in_=sp[:, :]), reads=[spr], writes=[spbr])
                    if n + 1 < nst:
                        ra0, rar0 = RACC[n % 3]
                        ra1, rar1 = RACC[(n + 1) % 3]
                        P.op("pool", lambda h: h.tensor_tensor(out=ra1[:, c0:512], in0=ra0[:, c0:512],
                                                               in1=spb[:, c0:512], op=ALU.add),
                             reads=[rar0, spbr], writes=[rar1])

                def emit_cum(n):
                    st = steps[n]
                    c0 = c0_of(st)
                    b = cb_[(gstep + n) % 2]
                    spb, spbr = SPB[(gstep + n) % 3]
                    ra0, rar0 = RACC[n % 3]
                    at, atr = ATW[(gstep + n) % 3]
                    diag = st >= 4 * tb
                    P.op("pe", lambda h: h.matmul(bank[b][:, c0:512], lhsT=self.tri, rhs=spb[:, c0:512],
                                                  start=True, stop=False),
                         reads=[spbr, self.cres], writes=[bres[b]])
                    if n > 0:
                        P.op("pe", lambda h: h.matmul(bank[b][:, c0:512], lhsT=self.ones, rhs=ra0[:, c0:512],
                                                      start=False, stop=False),
                             reads=[rar0, self.cres], writes=[bres[b]])
                    P.op("pe", lambda h: h.matmul(
                        bank[b][:, c0:512], lhsT=NKT[r0:r0 + 64, c, st * 128:(st + 1) * 128],
                        rhs=QT[r0:r0 + 64, c, tb * 512 + c0:(tb + 1) * 512], start=False, stop=not diag),
                        reads=[kr_[(c, st // 4)], qr[(c, tb)]], writes=[bres[b]])
                    if diag:
                        P.op("pe", lambda h: h.matmul(bank[b][:, c0:c0 + 128], lhsT=self.ident, rhs=self.mbig,
                                                      start=False, stop=True),
                             reads=[self.cres], writes=[bres[b]])
                    P.op("act", lambda h: h.activation(out=at[:, c0:512], in_=bank[b][:, c0:512], func=AF.Exp,
                                                       scale=-1.0),
                         reads=[bres[b]], writes=[atr])

                def emit_av(n):
                    st = steps[n]
                    c0 = c0_of(st)
                    at, atr = ATW[(gstep + n) % 3]
                    P.op("pe", lambda h: h.matmul(
                        bank[obk][0:64, c0:512], lhsT=VS[:, st, h_ * 64:(h_ + 1) * 64], rhs=at[:, c0:512],
                        start=False, stop=(n == nst - 1)),
                        reads=[atr, vr[st]], writes=[bres[obk]])

                emit_z(0)
                if nst > 1:
                    emit_z(1)
                emit_sp(0)
                for n in range(nst):
                    if n + 2 < nst:
                        emit_z(n + 2)
                    if n + 1 < nst:
                        emit_sp(n + 1)
                    emit_cum(n)
                    emit_av(n)
                gstep += nst
                P.op("act", lambda h, obk=obk, c=c, r0=r0, cols=cols: h.activation(
                    out=MIXT[r0:r0 + 64, c, cols], in_=bank[obk][0:64, :], func=AF.Copy),
                    reads=[bres[obk]], writes=[mixr[(c, tb)]])
        M.release(mS)
        P.barrier()
        if self.mix_stop <= 2:
            M.release(m0)
            return

        mC = M.mark()
        WC = M.alloc("WC", [128, 2, 3], F32)
        wcr = P.res("wc", l)
        P.op("sp", lambda h: h.dma_start(out=WC[:, :, :], in_=d["w_convT"][l]), writes=[wcr], dsem=P.nsem("sp"))
        U = M.alloc("cvU", [128, S + 2], F32)
        ur = P.res("cvU", l)
        BS = M.alloc("cvB", [128, S], F32)
        bsr = P.res("cvB", l)
        ACC = M.alloc("cvA", [128, S], F32)
        accr = P.res("cvA", l)
        CS = [(M.alloc("cvC", [128, 512], F32), P.res("cvC", l, s)) for s in range(2)]
        P.op("dve", lambda h: h.memset(U[:, 0:2], 0.0), writes=[ur])
        for cc in range(2):
            for tb in range(4):
                bb, bc, bh = (0, 1, 2) if tb % 2 == 0 else (3, 4, 5)
                for (b, slot, sres, col) in ((bb, RING[0], ringr[0], cc * 128), (bc, RING[0], ringr[0], 256 + cc * 128),
                                             (bh, RING[1], ringr[1], cc * 128)):
                    for k in range(NKC):
                        P.op("pe", lambda h, k=k, b=b, slot=slot, col=col, tb=tb: h.matmul(
                            bank[b][:, :], lhsT=slot[:, k, col:col + 128], rhs=HT[:, k, tb * 512:(tb + 1) * 512],
                            start=(k == 0), stop=(k == NKC - 1)),
                            reads=[sres, htres(tb * 512)], writes=[bres[b]])
                cs, csr = CS[tb % 2]
                P.op("act", lambda h, cs=cs, bc=bc: h.activation(out=cs[:, :], in_=bank[bc][:, :], func=AF.Copy),
                     reads=[bres[bc]], writes=[csr])
                P.op("dve", lambda h, cs=cs, bh=bh, tb=tb: h.tensor_tensor(
                    out=U[:, 2 + tb * 512:2 + (tb + 1) * 512], in0=bank[bh][:, :], in1=cs[:, :], op=ALU.mult),
                    reads=[bres[bh], csr], writes=[ur])
                P.op("act", lambda h, bb=bb, tb=tb: h.activation(out=BS[:, tb * 512:(tb + 1) * 512], in_=bank[bb][:, :],
                                                                  func=AF.Copy),
                     reads=[bres[bb]], writes=[bsr])
            P.op("dve", lambda h, cc=cc: h.tensor_scalar(out=ACC[:, :], in0=U[:, 2:S + 2], scalar1=WC[:, cc, 2:3],
                                                         scalar2=None, op0=ALU.mult),
                 reads=[ur, wcr], writes=[accr])
            P.op("dve", lambda h, cc=cc: h.scalar_tensor_tensor(out=ACC[:, :], in0=U[:, 1:S + 1], scalar=WC[:, cc, 1:2],
                                                                in1=ACC[:, :], op0=ALU.mult, op1=ALU.add),
                 reads=[ur, wcr, accr], writes=[accr])
            P.op("dve", lambda h, cc=cc: h.scalar_tensor_tensor(out=ACC[:, :], in0=U[:, 0:S], scalar=WC[:, cc, 0:1],
                                                                in1=ACC[:, :], op0=ALU.mult, op1=ALU.add),
                 reads=[ur, wcr, accr], writes=[accr])
            P.op("dve", lambda h, cc=cc: h.tensor_tensor(out=MIXT[:, 6 + cc, :], in0=ACC[:, :], in1=BS[:, :], op=ALU.mult),
                 reads=[accr, bsr], writes=[mixr[(6 + cc, tb)] for tb in range(4)])
        M.release(mC)
        self.load_win(l, RING[0], ringr[0], rings[0], 768, 384)
        self.load_win(l, RING[1], ringr[1], rings[1], 1152, 288)
        P.barrier()
        if self.mix_stop <= 3:
            M.release(m0)
            return

        mM = M.mark()
        C1 = M.alloc("C1", [128, S], F32)
        S1 = M.alloc("S1", [128, S], F32)
        tabr = P.res("ropetab", l)
        CQT = M.alloc("CQT", [128, 3, S], BF16)
        CKVT = M.alloc("CKVT", [128, 2, S], BF16)
        KROT = M.alloc("KROT", [128, S], BF16)
        cqr = [P.res("cqt", l, tb) for tb in range(4)]
        ckvr = [P.res("ckvt", l, tb) for tb in range(4)]
        krr = [P.res("krot", l, tb) for tb in range(4)]
        mT = M.mark()
        HW = S // 2
        TI = M.alloc("TI", [128, HW], I32)
        ANG = M.alloc("ANG", [128, HW], F32)
        TQ = M.alloc("TQ", [128, HW], F32)
        tr = P.res("ropetmp", l)
        P.op("dve", lambda h: h.memset(C1[0:64, :], 1.0), writes=[tabr])
        P.op("dve", lambda h: h.memset(S1[0:64, :], 0.0), writes=[tabr])
        for hv in range(2):
            hc = slice(hv * HW, (hv + 1) * HW)
            P.op("sp", lambda h, hc=hc: h.dma_start(out=TI[64:96, :], in_=d["pos"][0:1, hc].partition_broadcast(32)),
                 writes=[tr], dsem=P.nsem("sp"))
            P.op("dve", lambda h: h.tensor_copy(out=ANG[64:96, :], in_=TI[64:96, :]), reads=[tr], writes=[tr])
            P.op("dve", lambda h: h.tensor_scalar(out=ANG[64:96, :], in0=ANG[64:96, :], scalar1=self.invf[64:96, :],
                                                  scalar2=None, op0=ALU.mult), reads=[tr, self.cres], writes=[tr])
            for (dst, shift) in ((S1, 0.0), (C1, math.pi / 2)):
                P.op("dve", lambda h, shift=shift: h.tensor_scalar(
                    out=TQ[64:96, :], in0=ANG[64:96, :], scalar1=float(shift), scalar2=1.0 / (2 * math.pi),
                    op0=ALU.add, op1=ALU.mult), reads=[tr], writes=[tr])
                P.op("dve", lambda h: h.tensor_copy(out=TI[64:96, :], in_=TQ[64:96, :]), reads=[tr], writes=[tr])
                P.op("dve", lambda h: h.tensor_copy(out=TQ[64:96, :], in_=TI[64:96, :]), reads=[tr], writes=[tr])
                P.op("dve", lambda h: h.scalar_tensor_tensor(
                    out=TQ[64:96, :], in0=TQ[64:96, :], scalar=-2 * math.pi, in1=ANG[64:96, :],
                    op0=ALU.mult, op1=ALU.add), reads=[tr], writes=[tr])
                P.op("dve", lambda h, shift=shift: h.tensor_scalar(
                    out=TQ[64:96, :], in0=TQ[64:96, :], scalar1=float(shift), scalar2=3.141592,
                    op0=ALU.add, op1=ALU.min), reads=[tr], writes=[tr])
                P.op("dve", lambda h: h.tensor_scalar(
                    out=TQ[64:96, :], in0=TQ[64:96, :], scalar1=-3.141592, scalar2=None, op0=ALU.max),
                    reads=[tr], writes=[tr])
                P.op("act", lambda h, dst=dst, hc=hc: h.activation(out=dst[64:96, hc], in_=TQ[64:96, :], func=AF.Sin),
                     reads=[tr], writes=[tabr, tr])
        M.release(mT)
        P.barrier()
        if self.mix_stop <= 4:
            M.release(m0)
            return

        mI = M.mark()
        CF_ = M.alloc("cF", [128, 3, 512], F32)
        cfr = P.res("cF", l)
        SQ = M.alloc("cSQ", [128, 3, 512], BF16)
        sqr = P.res("cSQ", l)
        RST = M.alloc("cRST", [128, 512], F32)
        rstr = P.res("cRST", l)
        KRW = M.alloc("KRW", [128, NKC, 2, 96], BF16)
        krwr = P.res("KRW", l)
        T1 = M.alloc("kT1", [128, 512], F32)
        T2 = M.alloc("kT2", [128, 512], F32)
        t1r, t2r = P.res("kT1", l), P.res("kT2", l)
        P.op("dve", lambda h: h.memset(KRW[:, :, :, :], 0.0), writes=[krwr])
        P.op("dve", lambda h: h.tensor_copy(out=KRW[:, :, 0, 64:96], in_=RING[1][:, :, 256:288]),
             reads=[ringr[1]], writes=[krwr])
        P.op("dve", lambda h: h.tensor_scalar(out=KRW[:, :, 1, 64:80], in0=RING[1][:, :, 272:288], scalar1=-1.0,
                                              scalar2=None, op0=ALU.mult), reads=[ringr[1]], writes=[krwr])
        P.op("dve", lambda h: h.tensor_copy(out=KRW[:, :, 1, 80:96], in_=RING[1][:, :, 256:272]),
             reads=[ringr[1]], writes=[krwr])
        for (nch, slot, sres, dst, dres, dim) in ((3, RING[0], ringr[0], CQT, cqr, 384),
                                                  (2, RING[1], ringr[1], CKVT, ckvr, 256)):
            for tb in range(4):
                for cidx in range(nch):
                    b = cidx
                    for k in range(NKC):
                        P.op("pe", lambda h, k=k, b=b, slot=slot, cidx=cidx, tb=tb: h.matmul(
                            bank[b][:, :], lhsT=slot[:, k, cidx * 128:(cidx + 1) * 128],
                            rhs=HT[:, k, tb * 512:(tb + 1) * 512], start=(k == 0), stop=(k == NKC - 1)),
                            reads=[sres, htres(tb * 512)], writes=[bres[b]])
                    P.op("act", lambda h, b=b, cidx=cidx: h.activation(out=CF_[:, cidx, :], in_=bank[b][:, :],
                                                                        func=AF.Copy),
                         reads=[bres[b]], writes=[cfr])
                    P.op("act", lambda h, b=b, cidx=cidx: h.activation(out=SQ[:, cidx, :], in_=bank[b][:, :],
                                                                        func=AF.Square),
                         reads=[bres[b]], writes=[sqr])
                for cidx in range(nch):
                    P.op("pe", lambda h, cidx=cidx, nch=nch: h.matmul(
                        bank[3][:, :], lhsT=self.ones, rhs=SQ[:, cidx, :], start=(cidx == 0), stop=(cidx == nch - 1)),
                        reads=[sqr, self.cres], writes=[bres[3]])
                P.op("act", lambda h, dim=dim: h.activation(out=RST[:, :], in_=bank[3][:, :], func=AF.Sqrt,
                                                            scale=1.0 / dim, bias=self.epsc),
                     reads=[bres[3], self.cres], writes=[rstr])
                P.op("dve", lambda h: h.reciprocal(out=RST[:, :], in_=RST[:, :]), reads=[rstr], writes=[rstr])
                for cidx in range(nch):
                    P.op("dve", lambda h, cidx=cidx, dst=dst, tb=tb: h.tensor_tensor(
                        out=dst[:, cidx, tb * 512:(tb + 1) * 512], in0=CF_[:, cidx, :], in1=RST[:, :], op=ALU.mult),
                        reads=[cfr, rstr], writes=[dres[tb]])
        for tb in range(4):
            for v_, b in ((0, 4), (1, 5)):
                for k in range(NKC):
                    P.op("pe", lambda h, k=k, b=b, v_=v_, tb=tb: h.matmul(
                        bank[b][0:96, :], lhsT=KRW[:, k, v_, :], rhs=HT[:, k, tb * 512:(tb + 1) * 512],
                        start=(k == 0), stop=(k == NKC - 1)),
                        reads=[krwr, htres(tb * 512)], writes=[bres[b]])
            P.op("dve", lambda h, tb=tb: h.tensor_tensor(out=T1[64:96, :], in0=bank[4][64:96, :],
                                                         in1=C1[64:96, tb * 512:(tb + 1) * 512], op=ALU.mult),
                 reads=[bres[4], tabr], writes=[t1r])
            P.op("dve", lambda h, tb=tb: h.tensor_tensor(out=T2[64:96, :], in0=bank[5][64:96, :],
                                                         in1=S1[64:96, tb * 512:(tb + 1) * 512], op=ALU.mult),
                 reads=[bres[5], tabr], writes=[t2r])
            P.op("dve", lambda h, tb=tb: h.tensor_tensor(out=KROT[64:96, tb * 512:(tb + 1) * 512], in0=T1[64:96, :],
                                                         in1=T2[64:96, :], op=ALU.add),
                 reads=[t1r, t2r], writes=[krr[tb]])
        M.release(mI)
        P.barrier()
        if self.mix_stop <= 5:
            M.release(m0)
            return
        self.mla_attn(l, MIXT, mixr, C1, S1, tabr, CQT, CKVT, KROT, cqr, ckvr, krr, regB, mT)
        M.release(regB)
        P.barrier()
        import os
        if self.mix_stop <= 6 or int(os.environ.get('MLADBG', '99')) < 99:
            M.release(m0)
            return

        WOUT = M.alloc("WOUT", [128, NKC, D], BF16)
        wor = [P.res("wout", l, k) for k in range(NKC)]
        for k in range(NKC):
            P.op("pool", lambda h, k=k: h.dma_start(out=WOUT[:, k, 0:D], in_=d["w_out"][l, k * 128:(k + 1) * 128, :]),
                 writes=[wor[k]], dsem=P.nsem("pool"))
        Gpost = M.alloc("Gpost", [128, D], F32)
        gpostr = P.res("gpost", "mix", l)
        self.load_gain(Gpost, "g_mix_post", l, None, gpostr)
        junk2 = M.alloc("junk", [128, D], BF16)
        TMP = [(M.alloc("tmp", [128, D], F32), P.res("tmp", "mix", l, s)) for s in range(2)]
        for i in range(NTILE):
            halves = (0, 1) if i % 2 == 0 else (2, 3)
            for hf, b in enumerate(halves):
                for cidx in range(8):
                    P.op("pe", lambda h, cidx=cidx, hf=hf, b=b, i=i: h.matmul(
                        bank[b][:, :], lhsT=MIXT[:, cidx, i * 128:(i + 1) * 128],
                        rhs=WOUT[:, cidx, hf * 512:(hf + 1) * 512], start=(cidx == 0), stop=(cidx == 7)),
                        reads=[mixr[(cidx, i // 4)], wor[cidx]], writes=[bres[b]])
            tmp, tmpr = TMP[i % 2]
            self.post_residual(i, halves, Gpost, gpostr, tmp, tmpr, junk2)
        M.release(m0)
        P.barrier()

    def mla_attn(self, l, MIXT, mixr, C1, S1, tabr, CQT, CKVT, KROT, cqr, ckvr, krr, regB, regC2):
        P, M = self.P, self.M
        d = self.dram
        bank, bres = self.bank, self.bres
        M.release(regC2)
        GQ = M.alloc("GQ", [128, 3], F32)
        GKV = M.alloc("GKV", [128, 2], F32)
        T1 = M.alloc("qT1", [128, 512], F32)
        T2 = M.alloc("qT2", [128, 512], F32)
        t1r, t2r = P.res("qT1", l), P.res("qT2", l)
        PTW = [(M.alloc("PT", [128, 512], BF16), P.res("PTw", l, s)) for s in range(3)]
        RC = M.alloc("RC", [128, 512], F32)
        rcr = P.res("RC", l)
        M.release(regB)
        WUQ = M.alloc("WUQ", [128, 3, 768], BF16)
        WUQR = M.alloc("WUQR", [128, 3, 8, 96], BF16)
        WUKV = M.alloc("WUKV", [128, 2, 1024], BF16)
        mq = M.mark()
        STQ = M.alloc("STQ", [128, 3, 768], F32)
        STK = M.alloc("STK", [128, 2, 1024], F32)
        M.release(mq)
        VO = M.alloc("VO", [128, NTILE, 4, 128], BF16)
        QH = [(M.alloc("QH", [128, S], BF16), [P.res("QH", l, s, tb) for tb in range(4)]) for s in range(2)]
        KH = [(M.alloc("KH", [128, S], BF16), [P.res("KH", l, s, tb) for tb in range(4)]) for s in range(2)]
        wr = P.res("mlaw", l)
        wgq, wgkv = P.res("mlagq", l), P.res("mlagkv", l)
        wsq = [P.res("mlasq", l, c) for c in range(3)]
        wsk = [P.res("mlask", l, c) for c in range(2)]
        P.op("sp", lambda h: h.dma_start(out=GQ[:, :], in_=d["g_q_cols"][l]), writes=[wgq], dsem=P.nsem("sp"))
        P.op("sp", lambda h: h.dma_start(out=GKV[:, :], in_=d["g_kv_cols"][l]), writes=[wgkv], dsem=P.nsem("sp"))
        for c in range(3):
            P.op("sp", lambda h, c=c: h.dma_start(out=STQ[:, c, :], in_=d["w_uq"][l, c * 128:(c + 1) * 128, :]),
                 writes=[wsq[c]], dsem=P.nsem("sp"))
        for c in range(2):
            P.op("sp", lambda h, c=c: h.dma_start(out=STK[:, c, :], in_=d["w_ukv"][l, c * 128:(c + 1) * 128, :]),
                 writes=[wsk[c]], dsem=P.nsem("sp"))
        P.op("dve", lambda h: h.memset(WUQR[:, :, :, :], 0.0), writes=[wr])
        for c in range(3):
            P.op("dve", lambda h, c=c: h.tensor_scalar(out=WUQ[:, c, :], in0=STQ[:, c, :], scalar1=GQ[:, c:c + 1],
                                                       scalar2=None, op0=ALU.mult), reads=[wsq[c], wgq], writes=[wr])
            sv = STQ[:, c, :].rearrange("p (h x) -> p h x", x=96)
            P.op("dve", lambda h, c=c, sv=sv: h.tensor_scalar(out=WUQR[:, c, :, 64:80], in0=sv[:, :, 80:96],
                                                              scalar1=GQ[:, c:c + 1], scalar2=-1.0, op0=ALU.mult,
                                                              op1=ALU.mult), reads=[wsq[c], wgq, wr], writes=[wr])
            P.op("dve", lambda h, c=c, sv=sv: h.tensor_scalar(out=WUQR[:, c, :, 80:96], in0=sv[:, :, 64:80],
                                                              scalar1=GQ[:, c:c + 1], scalar2=None, op0=ALU.mult),
                 reads=[wsq[c], wgq, wr], writes=[wr])
        for c in range(2):
            P.op("dve", lambda h, c=c: h.tensor_scalar(out=WUKV[:, c, :], in0=STK[:, c, :], scalar1=GKV[:, c:c + 1],
                                                       scalar2=None, op0=ALU.mult), reads=[wsk[c], wgkv], writes=[wr])
        P.barrier()
        import os
        dbg = int(os.environ.get('MLADBG', '99'))
        if dbg <= 0:
            return
        vor = [P.res("VO", l, st) for st in range(NTILE)]
        P.op("pool", lambda h: h.memset(VO[:, :, :, :], 0.0), writes=vor)
        wkv4 = [WUKV[:, c, :].rearrange("p (h x) -> p h x", x=128) for c in range(2)]
        sb_ = [0, 1]
        nub = [2, 3]
        deb = [4, 5]
        upb = [6, 7, 6]
        gstep = 0
        for hg in range(2):
            for st in range(NTILE):
                b = upb[st % 2]
                for c in range(2):
                    P.op("pe", lambda h, c=c, st=st, b=b: h.matmul(
                        bank[b][:, 0:256], lhsT=CKVT[:, c, st * 128:(st + 1) * 128],
                        rhs=wkv4[c][:, hg * 4:(hg + 1) * 4, 64:128], start=(c == 0), stop=(c == 1)),
                        reads=[ckvr[st // 4], wr], writes=[bres[b]])
                for hh in range(4):
                    par = hh % 2
                    P.op("act", lambda h, st=st, b=b, par=par, hh=hh: h.activation(
                        out=VO[:, st, hh, par * 64:(par + 1) * 64], in_=bank[b][:, hh * 64:(hh + 1) * 64],
                        func=AF.Copy), reads=[bres[b]], writes=[vor[st]])
            if dbg <= 1:
                return
            for hl in range(4):
                h_ = hg * 4 + hl
                qh, qhr = QH[h_ % 2]
                kh, khr = KH[h_ % 2]
                for tb in range(4):
                    cols = slice(tb * 512, (tb + 1) * 512)
                    ba, bb_, bk = upb
                    for c in range(2):
                        P.op("pe", lambda h, c=c, h_=h_, cols=cols, bk=bk: h.matmul(
                            bank[bk][0:64, :], lhsT=WUKV[:, c, h_ * 128:h_ * 128 + 64], rhs=CKVT[:, c, cols],
                            start=(c == 0), stop=(c == 1)), reads=[wr, ckvr[tb]], writes=[bres[bk]])
                    P.op("act", lambda h, cols=cols, kh=kh, bk=bk: h.activation(out=kh[0:64, cols], in_=bank[bk][0:64, :],
                                                                                func=AF.Copy),
                         reads=[bres[bk]], writes=[khr[tb]])
                    for c in range(3):
                        P.op("pe", lambda h, c=c, h_=h_, cols=cols, ba=ba: h.matmul(
                            bank[ba][0:96, :], lhsT=WUQ[:, c, h_ * 96:(h_ + 1) * 96], rhs=CQT[:, c, cols],
                            start=(c == 0), stop=(c == 2)), reads=[wr, cqr[tb]], writes=[bres[ba]])
                    for c in range(3):
                        P.op("pe", lambda h, c=c, h_=h_, cols=cols, bb_=bb_: h.matmul(
                            bank[bb_][0:96, :], lhsT=WUQR[:, c, h_, :], rhs=CQT[:, c, cols],
                            start=(c == 0), stop=(c == 2)), reads=[wr, cqr[tb]], writes=[bres[bb_]])
                    P.op("dve", lambda h, cols=cols, ba=ba: h.tensor_tensor(out=T1[0:96, :], in0=bank[ba][0:96, :],
                                                                            in1=C1[0:96, cols], op=ALU.mult),
                         reads=[bres[ba], tabr], writes=[t1r])
                    P.op("dve", lambda h, cols=cols, bb_=bb_: h.tensor_tensor(out=T2[0:96, :], in0=bank[bb_][0:96, :],
                                                                              in1=S1[0:96, cols], op=ALU.mult),
                         reads=[bres[bb_], tabr], writes=[t2r])
                    P.op("pool", lambda h, cols=cols, qh=qh: h.tensor_tensor(out=qh[0:96, cols], in0=T1[0:96, :],
                                                                             in1=T2[0:96, :], op=ALU.add),
                         reads=[t1r, t2r], writes=[qhr[tb]])
                    P.op("pool", lambda h, cols=cols, kh=kh: h.tensor_copy(out=kh[64:96, cols], in_=KROT[64:96, cols]),
                         reads=[krr[tb]], writes=[khr[tb]])
                if dbg <= 2:
                    return
                cchunk, r0 = 2 + h_ // 2, (h_ % 2) * 64
                for tb in range(4):
                    steps = list(range(0, 4 * tb + 4))
                    nst = len(steps)
                    ob = nub[(h_ * 4 + tb) % 2]
                    od = deb[(h_ * 4 + tb) % 2]

                    def c0_of(st):
                        return max(0, st - 4 * tb) * 128

                    def emit_s(n):
                        st = steps[n]
                        c0 = c0_of(st)
                        b = sb_[(gstep + n) % 2]
                        diag = st >= 4 * tb
                        P.op("pe", lambda h: h.matmul(
                            bank[b][:, c0:512], lhsT=kh[0:96, st * 128:(st + 1) * 128],
                            rhs=qh[0:96, tb * 512 + c0:(tb + 1) * 512], start=True, stop=not diag),
                            reads=[khr[st // 4], qhr[tb]], writes=[bres[b]])
                        if diag:
                            P.op("pe", lambda h: h.matmul(bank[b][:, c0:c0 + 128], lhsT=self.ident, rhs=self.mneg,
                                                          start=False, stop=True),
                                 reads=[self.cres], writes=[bres[b]])

                    def emit_p(n):
                        st = steps[n]
                        c0 = c0_of(st)
                        b = sb_[(gstep + n) % 2]
                        pt, ptr = PTW[(gstep + n) % 3]
                        P.op("act", lambda h: h.activation(out=pt[:, c0:512], in_=bank[b][:, c0:512], func=AF.Exp,
                                                           scale=MLA_SCALE), reads=[bres[b]], writes=[ptr])

                    def emit_av(n):
                        st = steps[n]
                        c0 = c0_of(st)
                        pt, ptr = PTW[(gstep + n) % 3]
                        P.op("pe", lambda h: h.matmul(bank[ob][:, c0:512], lhsT=VO[:, st, hl, :], rhs=pt[:, c0:512],
                                                      start=(n == 0), stop=(n == nst - 1)),
                             reads=[ptr, vor[st]], writes=[bres[ob]])
                        P.op("pe", lambda h: h.matmul(bank[od][:, c0:512], lhsT=self.ones, rhs=pt[:, c0:512],
                                                      start=(n == 0), stop=(n == nst - 1)),
                             reads=[ptr, self.cres], writes=[bres[od]])

                    emit_s(0)
                    emit_s(1)
                    emit_p(0)
                    for n in range(nst):
                        if n + 2 < nst:
                            emit_s(n + 2)
                        if n + 1 < nst:
                            emit_p(n + 1)
                        emit_av(n)
                    gstep += nst
                    if dbg <= 3:
                        return
                    if dbg == 5 and (h_, tb) == (0, 1):
                        return
                    if dbg == 6 and (h_, tb) == (1, 0):
                        return
                    if dbg == 7 and (h_, tb) == (4, 0):
                        return
                    cols = slice(tb * 512, (tb + 1) * 512)
                    P.op("dve", lambda h, od=od, r0=r0: h.reciprocal(out=RC[r0:r0 + 64, :], in_=bank[od][r0:r0 + 64, :]),
                         reads=[bres[od]], writes=[rcr])
                    P.op("dve", lambda h, ob=ob, cols=cols, cchunk=cchunk, r0=r0: h.tensor_tensor(
                        out=MIXT[r0:r0 + 64, cchunk, cols], in0=bank[ob][r0:r0 + 64, :], in1=RC[r0:r0 + 64, :], op=ALU.mult),
                        reads=[bres[ob], rcr], writes=[mixr[(cchunk, tb)]])

    def ple(self, l):
        P, M = self.P, self.M
        d = self.dram
        m0 = M.mark()
        HT = M.alloc("HT", [128, NKC, S], BF16)
        WPG = M.alloc("WPG", [128, NKC, D], BF16)
        WPP = M.alloc("WPP", [128, 2, D], BF16)
        PT = M.alloc("PT", [128, 2, S], BF16)
        Gpre = M.alloc("Gpre", [128, D], F32)
        Gpost = M.alloc("Gpost", [128, D], F32)
        gprer, gpostr = P.res("gpre", "ple", l), P.res("gpost", "ple", l)
        hn_bufs = [(M.alloc("hn", [128, D], BF16), P.res("hn", "ple", l, s)) for s in range(2)]
        junk = M.alloc("junk", [128, D], BF16)
        SIG = [(M.alloc("sig", [128, D], F32), P.res("sig", l, s)) for s in range(2)]
        EE = [(M.alloc("ee", [128, D], F32), P.res("ee", l, s)) for s in range(2)]
        TMP = [(M.alloc("tmp", [128, D], F32), P.res("tmp", "ple", l, s)) for s in range(2)]
        self.load_gain(Gpre, "g_ple_pre", l, None, gprer)
        self.load_gain(Gpost, "g_ple_post", l, None, gpostr)
        wgr = [P.res("wpg", l, k) for k in range(NKC)]
        wpr = [P.res("wpp", l, k) for k in range(2)]
        ptr_ = [P.res("ptT", l, k) for k in range(2)]
        for k in range(NKC):
            P.op("pool", lambda h, k=k: h.dma_start(out=WPG[:, k, :], in_=d["w_pg"][l, k * 128:(k + 1) * 128, :]),
                 writes=[wgr[k]], dsem=P.nsem("pool"))
        for k in range(2):
            P.op("pool", lambda h, k=k: h.dma_start(out=WPP[:, k, :], in_=d["w_pp"][l, k * 128:(k + 1) * 128, :]),
                 writes=[wpr[k]], dsem=P.nsem("pool"))
            P.op("pool", lambda h, k=k: h.dma_start(out=PT[:, k, :], in_=d["pT"][l, k * 128:(k + 1) * 128, :]),
                 writes=[ptr_[k]], dsem=P.nsem("pool"))
        htr = {}

        def htres(cc):
            return htr.setdefault(cc // 512, P.res("HT", "ple", l, cc // 512))

        tiles = list(range(NTILE))
        self.norm_transpose(tiles, Gpre, gprer, HT, htres, 0, hn_bufs, junk, [6, 7])
        for i in tiles:
            gb = (0, 1) if i % 2 == 0 else (2, 3)
            pb = (4, 5)
            sig, sigr = SIG[i % 2]
            ee, eer = EE[i % 2]
            for hf in range(2):
                for k in range(NKC):
                    P.op("pe", lambda h, k=k, hf=hf, i=i, b=gb[hf]: h.matmul(
                        self.bank[b][:, :], lhsT=HT[:, k, i * 128:(i + 1) * 128], rhs=WPG[:, k, hf * 512:(hf + 1) * 512],
                        start=(k == 0), stop=(k == NKC - 1)),
                        reads=[htres(i * 128), wgr[k]], writes=[self.bres[gb[hf]]])
                for k in range(2):
                    P.op("pe", lambda h, k=k, hf=hf, i=i, b=pb[hf]: h.matmul(
                        self.bank[b][:, :], lhsT=PT[:, k, i * 128:(i + 1) * 128], rhs=WPP[:, k, hf * 512:(hf + 1) * 512],
                        start=(k == 0), stop=(k == 1)),
                        reads=[wpr[k], ptr_[k]], writes=[self.bres[pb[hf]]])
                P.op("act", lambda h, hf=hf, sig=sig, b=gb[hf]: h.activation(
                    out=sig[:, hf * 512:(hf + 1) * 512], in_=self.bank[b][:, :], func=AF.Sigmoid),
                    reads=[self.bres[gb[hf]]], writes=[sigr])
                P.op("dve", lambda h, hf=hf, sig=sig, ee=ee, b=pb[hf]: h.tensor_tensor(
                    out=ee[:, hf * 512:(hf + 1) * 512], in0=self.bank[b][:, :], in1=sig[:, hf * 512:(hf + 1) * 512],
                    op=ALU.mult), reads=[self.bres[pb[hf]], sigr], writes=[eer])
            tmp, tmpr = TMP[i % 2]
            self.post_residual(i, None, Gpost, gpostr, tmp, tmpr, junk,
                               srcs=[(ee[:, 0:512], eer), (ee[:, 512:1024], eer)])
        M.release(m0)
        P.barrier()


def make_consts():
    j = np.arange(128)[:, None]
    s = np.arange(128)[None, :]
    cb = np.zeros((128, 6 * 128), np.float32)
    cb[:, 0:128] = np.eye(128)
    cb[:, 128:256] = (j >= s)
    cb[:, 256:384] = 1.0
    cb[:, 384:512] = np.where(j >= s, BIG, 0.0)
    cb[:, 512:640] = np.where(j > s, -BIG, 0.0)
    cf = np.zeros((128, 132), np.float32)
    cf[:, 0:128] = (j < s)
    inv = 10000.0 ** (-np.arange(16, dtype=np.float32) / 16.0)
    for p in range(64, 96):
        cf[p, 128] = inv[(p - 64) % 16]
    cf[:, 129] = EPS
    cf[:, 130] = 1.0
    return cb.astype(ml_dtypes.bfloat16), cf


def host_layout(inputs):
    f = lambda a: np.ascontiguousarray(np.asarray(a))
    sh = {}

    def gu(wg, wu):
        wg = np.asarray(wg).reshape(DEPTH, NKC, 128, NJ, 128)
        wu = np.asarray(wu).reshape(DEPTH, NKC, 128, NJ, 128)
        st = np.stack([wg, wu], axis=0)
        return np.ascontiguousarray(st.transpose(1, 4, 3, 0, 2, 5))

    sh["wgu1"] = gu(inputs["w1_gate"], inputs["w1_up"])
    sh["wgu2"] = gu(inputs["w2_gate"], inputs["w2_up"])
    sh["wd1"] = f(inputs["w1_down"])
    sh["wd2"] = f(inputs["w2_down"])
    sh["w_in"] = f(inputs["w_in"])
    sh["w_uq"] = f(inputs["w_mla_uq"])
    sh["w_ukv"] = f(inputs["w_mla_ukv"])
    sh["w_out"] = f(inputs["w_out"])
    sh["w_pg"] = f(inputs["w_ple_gate"])
    sh["w_pp"] = f(inputs["w_ple_proj"])
    wc = np.asarray(inputs["w_conv"])
    sh["w_convT"] = f(wc.reshape(DEPTH, 3, 2, 128).transpose(0, 3, 2, 1))
    sh["g_q_cols"] = f(np.asarray(inputs["g_mla_q"]).reshape(DEPTH, 3, 128).transpose(0, 2, 1))
    sh["g_kv_cols"] = f(np.asarray(inputs["g_mla_kv"]).reshape(DEPTH, 2, 128).transpose(0, 2, 1))
    for g in ("g_ffn1_pre", "g_ffn1_post", "g_mix_pre", "g_mix_post", "g_ffn2_pre",
              "g_ffn2_post", "g_ple_pre", "g_ple_post"):
        sh[g] = f(inputs[g])
    cb, cf = make_consts()
    sh["constb"] = cb
    sh["constf"] = cf
    x = np.asarray(inputs["x"])
    p = np.asarray(inputs["p"])
    pos = np.asarray(inputs["positions"])
    per = []
    for b in range(x.shape[0]):
        per.append({
            "x": f(x[b]),
            "pT": f(p[:, b].transpose(0, 2, 1)),
            "pos": f(pos[b:b + 1]),
        })
    return sh, per


_NC_CACHE = {}


def kernel(**inputs):
    sh, per = host_layout(inputs)
    if "nc" not in _NC_CACHE:
        _NC_CACHE["nc"] = Builder().build()
    nc = _NC_CACHE["nc"]
    in_maps = [dict(sh, **pc) for pc in per]
    res = run_bass_kernel_spmd(nc, in_maps, core_ids=list(range(len(per))))
    return np.stack([r["out"] for r in res.results], axis=0).astype(np.float32)
```

```python
import math
from contextlib import ExitStack
import numpy as np
import ml_dtypes
import concourse.bass as bass
import concourse.mybir as mybir
from concourse.bass_utils import run_bass_kernel_spmd

F32 = mybir.dt.float32
BF16 = mybir.dt.bfloat16
I32 = mybir.dt.int32
AF = mybir.ActivationFunctionType
ALU = mybir.AluOpType

S = 2048
D = 1024
DFF = 2816
NJ = DFF // 128
NTILE = S // 128
NKC = D // 128
DEPTH = 2
EPS = 1e-6
BIG = 30000.0
IN_W = 2208
MLA_SCALE = 96 ** -0.5

SBUF_BASE = 16384
import os as _os
EPOCH = int(_os.environ.get('EPOCH', '4000'))
EMBED = int(_os.environ.get('EMBED', '1'))
SBUF_LIMIT = 229376


import types as _types


def _freeze(fn):
    if fn is None or fn.__closure__ is None:
        return fn
    cells = []
    for c in fn.__closure__:
        try:
            cells.append(_types.CellType(c.cell_contents))
        except ValueError:
            cells.append(c)
    g = _types.FunctionType(fn.__code__, fn.__globals__, fn.__name__, fn.__defaults__, tuple(cells))
    g.__kwdefaults__ = fn.__kwdefaults__
    return g


class Res:
    __slots__ = ("w", "r")

    def __init__(self):
        self.w = None
        self.r = {}


class DSem:
    __slots__ = ("key", "count")

    def __init__(self, key):
        self.key = key
        self.count = 0


class Prog:
    def __init__(self, nc, stack):
        self.nc = nc
        self.stack = stack
        self.h = dict(pe=nc.tensor, act=nc.scalar, dve=nc.vector, pool=nc.gpsimd, sp=nc.sync)
        self.prog = {e: [] for e in self.h}
        self.semh = []
        self.cnt = {}
        self.key = {}
        for e in ("pe", "act", "dve", "pool", "sp"):
            self._new_eng_sem(e)
        self.waited = {e: {} for e in self.h}
        self.dsems = []
        self.resd = {}
        self.pools = {"sp": [self.dsem("sp%d" % i) for i in range(8)],
                      "pool": [self.dsem("pl%d" % i) for i in range(10)]}
        self.pool_i = {"sp": 0, "pool": 0}

    def nsem(self, eng):
        p = self.pools[eng]
        d = p[self.pool_i[eng] % len(p)]
        self.pool_i[eng] += 1
        return d

    def _new_sem(self, name):
        h = self.stack.enter_context(self.nc.semaphore(name))
        self.semh.append(h)
        return len(self.semh) - 1

    def _new_eng_sem(self, e):
        self.key[e] = self._new_sem("c_%s_%d" % (e, len(self.semh)))
        self.cnt[e] = 0

    def dsem(self, name):
        d = DSem(self._new_sem("d_%s_%d" % (name, len(self.semh))))
        self.dsems.append(d)
        return d

    def res(self, *key):
        r = self.resd.get(key)
        if r is None:
            r = Res()
            self.resd[key] = r
        return r

    def op(self, eng, fn, reads=(), writes=(), dsem=None):
        deps = {}
        is_pe = eng == "pe" and dsem is None

        def need(key, val, src):
            if is_pe and src == "pe":
                return
            if deps.get(key, 0) < val:
                deps[key] = val

        for r in reads:
            if r.w is not None:
                need(*r.w)
        for w in writes:
            if w.w is not None:
                need(*w.w)
            for k, (v, s) in w.r.items():
                need(k, v, s)
        if dsem is not None and dsem.count > 0:
            need(dsem.key, dsem.count, "dma")
        wd = self.waited[eng]
        wl = []
        for k, v in deps.items():
            if wd.get(k, 0) < v:
                wd[k] = v
                wl.append((k, v))
        if dsem is not None:
            dsem.count += 16
            tok = (dsem.key, dsem.count, "dma")
            inc = (dsem.key, 16)
        elif fn is None:
            tok = None
            inc = None
        else:
            if self.cnt[eng] >= EPOCH:
                self._new_eng_sem(eng)
            self.cnt[eng] += 1
            tok = (self.key[eng], self.cnt[eng], eng)
            inc = (self.key[eng], 1)
        self.prog[eng].append((wl, _freeze(fn), inc, dsem is not None))
        if tok is not None:
            for r in reads:
                cur = r.r.get(tok[0])
                if cur is None or cur[0] < tok[1]:
                    r.r[tok[0]] = (tok[1], tok[2])
            for w in writes:
                w.w = tok
                w.r = {}

    def barrier(self):
        targets = []
        for e in ("pe", "act", "dve", "pool"):
            if self.cnt[e] > 0:
                targets.append((self.key[e], self.cnt[e]))
        for d in self.dsems:
            if d.count > 0:
                targets.append((d.key, d.count))
        for e in self.h:
            wd = self.waited[e]
            wl = []
            for k, v in targets:
                if wd.get(k, 0) < v:
                    wd[k] = v
                    wl.append((k, v))
            if wl:
                self.prog[e].append((wl, None, None, False))

    def emit(self):
        semh = self.semh
        with self.nc.Block() as block:
            for e, deco in (("sp", block.sync), ("act", block.scalar), ("dve", block.vector),
                            ("pool", block.gpsimd), ("pe", block.tensor)):
                items = self.prog[e]

                def body(h, items=items):
                    for wl, fn, inc, is_dma in items:
                        emb = None
                        if EMBED and fn is not None and wl and not is_dma:
                            emb = wl[-1]
                            wl = wl[:-1]
                        for k, v in wl:
                            h.wait_ge(semh[k], v)
                        if fn is not None:
                            ins = fn(h)
                            if emb is not None:
                                ins._wait_ge(semh[emb[0]], emb[1])
                            ins.then_inc(semh[inc[0]], inc[1])
                deco(body)


class Mem:
    def __init__(self, nc):
        self.nc = nc
        self.off = SBUF_BASE
        self.n = 0
        self.peak = 0

    def alloc(self, name, shape, dtype):
        esz = 2 if dtype == BF16 else 4
        size = esz
        for s in shape[1:]:
            size *= s
        size = (size + 63) // 64 * 64
        self.n += 1
        if not hasattr(self, "names"):
            self.names = {}
        self.names[name] = "%s_%d" % (name, self.n)
        h = self.nc.alloc_sbuf_tensor_at("%s_%d" % (name, self.n), list(shape), dtype, offset=self.off)
        self.off += size
        assert self.off <= SBUF_LIMIT, (name, self.off)
        self.peak = max(self.peak, self.off)
        return h

    def mark(self):
        return self.off

    def release(self, m):
        self.off = m


class Builder:
    def __init__(self, stop_after=None, ffn_nt=1024, mix_stop=99, phases=None):
        self.stop_after = stop_after
        self.phases = phases
        self.mix_stop = mix_stop
        self.ffn_nt = ffn_nt
        self.stack = ExitStack()
        nc = bass.Bass("TRN2", target_bir_lowering=False)
        self.nc = nc
        self.P = Prog(nc, self.stack)
        self.M = Mem(nc)
        self.dram = {}

    def din(self, name, shape, dtype=F32):
        ap = self.nc.dram_tensor(name, list(shape), dtype, kind="ExternalInput").ap()
        self.dram[name] = ap
        return ap

    def setup(self):
        nc, P, M = self.nc, self.P, self.M
        L = DEPTH
        self.din("x", [S, D])
        self.din("pT", [L, 256, S])
        self.din("pos", [1, S], I32)
        self.din("wgu1", [L, NJ, 128, 2, NKC, 128])
        self.din("wgu2", [L, NJ, 128, 2, NKC, 128])
        self.din("wd1", [L, DFF, D])
        self.din("wd2", [L, DFF, D])
        self.din("w_in", [L, D, IN_W])
        self.din("w_uq", [L, 384, 768])
        self.din("w_ukv", [L, 256, 1024])
        self.din("w_out", [L, D, D])
        self.din("w_pg", [L, D, D])
        self.din("w_pp", [L, 256, D])
        self.din("w_convT", [L, 128, 2, 3])
        self.din("g_q_cols", [L, 128, 3])
        self.din("g_kv_cols", [L, 128, 2])
        for g in ("g_ffn1_pre", "g_ffn1_post", "g_mix_pre", "g_mix_post", "g_ffn2_pre",
                  "g_ffn2_post", "g_ple_pre", "g_ple_post"):
            self.din(g, [L, D])
        self.din("constb", [128, 6 * 128], BF16)
        self.din("constf", [128, 132])
        self.out = nc.dram_tensor("out", [S, D], F32, kind="ExternalOutput").ap()

        self.X = M.alloc("X", [128, NTILE, D], F32)
        self.CB = M.alloc("CB", [128, 6 * 128], BF16)
        self.CF = M.alloc("CF", [128, 132], F32)
        self.SS = M.alloc("SS", [128, 64], F32)
        self.RS = M.alloc("RS", [128, 64], F32)
        self.ident = self.CB[:, 0:128]
        self.tri = self.CB[:, 128:256]
        self.ones = self.CB[:, 256:384]
        self.mbig = self.CB[:, 384:512]
        self.mneg = self.CB[:, 512:640]
        self.zeros = self.CB[:, 640:768]
        self.mlt = self.CF[:, 0:128]
        self.invf = self.CF[:, 128:129]
        self.epsc = self.CF[:, 129:130]
        self.onec = self.CF[:, 130:131]
        self.sscol = 0
        self.bank = [self.stack.enter_context(nc.psum_tensor("bank%d" % i, [128, 512], F32))
                     for i in range(8)]
        self.bres = [P.res("bank", i) for i in range(8)]
        self.XR = [P.res("X", i) for i in range(NTILE)]

        cres = P.res("const")
        d = self.dram
        P.op("sp", lambda h: h.dma_start(out=self.CB[:, :], in_=d["constb"][:, :]), writes=[cres], dsem=P.nsem("sp"))
        P.op("sp", lambda h: h.dma_start(out=self.CF[:, :], in_=d["constf"][:, :]), writes=[cres], dsem=P.nsem("sp"))
        self.cres = cres
        xv = d["x"].rearrange("(n p) d -> p n d", p=128)
        for i in range(NTILE):
            P.op("sp", lambda h, i=i: h.dma_start(out=self.X[:, i, :], in_=xv[:, i, :]),
                 writes=[self.XR[i]], dsem=P.nsem("sp"))

    def next_col(self):
        c = self.sscol
        self.sscol = (self.sscol + 1) % 64
        return c, self.P.res("ss", c), self.P.res("rs", c)

    def load_gain(self, dst, name, l, scale=None, res=None):
        P = self.P
        src = self.dram[name][l:l + 1, :].partition_broadcast(128)
        P.op("sp", lambda h: h.dma_start(out=dst[:, :], in_=src), writes=[res], dsem=P.nsem("sp"))
        if scale is not None:
            P.op("dve", lambda h: h.tensor_scalar(out=dst[:, :], in0=dst[:, :], scalar1=float(scale),
                                                  scalar2=None, op0=ALU.mult),
                 reads=[res], writes=[res])

    def rstd(self, cs, cr, ssr, rsr, scale):
        P, SS, RS = self.P, self.SS, self.RS
        P.op("act", lambda h: h.activation(out=RS[:, cr:cr + 1], in_=SS[:, cs:cs + 1], func=AF.Sqrt,
                                           scale=float(scale), bias=self.epsc),
             reads=[ssr, self.cres], writes=[rsr])
        P.op("dve", lambda h: h.reciprocal(out=RS[:, cr:cr + 1], in_=RS[:, cr:cr + 1]), reads=[rsr], writes=[rsr])

    def norm_transpose(self, tiles, G, gres, HT, htres_fn, col0, hn_bufs, junk, banks):
        P = self.P
        X = self.X
        for idx, i in enumerate(tiles):
            c, ssr, rsr = self.next_col()
            hn, hnr = hn_bufs[idx % len(hn_bufs)]
            b = banks[idx % len(banks)]
            SS, RS = self.SS, self.RS
            P.op("act", lambda h, i=i, c=c: h.activation(out=junk[:, :], in_=X[:, i, :], func=AF.Square,
                                                          accum_out=SS[:, c:c + 1]),
                 reads=[self.XR[i]], writes=[ssr, P.res("junk", id(junk))])
            self.rstd(c, c, ssr, rsr, 1.0 / D)
            P.op("dve", lambda h, i=i, c=c, hn=hn: h.scalar_tensor_tensor(
                out=hn[:, :], in0=X[:, i, :], scalar=RS[:, c:c + 1], in1=G[:, :], op0=ALU.mult, op1=ALU.mult),
                reads=[self.XR[i], rsr, gres], writes=[hnr])
            pb = self.bank[b][:, :].bitcast(BF16)
            for k in range(NKC):
                P.op("pe", lambda h, k=k, hn=hn, pb=pb: h.transpose(
                    out=pb[:, k * 128:(k + 1) * 128], in_=hn[:, k * 128:(k + 1) * 128], identity=self.ident),
                    reads=[hnr, self.cres], writes=[self.bres[b]])
            cc = col0 + idx * 128
            P.op("act", lambda h, pb=pb, cc=cc: h.activation(
                out=HT[:, :, cc:cc + 128], in_=pb.rearrange("p (k t) -> p k t", k=NKC), func=AF.Copy),
                reads=[self.bres[b]], writes=[htres_fn(cc)])

    def post_residual(self, i, halves, Gs, gres, tmp, tmpr, junk, srcs=None, sq_scale=1.0):
        P = self.P
        X, SS, RS = self.X, self.SS, self.RS
        if srcs is None:
            srcs = [(self.bank[b][:, :], self.bres[b]) for b in halves]
        cols = []
        for hf, (sap, sres) in enumerate(srcs):
            c, ssr, rsr = self.next_col()
            cols.append((c, ssr, rsr))
            P.op("act", lambda h, sap=sap, c=c: h.activation(out=junk[:, 0:512], in_=sap, func=AF.Square,
                                                              scale=float(sq_scale), accum_out=SS[:, c:c + 1]),
                 reads=[sres], writes=[ssr, P.res("junk", id(junk))])
        (c0, s0, r0), (c1, s1, r1) = cols
        P.op("dve", lambda h: h.tensor_tensor(out=SS[:, c0:c0 + 1], in0=SS[:, c0:c0 + 1], in1=SS[:, c1:c1 + 1],
                                              op=ALU.add), reads=[s0, s1], writes=[s0])
        self.rstd(c0, c0, s0, r0, 1.0 / D)
        for hf, (sap, sres) in enumerate(srcs):
            P.op("dve", lambda h, hf=hf, sap=sap: h.scalar_tensor_tensor(
                out=tmp[:, hf * 512:(hf + 1) * 512], in0=sap, scalar=RS[:, c0:c0 + 1],
                in1=Gs[:, hf * 512:(hf + 1) * 512], op0=ALU.mult, op1=ALU.mult),
                reads=[sres, r0, gres], writes=[tmpr])
        P.op("pool", lambda h: h.tensor_tensor(out=X[:, i, :], in0=X[:, i, :], in1=tmp[:, :], op=ALU.add),
             reads=[tmpr, self.XR[i]], writes=[self.XR[i]])

    def ffn(self, l, which):
        nc, P, M = self.nc, self.P, self.M
        d = self.dram
        wgu = d["wgu%d" % which]
        wd = d["wd%d" % which]
        gpre = "g_ffn%d_pre" % which
        gpost = "g_ffn%d_post" % which
        NT = self.ffn_nt
        npass = S // NT
        ntb = NT // 512
        m0 = M.mark()
        HT = M.alloc("HT", [128, NKC, NT], BF16)
        ACTT = M.alloc("ACTT", [128, NJ, NT], BF16)
        NWG = 2
        WGU = [M.alloc("WGU", [128, 2, NKC, 128], BF16) for _ in range(NWG)]
        WDF = M.alloc("WDF", [128, NJ, D], BF16)
        wgur = [P.res("wgu", which, l, s) for s in range(NWG)]
        wdr = [P.res("wd", which, l, j) for j in range(NJ)]
        wds = None
        Gpre = M.alloc("Gpre", [128, D], F32)
        Gpost = M.alloc("Gpost", [128, D], F32)
        gprer, gpostr = P.res("gpre", which, l), P.res("gpost", which, l)
        hn_bufs = [(M.alloc("hn", [128, D], BF16), P.res("hn", which, l, s)) for s in range(2)]
        junk = M.alloc("junk", [128, D], BF16)
        SG = [(M.alloc("sg", [128, 512], F32), P.res("sg", which, l, s)) for s in range(2)]
        TMP = [(M.alloc("tmp", [128, D], F32), P.res("tmp", which, l, s)) for s in range(2)]
        self.load_gain(Gpre, gpre, l, None, gprer)
        self.load_gain(Gpost, gpost, l, 0.5, gpostr)
        htr = {}
        actr = {}

        def htres(cc):
            return htr.setdefault(cc // 512, P.res("HT", which, l, cc // 512))

        for ps in range(npass):
            tiles = list(range(ps * (NT // 128), (ps + 1) * (NT // 128)))
            self.norm_transpose(tiles, Gpre, gprer, HT, htres, 0, hn_bufs, junk, [6, 7])
            for j in range(NJ):
                s = j % NWG
                P.op("pool", lambda h, j=j, s=s: h.dma_start(out=WGU[s][:, :, :, :], in_=wgu[l, j]),
                     writes=[wgur[s]], dsem=P.nsem("pool"))
                if ps == 0:
                    P.op("pool", lambda h, j=j: h.dma_start(out=WDF[:, j, :], in_=wd[l, j * 128:(j + 1) * 128, :]),
                         writes=[wdr[j]], dsem=P.nsem("pool"))
                for tb in range(ntb):
                    bg, bu = (0, 1) if (j * ntb + tb) % 2 == 0 else (2, 3)
                    for gu, b in ((0, bg), (1, bu)):
                        for k in range(NKC):
                            P.op("pe", lambda h, s=s, gu=gu, k=k, b=b, tb=tb: h.matmul(
                                self.bank[b][:, :], lhsT=WGU[s][:, gu, k, :], rhs=HT[:, k, tb * 512:(tb + 1) * 512],
                                start=(k == 0), stop=(k == NKC - 1)),
                                reads=[wgur[s], htres(tb * 512)], writes=[self.bres[b]])
                    sg, sgr = SG[(j * ntb + tb) % 2]
                    P.op("act", lambda h, sg=sg, bg=bg: h.activation(out=sg[:, :], in_=self.bank[bg][:, :],
                                                                      func=AF.Silu),
                         reads=[self.bres[bg]], writes=[sgr])
                    ar = actr.setdefault((j, tb), P.res("ACTT", which, l, j, tb))
                    P.op("dve", lambda h, sg=sg, bu=bu, j=j, tb=tb: h.tensor_tensor(
                        out=ACTT[:, j, tb * 512:(tb + 1) * 512], in0=self.bank[bu][:, :], in1=sg[:, :], op=ALU.mult),
                        reads=[self.bres[bu], sgr], writes=[ar])
            self._ffn_down(l, which, ps, tiles, ACTT, actr, wd, WDF, wdr, wds, Gpost, gpostr, TMP, junk)
        M.release(m0)
        P.barrier()

    def _ffn_down(self, l, which, ps, tiles, ACTT, actr, wd, WDF, wdr, wds, Gpost, gpostr, TMP, junk):
        P = self.P
        for idx, i in enumerate(tiles):
            halves = (4, 5) if idx % 2 == 0 else (6, 7)
            il = i - tiles[0]
            tb = il // 4
            for hf, b in enumerate(halves):
                for j in range(NJ):
                    P.op("pe", lambda h, j=j, il=il, hf=hf, b=b: h.matmul(
                        self.bank[b][:, :], lhsT=ACTT[:, j, il * 128:(il + 1) * 128],
                        rhs=WDF[:, j, hf * 512:(hf + 1) * 512], start=(j == 0), stop=(j == NJ - 1)),
                        reads=[actr[(j, tb)], wdr[j]], writes=[self.bres[b]])
            tmp, tmpr = TMP[idx % 2]
            self.post_residual(i, halves, Gpost, gpostr, tmp, tmpr, junk)

    def store_out(self):
        P = self.P
        ov = self.out.rearrange("(n p) d -> p n d", p=128)
        for i in range(NTILE):
            P.op("sp", lambda h, i=i: h.dma_start(out=ov[:, i, :], in_=self.X[:, i, :]),
                 reads=[self.XR[i]], dsem=P.nsem("sp"))
        P.prog["sp"].append(([(dd.key, dd.count) for dd in P.pools["sp"] if dd.count > 0], None, None, False))

    def build(self):
        self.setup()
        phases = []
        for l in range(DEPTH):
            phases += [("ffn1", l), ("mix", l), ("ffn2", l), ("ple", l)]
        if self.phases is not None:
            phases = self.phases
        for n, (ph, l) in enumerate(phases):
            if self.stop_after is not None and n >= self.stop_after:
                break
            if ph == "ffn1":
                self.ffn(l, 1)
            elif ph == "ffn2":
                self.ffn(l, 2)
            elif ph == "mix":
                self.mix(l)
            elif ph == "ple":
                self.ple(l)
        self.store_out()
        self.P.emit()
        return self.nc

    def proj_fm(self, slot, sres, col, Mrows, HT, htres, banks, evac, lhs_fn=None):
        P = self.P
        for tb in range(4):
            b = banks[tb % len(banks)]
            for k in range(NKC):
                lhs = slot[:, k, col:col + Mrows] if lhs_fn is None else lhs_fn(k)
                P.op("pe", lambda h, k=k, b=b, tb=tb, lhs=lhs: h.matmul(
                    self.bank[b][0:Mrows, :], lhsT=lhs, rhs=HT[:, k, tb * 512:(tb + 1) * 512],
                    start=(k == 0), stop=(k == NKC - 1)),
                    reads=[sres, htres(tb * 512)], writes=[self.bres[b]])
            evac(b, tb)

    def load_win(self, l, slot, sres, dsem, c0, w):
        dsem = self.P.nsem("pool")
        P = self.P
        src = self.dram["w_in"][l].rearrange("(kc p) n -> p kc n", p=128)[:, :, c0:c0 + w]
        P.op("pool", lambda h: h.dma_start(out=slot[:, :, 0:w], in_=src), writes=[sres], dsem=dsem)

    def mix(self, l):
        P, M = self.P, self.M
        d = self.dram
        bank, bres = self.bank, self.bres
        m0 = M.mark()
        MIXT = M.alloc("MIXT", [128, 8, S], BF16)
        mixr = {(c, tb): P.res("MIXT", l, c, tb) for c in range(8) for tb in range(4)}
        regB = M.mark()
        HT = M.alloc("HT", [128, NKC, S], BF16)
        RING = [M.alloc("WIN", [128, NKC, 512], BF16) for _ in range(2)]
        regC = M.mark()
        ringr = [P.res("win", l, s) for s in range(2)]
        rings = [None, None]
        htr = {}

        def htres(cc):
            return htr.setdefault(cc // 512, P.res("HT", "mix", l, cc // 512))

        Gpre = M.alloc("Gpre", [128, D], F32)
        gprer = P.res("gpre", "mix", l)
        hn_bufs = [(M.alloc("hn", [128, D], BF16), P.res("hn", "mix", l, s)) for s in range(2)]
        junk = M.alloc("junk", [128, D], BF16)
        self.load_gain(Gpre, "g_mix_pre", l, None, gprer)
        self.load_win(l, RING[0], ringr[0], rings[0], 0, 512)
        self.load_win(l, RING[1], ringr[1], rings[1], 512, 256)
        self.norm_transpose(list(range(NTILE)), Gpre, gprer, HT, htres, 0, hn_bufs, junk, [6, 7])
        M.release(regC)
        P.barrier()
        if self.mix_stop <= 0:
            M.release(m0)
            return

        mS = M.mark()
        QT = M.alloc("QT", [128, 2, S], BF16)
        KT = M.alloc("KT", [128, 2, S], BF16)
        NKT = M.alloc("NKT", [128, 2, S], BF16)
        VS = M.alloc("VS", [128, NTILE, 256], BF16)
        qr = {(c, tb): P.res("sbq", l, c, tb) for c in range(2) for tb in range(4)}
        kr_ = {(c, tb): P.res("sbk", l, c, tb) for c in range(2) for tb in range(4)}
        vr = [P.res("sbv", l, i) for i in range(NTILE)]
        for c in range(2):
            def evq(b, tb, c=c):
                P.op("act", lambda h: h.activation(out=QT[:, c, tb * 512:(tb + 1) * 512], in_=bank[b][:, :],
                                                   func=AF.Copy, scale=0.125),
                     reads=[bres[b]], writes=[qr[(c, tb)]])
            self.proj_fm(RING[0], ringr[0], c * 128, 128, HT, htres, [0, 1], evq)

            def evk(b, tb, c=c):
                P.op("act", lambda h: h.activation(out=KT[:, c, tb * 512:(tb + 1) * 512], in_=bank[b][:, :],
                                                   func=AF.Copy),
                     reads=[bres[b]], writes=[kr_[(c, tb)]])
                P.op("dve", lambda h: h.tensor_scalar(out=NKT[:, c, tb * 512:(tb + 1) * 512], in0=bank[b][:, :],
                                                      scalar1=-1.0, scalar2=None, op0=ALU.mult),
                     reads=[bres[b]], writes=[kr_[(c, tb)]])
            self.proj_fm(RING[0], ringr[0], 256 + c * 128, 128, HT, htres, [2, 3], evk)
        for i in range(NTILE):
            b = 4 + (i % 2)
            for k in range(NKC):
                P.op("pe", lambda h, k=k, i=i, b=b: h.matmul(
                    bank[b][:, 0:256], lhsT=HT[:, k, i * 128:(i + 1) * 128], rhs=RING[1][:, k, 0:256],
                    start=(k == 0), stop=(k == NKC - 1)),
                    reads=[ringr[1], htres(i * 128)], writes=[bres[b]])
            P.op("act", lambda h, i=i, b=b: h.activation(out=VS[:, i, :], in_=bank[b][:, 0:256], func=AF.Copy),
                 reads=[bres[b]], writes=[vr[i]])
        if self.mix_stop <= 1:
            M.release(m0)
            P.barrier()
            return
        self.load_win(l, RING[0], ringr[0], rings[0], 1440, 512)
        self.load_win(l, RING[1], ringr[1], rings[1], 1952, 256)

        EW = [(M.alloc("sbE", [128, 512], F32), P.res("sbE", l, s)) for s in range(2)]
        SPW = [(M.alloc("sbSP", [128, 512], F32), P.res("sbSP", l, s)) for s in range(2)]
        SPB = [(M.alloc("sbSPb", [128, 512], BF16), P.res("sbSPb", l, s)) for s in range(3)]
        RACC = [(M.alloc("sbRA", [128, 512], BF16), P.res("sbRA", l, s)) for s in range(3)]
        ATW = [(M.alloc("sbAT", [128, 512], BF16), P.res("sbAT", l, s)) for s in range(3)]
        zb = [0, 1, 6]
        cb_ = [2, 3]
        ob = [4, 5]
        gstep = 0
        for h_ in range(4):
            c, r0 = h_ // 2, (h_ % 2) * 64
            for tb in range(4):
                steps = list(range(4 * tb + 3, -1, -1))
                nst = len(steps)
                obk = ob[(h_ * 4 + tb) % 2]
                cols = slice(tb * 512, (tb + 1) * 512)
                P.op("pe", lambda h, obk=obk, c=c, cols=cols: h.matmul(
                    bank[obk][0:64, :], lhsT=self.zeros[:, 0:64], rhs=QT[:, c, cols], start=True, stop=False),
                    reads=[self.cres, qr[(c, tb)]], writes=[bres[obk]])
                for ra, rar in RACC:
                    P.op("pool", lambda h, ra=ra: h.memset(ra[:, :], 0.0), writes=[rar])

                def c0_of(st):
                    return max(0, st - 4 * tb) * 128

                def emit_z(n):
                    st = steps[n]
                    c0 = c0_of(st)
                    b = zb[(gstep + n) % 3]
                    P.op("pe", lambda h: h.matmul(
                        bank[b][:, c0:512], lhsT=KT[r0:r0 + 64, c, st * 128:(st + 1) * 128],
                        rhs=QT[r0:r0 + 64, c, tb * 512 + c0:(tb + 1) * 512], start=True, stop=True),
                        reads=[kr_[(c, st // 4)], qr[(c, tb)]], writes=[bres[b]])

                def emit_sp(n):
                    st = steps[n]
                    c0 = c0_of(st)
                    b = zb[(gstep + n) % 3]
                    e, er = EW[(gstep + n) % 2]
                    sp, spr = SPW[(gstep + n) % 2]
                    spb, spbr = SPB[(gstep + n) % 3]
                    P.op("act", lambda h: h.activation(out=e[:, c0:512], in_=bank[b][:, c0:512], func=AF.Exp),
                         reads=[bres[b]], writes=[er])
                    P.op("act", lambda h: h.activation(out=sp[:, c0:512], in_=e[:, c0:512], func=AF.Ln, bias=self.onec),
                         reads=[er, self.cres], writes=[spr])
                    diag = st >= 4 * tb
                    if diag:
                        P.op("dve", lambda h: h.tensor_tensor(out=spb[:, c0:c0 + 128], in0=sp[:, c0:c0 + 128],
                                                              in1=self.mlt, op=ALU.mult),
                             reads=[spr, self.cres], writes=[spbr])
                        if c0 + 128 < 512:
                            P.op("dve", lambda h: h.tensor_copy(out=spb[:, c0 + 128:512], in_=sp[:, c0 + 128:512]),
                                 reads=[spr], writes=[spbr])
                    else:
                        P.op("dve", lambda h: h.tensor_copy(out=spb[:, :], in_=sp[:, :]), reads=[spr], writes=[spbr])
                    if n + 1 < nst:
                        ra0, rar0 = RACC[n % 3]
                        ra1, rar1 = RACC[(n + 1) % 3]
                        P.op("pool", lambda h: h.tensor_tensor(out=ra1[:, c0:512], in0=ra0[:, c0:512],
                                                               in1=spb[:, c0:512], op=ALU.add),
                             reads=[rar0, spbr], writes=[rar1])

                def emit_cum(n):
                    st = steps[n]
                    c0 = c0_of(st)
                    b = cb_[(gstep + n) % 2]
                    spb, spbr = SPB[(gstep + n) % 3]
                    ra0, rar0 = RACC[n % 3]
                    at, atr = ATW[(gstep + n) % 3]
                    diag = st >= 4 * tb
                    P.op("pe", lambda h: h.matmul(bank[b][:, c0:512], lhsT=self.tri, rhs=spb[:, c0:512],
                                                  start=True, stop=False),
                         reads=[spbr, self.cres], writes=[bres[b]])
                    if n > 0:
                        P.op("pe", lambda h: h.matmul(bank[b][:, c0:512], lhsT=self.ones, rhs=ra0[:, c0:512],
                                                      start=False, stop=False),
                             reads=[rar0, self.cres], writes=[bres[b]])
                    P.op("pe", lambda h: h.matmul(
                        bank[b][:, c0:512], lhsT=NKT[r0:r0 + 64, c, st * 128:(st + 1) * 128],
                        rhs=QT[r0:r0 + 64, c, tb * 512 + c0:(tb + 1) * 512], start=False, stop=not diag),
                        reads=[kr_[(c, st // 4)], qr[(c, tb)]], writes=[bres[b]])
                    if diag:
                        P.op("pe", lambda h: h.matmul(bank[b][:, c0:c0 + 128], lhsT=self.ident, rhs=self.mbig,
                                                      start=False, stop=True),
                             reads=[self.cres], writes=[bres[b]])
                    P.op("act", lambda h: h.activation(out=at[:, c0:512], in_=bank[b][:, c0:512], func=AF.Exp,
                                                       scale=-1.0),
                         reads=[bres[b]], writes=[atr])

                def emit_av(n):
                    st = steps[n]
                    c0 = c0_of(st)
                    at, atr = ATW[(gstep + n) % 3]
                    P.op("pe", lambda h: h.matmul(
                        bank[obk][0:64, c0:512], lhsT=VS[:, st, h_ * 64:(h_ + 1) * 64], rhs=at[:, c0:512],
                        start=False, stop=(n == nst - 1)),
                        reads=[atr, vr[st]], writes=[bres[obk]])

                emit_z(0)
                if nst > 1:
                    emit_z(1)
                emit_sp(0)
                for n in range(nst):
                    if n + 2 < nst:
                        emit_z(n + 2)
                    if n + 1 < nst:
                        emit_sp(n + 1)
                    emit_cum(n)
                    emit_av(n)
                gstep += nst
                P.op("act", lambda h, obk=obk, c=c, r0=r0, cols=cols: h.activation(
                    out=MIXT[r0:r0 + 64, c, cols], in_=bank[obk][0:64, :], func=AF.Copy),
                    reads=[bres[obk]], writes=[mixr[(c, tb)]])
        M.release(mS)
        P.barrier()
        if self.mix_stop <= 2:
            M.release(m0)
            return

        mC = M.mark()
        WC = M.alloc("WC", [128, 2, 3], F32)
        wcr = P.res("wc", l)
        P.op("sp", lambda h: h.dma_start(out=WC[:, :, :], in_=d["w_convT"][l]), writes=[wcr], dsem=P.nsem("sp"))
        U = M.alloc("cvU", [128, S + 2], F32)
        ur = P.res("cvU", l)
        BS = M.alloc("cvB", [128, S], F32)
        bsr = P.res("cvB", l)
        ACC = M.alloc("cvA", [128, S], F32)
        accr = P.res("cvA", l)
        CS = [(M.alloc("cvC", [128, 512], F32), P.res("cvC", l, s)) for s in range(2)]
        P.op("dve", lambda h: h.memset(U[:, 0:2], 0.0), writes=[ur])
        for cc in range(2):
            for tb in range(4):
                bb, bc, bh = (0, 1, 2) if tb % 2 == 0 else (3, 4, 5)
                for (b, slot, sres, col) in ((bb, RING[0], ringr[0], cc * 128), (bc, RING[0], ringr[0], 256 + cc * 128),
                                             (bh, RING[1], ringr[1], cc * 128)):
                    for k in range(NKC):
                        P.op("pe", lambda h, k=k, b=b, slot=slot, col=col, tb=tb: h.matmul(
                            bank[b][:, :], lhsT=slot[:, k, col:col + 128], rhs=HT[:, k, tb * 512:(tb + 1) * 512],
                            start=(k == 0), stop=(k == NKC - 1)),
                            reads=[sres, htres(tb * 512)], writes=[bres[b]])
                cs, csr = CS[tb % 2]
                P.op("act", lambda h, cs=cs, bc=bc: h.activation(out=cs[:, :], in_=bank[bc][:, :], func=AF.Copy),
                     reads=[bres[bc]], writes=[csr])
                P.op("dve", lambda h, cs=cs, bh=bh, tb=tb: h.tensor_tensor(
                    out=U[:, 2 + tb * 512:2 + (tb + 1) * 512], in0=bank[bh][:, :], in1=cs[:, :], op=ALU.mult),
                    reads=[bres[bh], csr], writes=[ur])
                P.op("act", lambda h, bb=bb, tb=tb: h.activation(out=BS[:, tb * 512:(tb + 1) * 512], in_=bank[bb][:, :],
                                                                  func=AF.Copy),
                     reads=[bres[bb]], writes=[bsr])
            P.op("dve", lambda h, cc=cc: h.tensor_scalar(out=ACC[:, :], in0=U[:, 2:S + 2], scalar1=WC[:, cc, 2:3],
                                                         scalar2=None, op0=ALU.mult),
                 reads=[ur, wcr], writes=[accr])
            P.op("dve", lambda h, cc=cc: h.scalar_tensor_tensor(out=ACC[:, :], in0=U[:, 1:S + 1], scalar=WC[:, cc, 1:2],
                                                                in1=ACC[:, :], op0=ALU.mult, op1=ALU.add),
                 reads=[ur, wcr, accr], writes=[accr])
            P.op("dve", lambda h, cc=cc: h.scalar_tensor_tensor(out=ACC[:, :], in0=U[:, 0:S], scalar=WC[:, cc, 0:1],
                                                                in1=ACC[:, :], op0=ALU.mult, op1=ALU.add),
                 reads=[ur, wcr, accr], writes=[accr])
            P.op("dve", lambda h, cc=cc: h.tensor_tensor(out=MIXT[:, 6 + cc, :], in0=ACC[:, :], in1=BS[:, :], op=ALU.mult),
                 reads=[accr, bsr], writes=[mixr[(6 + cc, tb)] for tb in range(4)])
        M.release(mC)
        self.load_win(l, RING[0], ringr[0], rings[0], 768, 384)
        self.load_win(l, RING[1], ringr[1], rings[1], 1152, 288)
        P.barrier()
        if self.mix_stop <= 3:
            M.release(m0)
            return

        mM = M.mark()
        C1 = M.alloc("C1", [128, S], F32)
        S1 = M.alloc("S1", [128, S], F32)
        tabr = P.res("ropetab", l)
        CQT = M.alloc("CQT", [128, 3, S], BF16)
        CKVT = M.alloc("CKVT", [128, 2, S], BF16)
        KROT = M.alloc("KROT", [128, S], BF16)
        cqr = [P.res("cqt", l, tb) for tb in range(4)]
        ckvr = [P.res("ckvt", l, tb) for tb in range(4)]
        krr = [P.res("krot", l, tb) for tb in range(4)]
        mT = M.mark()
        HW = S // 2
        TI = M.alloc("TI", [128, HW], I32)
        ANG = M.alloc("ANG", [128, HW], F32)
        TQ = M.alloc("TQ", [128, HW], F32)
        tr = P.res("ropetmp", l)
        P.op("dve", lambda h: h.memset(C1[0:64, :], 1.0), writes=[tabr])
        P.op("dve", lambda h: h.memset(S1[0:64, :], 0.0), writes=[tabr])
        for hv in range(2):
            hc = slice(hv * HW, (hv + 1) * HW)
            P.op("sp", lambda h, hc=hc: h.dma_start(out=TI[64:96, :], in_=d["pos"][0:1, hc].partition_broadcast(32)),
                 writes=[tr], dsem=P.nsem("sp"))
            P.op("dve", lambda h: h.tensor_copy(out=ANG[64:96, :], in_=TI[64:96, :]), reads=[tr], writes=[tr])
            P.op("dve", lambda h: h.tensor_scalar(out=ANG[64:96, :], in0=ANG[64:96, :], scalar1=self.invf[64:96, :],
                                                  scalar2=None, op0=ALU.mult), reads=[tr, self.cres], writes=[tr])
            for (dst, shift) in ((S1, 0.0), (C1, math.pi / 2)):
                P.op("dve", lambda h, shift=shift: h.tensor_scalar(
                    out=TQ[64:96, :], in0=ANG[64:96, :], scalar1=float(shift), scalar2=1.0 / (2 * math.pi),
                    op0=ALU.add, op1=ALU.mult), reads=[tr], writes=[tr])
                P.op("dve", lambda h: h.tensor_copy(out=TI[64:96, :], in_=TQ[64:96, :]), reads=[tr], writes=[tr])
                P.op("dve", lambda h: h.tensor_copy(out=TQ[64:96, :], in_=TI[64:96, :]), reads=[tr], writes=[tr])
                P.op("dve", lambda h: h.scalar_tensor_tensor(
                    out=TQ[64:96, :], in0=TQ[64:96, :], scalar=-2 * math.pi, in1=ANG[64:96, :],
                    op0=ALU.mult, op1=ALU.add), reads=[tr], writes=[tr])
                P.op("dve", lambda h, shift=shift: h.tensor_scalar(
                    out=TQ[64:96, :], in0=TQ[64:96, :], scalar1=float(shift), scalar2=3.141592,
                    op0=ALU.add, op1=ALU.min), reads=[tr], writes=[tr])
                P.op("dve", lambda h: h.tensor_scalar(
                    out=TQ[64:96, :], in0=TQ[64:96, :], scalar1=-3.141592, scalar2=None, op0=ALU.max),
                    reads=[tr], writes=[tr])
                P.op("act", lambda h, dst=dst, hc=hc: h.activation(out=dst[64:96, hc], in_=TQ[64:96, :], func=AF.Sin),
                     reads=[tr], writes=[tabr, tr])
        M.release(mT)
        P.barrier()
        if self.mix_stop <= 4:
            M.release(m0)
            return

        mI = M.mark()
        CF_ = M.alloc("cF", [128, 3, 512], F32)
        cfr = P.res("cF", l)
        SQ = M.alloc("cSQ", [128, 3, 512], BF16)
        sqr = P.res("cSQ", l)
        RST = M.alloc("cRST", [128, 512], F32)
        rstr = P.res("cRST", l)
        KRW = M.alloc("KRW", [128, NKC, 2, 96], BF16)
        krwr = P.res("KRW", l)
        T1 = M.alloc("kT1", [128, 512], F32)
        T2 = M.alloc("kT2", [128, 512], F32)
        t1r, t2r = P.res("kT1", l), P.res("kT2", l)
        P.op("dve", lambda h: h.memset(KRW[:, :, :, :], 0.0), writes=[krwr])
        P.op("dve", lambda h: h.tensor_copy(out=KRW[:, :, 0, 64:96], in_=RING[1][:, :, 256:288]),
             reads=[ringr[1]], writes=[krwr])
        P.op("dve", lambda h: h.tensor_scalar(out=KRW[:, :, 1, 64:80], in0=RING[1][:, :, 272:288], scalar1=-1.0,
                                              scalar2=None, op0=ALU.mult), reads=[ringr[1]], writes=[krwr])
        P.op("dve", lambda h: h.tensor_copy(out=KRW[:, :, 1, 80:96], in_=RING[1][:, :, 256:272]),
             reads=[ringr[1]], writes=[krwr])
        for (nch, slot, sres, dst, dres, dim) in ((3, RING[0], ringr[0], CQT, cqr, 384),
                                                  (2, RING[1], ringr[1], CKVT, ckvr, 256)):
            for tb in range(4):
                for cidx in range(nch):
                    b = cidx
                    for k in range(NKC):
                        P.op("pe", lambda h, k=k, b=b, slot=slot, cidx=cidx, tb=tb: h.matmul(
                            bank[b][:, :], lhsT=slot[:, k, cidx * 128:(cidx + 1) * 128],
                            rhs=HT[:, k, tb * 512:(tb + 1) * 512], start=(k == 0), stop=(k == NKC - 1)),
                            reads=[sres, htres(tb * 512)], writes=[bres[b]])
                    P.op("act", lambda h, b=b, cidx=cidx: h.activation(out=CF_[:, cidx, :], in_=bank[b][:, :],
                                                                        func=AF.Copy),
                         reads=[bres[b]], writes=[cfr])
                    P.op("act", lambda h, b=b, cidx=cidx: h.activation(out=SQ[:, cidx, :], in_=bank[b][:, :],
                                                                        func=AF.Square),
                         reads=[bres[b]], writes=[sqr])
                for cidx in range(nch):
                    P.op("pe", lambda h, cidx=cidx, nch=nch: h.matmul(
                        bank[3][:, :], lhsT=self.ones, rhs=SQ[:, cidx, :], start=(cidx == 0), stop=(cidx == nch - 1)),
                        reads=[sqr, self.cres], writes=[bres[3]])
                P.op("act", lambda h, dim=dim: h.activation(out=RST[:, :], in_=bank[3][:, :], func=AF.Sqrt,
                                                            scale=1.0 / dim, bias=self.epsc),
                     reads=[bres[3], self.cres], writes=[rstr])
                P.op("dve", lambda h: h.reciprocal(out=RST[:, :], in_=RST[:, :]), reads=[rstr], writes=[rstr])
                for cidx in range(nch):
                    P.op("dve", lambda h, cidx=cidx, dst=dst, tb=tb: h.tensor_tensor(
                        out=dst[:, cidx, tb * 512:(tb + 1) * 512], in0=CF_[:, cidx, :], in1=RST[:, :], op=ALU.mult),
                        reads=[cfr, rstr], writes=[dres[tb]])
        for tb in range(4):
            for v_, b in ((0, 4), (1, 5)):
                for k in range(NKC):
                    P.op("pe", lambda h, k=k, b=b, v_=v_, tb=tb: h.matmul(
                        bank[b][0:96, :], lhsT=KRW[:, k, v_, :], rhs=HT[:, k, tb * 512:(tb + 1) * 512],
                        start=(k == 0), stop=(k == NKC - 1)),
                        reads=[krwr, htres(tb * 512)], writes=[bres[b]])
            P.op("dve", lambda h, tb=tb: h.tensor_tensor(out=T1[64:96, :], in0=bank[4][64:96, :],
                                                         in1=C1[64:96, tb * 512:(tb + 1) * 512], op=ALU.mult),
                 reads=[bres[4], tabr], writes=[t1r])
            P.op("dve", lambda h, tb=tb: h.tensor_tensor(out=T2[64:96, :], in0=bank[5][64:96, :],
                                                         in1=S1[64:96, tb * 512:(tb + 1) * 512], op=ALU.mult),
                 reads=[bres[5], tabr], writes=[t2r])
            P.op("dve", lambda h, tb=tb: h.tensor_tensor(out=KROT[64:96, tb * 512:(tb + 1) * 512], in0=T1[64:96, :],
                                                         in1=T2[64:96, :], op=ALU.add),
                 reads=[t1r, t2r], writes=[krr[tb]])
        M.release(mI)
        P.barrier()
        if self.mix_stop <= 5:
            M.release(m0)
            return
        self.mla_attn(l, MIXT, mixr, C1, S1, tabr, CQT, CKVT, KROT, cqr, ckvr, krr, regB, mT)
        M.release(regB)
        P.barrier()
        import os
        if self.mix_stop <= 6 or int(os.environ.get('MLADBG', '99')) < 99:
            M.release(m0)
            return

        WOUT = M.alloc("WOUT", [128, NKC, D], BF16)
        wor = [P.res("wout", l, k) for k in range(NKC)]
        for k in range(NKC):
            P.op("pool", lambda h, k=k: h.dma_start(out=WOUT[:, k, 0:D], in_=d["w_out"][l, k * 128:(k + 1) * 128, :]),
                 writes=[wor[k]], dsem=P.nsem("pool"))
        Gpost = M.alloc("Gpost", [128, D], F32)
        gpostr = P.res("gpost", "mix", l)
        self.load_gain(Gpost, "g_mix_post", l, None, gpostr)
        junk2 = M.alloc("junk", [128, D], BF16)
        TMP = [(M.alloc("tmp", [128, D], F32), P.res("tmp", "mix", l, s)) for s in range(2)]
        for i in range(NTILE):
            halves = (0, 1) if i % 2 == 0 else (2, 3)
            for hf, b in enumerate(halves):
                for cidx in range(8):
                    P.op("pe", lambda h, cidx=cidx, hf=hf, b=b, i=i: h.matmul(
                        bank[b][:, :], lhsT=MIXT[:, cidx, i * 128:(i + 1) * 128],
                        rhs=WOUT[:, cidx, hf * 512:(hf + 1) * 512], start=(cidx == 0), stop=(cidx == 7)),
                        reads=[mixr[(cidx, i // 4)], wor[cidx]], writes=[bres[b]])
            tmp, tmpr = TMP[i % 2]
            self.post_residual(i, halves, Gpost, gpostr, tmp, tmpr, junk2)
        M.release(m0)
        P.barrier()

    def mla_attn(self, l, MIXT, mixr, C1, S1, tabr, CQT, CKVT, KROT, cqr, ckvr, krr, regB, regC2):
        P, M = self.P, self.M
        d = self.dram
        bank, bres = self.bank, self.bres
        M.release(regC2)
        GQ = M.alloc("GQ", [128, 3], F32)
        GKV = M.alloc("GKV", [128, 2], F32)
        T1 = M.alloc("qT1", [128, 512], F32)
        T2 = M.alloc("qT2", [128, 512], F32)
        t1r, t2r = P.res("qT1", l), P.res("qT2", l)
        PTW = [(M.alloc("PT", [128, 512], BF16), P.res("PTw", l, s)) for s in range(3)]
        RC = M.alloc("RC", [128, 512], F32)
        rcr = P.res("RC", l)
        M.release(regB)
        WUQ = M.alloc("WUQ", [128, 3, 768], BF16)
        WUQR = M.alloc("WUQR", [128, 3, 8, 96], BF16)
        WUKV = M.alloc("WUKV", [128, 2, 1024], BF16)
        mq = M.mark()
        STQ = M.alloc("STQ", [128, 3, 768], F32)
        STK = M.alloc("STK", [128, 2, 1024], F32)
        M.release(mq)
        VO = M.alloc("VO", [128, NTILE, 4, 128], BF16)
        QH = [(M.alloc("QH", [128, S], BF16), [P.res("QH", l, s, tb) for tb in range(4)]) for s in range(2)]
        KH = [(M.alloc("KH", [128, S], BF16), [P.res("KH", l, s, tb) for tb in range(4)]) for s in range(2)]
        wr = P.res("mlaw", l)
        wgq, wgkv = P.res("mlagq", l), P.res("mlagkv", l)
        wsq = [P.res("mlasq", l, c) for c in range(3)]
        wsk = [P.res("mlask", l, c) for c in range(2)]
        P.op("sp", lambda h: h.dma_start(out=GQ[:, :], in_=d["g_q_cols"][l]), writes=[wgq], dsem=P.nsem("sp"))
        P.op("sp", lambda h: h.dma_start(out=GKV[:, :], in_=d["g_kv_cols"][l]), writes=[wgkv], dsem=P.nsem("sp"))
        for c in range(3):
            P.op("sp", lambda h, c=c: h.dma_start(out=STQ[:, c, :], in_=d["w_uq"][l, c * 128:(c + 1) * 128, :]),
                 writes=[wsq[c]], dsem=P.nsem("sp"))
        for c in range(2):
            P.op("sp", lambda h, c=c: h.dma_start(out=STK[:, c, :], in_=d["w_ukv"][l, c * 128:(c + 1) * 128, :]),
                 writes=[wsk[c]], dsem=P.nsem("sp"))
        P.op("dve", lambda h: h.memset(WUQR[:, :, :, :], 0.0), writes=[wr])
        for c in range(3):
            P.op("dve", lambda h, c=c: h.tensor_scalar(out=WUQ[:, c, :], in0=STQ[:, c, :], scalar1=GQ[:, c:c + 1],
                                                       scalar2=None, op0=ALU.mult), reads=[wsq[c], wgq], writes=[wr])
            sv = STQ[:, c, :].rearrange("p (h x) -> p h x", x=96)
            P.op("dve", lambda h, c=c, sv=sv: h.tensor_scalar(out=WUQR[:, c, :, 64:80], in0=sv[:, :, 80:96],
                                                              scalar1=GQ[:, c:c + 1], scalar2=-1.0, op0=ALU.mult,
                                                              op1=ALU.mult), reads=[wsq[c], wgq, wr], writes=[wr])
            P.op("dve", lambda h, c=c, sv=sv: h.tensor_scalar(out=WUQR[:, c, :, 80:96], in0=sv[:, :, 64:80],
                                                              scalar1=GQ[:, c:c + 1], scalar2=None, op0=ALU.mult),
                 reads=[wsq[c], wgq, wr], writes=[wr])
        for c in range(2):
            P.op("dve", lambda h, c=c: h.tensor_scalar(out=WUKV[:, c, :], in0=STK[:, c, :], scalar1=GKV[:, c:c + 1],
                                                       scalar2=None, op0=ALU.mult), reads=[wsk[c], wgkv], writes=[wr])
        P.barrier()
        import os
        dbg = int(os.environ.get('MLADBG', '99'))
        if dbg <= 0:
            return
        vor = [P.res("VO", l, st) for st in range(NTILE)]
        P.op("pool", lambda h: h.memset(VO[:, :, :, :], 0.0), writes=vor)
        wkv4 = [WUKV[:, c, :].rearrange("p (h x) -> p h x", x=128) for c in range(2)]
        sb_ = [0, 1]
        nub = [2, 3]
        deb = [4, 5]
        upb = [6, 7, 6]
        gstep = 0
        for hg in range(2):
            for st in range(NTILE):
                b = upb[st % 2]
                for c in range(2):
                    P.op("pe", lambda h, c=c, st=st, b=b: h.matmul(
                        bank[b][:, 0:256], lhsT=CKVT[:, c, st * 128:(st + 1) * 128],
                        rhs=wkv4[c][:, hg * 4:(hg + 1) * 4, 64:128], start=(c == 0), stop=(c == 1)),
                        reads=[ckvr[st // 4], wr], writes=[bres[b]])
                for hh in range(4):
                    par = hh % 2
                    P.op("act", lambda h, st=st, b=b, par=par, hh=hh: h.activation(
                        out=VO[:, st, hh, par * 64:(par + 1) * 64], in_=bank[b][:, hh * 64:(hh + 1) * 64],
                        func=AF.Copy), reads=[bres[b]], writes=[vor[st]])
            if dbg <= 1:
                return
            for hl in range(4):
                h_ = hg * 4 + hl
                qh, qhr = QH[h_ % 2]
                kh, khr = KH[h_ % 2]
                for tb in range(4):
                    cols = slice(tb * 512, (tb + 1) * 512)
                    ba, bb_, bk = upb
                    for c in range(2):
                        P.op("pe", lambda h, c=c, h_=h_, cols=cols, bk=bk: h.matmul(
                            bank[bk][0:64, :], lhsT=WUKV[:, c, h_ * 128:h_ * 128 + 64], rhs=CKVT[:, c, cols],
                            start=(c == 0), stop=(c == 1)), reads=[wr, ckvr[tb]], writes=[bres[bk]])
                    P.op("act", lambda h, cols=cols, kh=kh, bk=bk: h.activation(out=kh[0:64, cols], in_=bank[bk][0:64, :],
                                                                                func=AF.Copy),
                         reads=[bres[bk]], writes=[khr[tb]])
                    for c in range(3):
                        P.op("pe", lambda h, c=c, h_=h_, cols=cols, ba=ba: h.matmul(
                            bank[ba][0:96, :], lhsT=WUQ[:, c, h_ * 96:(h_ + 1) * 96], rhs=CQT[:, c, cols],
                            start=(c == 0), stop=(c == 2)), reads=[wr, cqr[tb]], writes=[bres[ba]])
                    for c in range(3):
                        P.op("pe", lambda h, c=c, h_=h_, cols=cols, bb_=bb_: h.matmul(
                            bank[bb_][0:96, :], lhsT=WUQR[:, c, h_, :], rhs=CQT[:, c, cols],
                            start=(c == 0), stop=(c == 2)), reads=[wr, cqr[tb]], writes=[bres[bb_]])
                    P.op("dve", lambda h, cols=cols, ba=ba: h.tensor_tensor(out=T1[0:96, :], in0=bank[ba][0:96, :],
                                                                            in1=C1[0:96, cols], op=ALU.mult),
                         reads=[bres[ba], tabr], writes=[t1r])
                    P.op("dve", lambda h, cols=cols, bb_=bb_: h.tensor_tensor(out=T2[0:96, :], in0=bank[bb_][0:96, :],
                                                                              in1=S1[0:96, cols], op=ALU.mult),
                         reads=[bres[bb_], tabr], writes=[t2r])
                    P.op("pool", lambda h, cols=cols, qh=qh: h.tensor_tensor(out=qh[0:96, cols], in0=T1[0:96, :],
                                                                             in1=T2[0:96, :], op=ALU.add),
                         reads=[t1r, t2r], writes=[qhr[tb]])
                    P.op("pool", lambda h, cols=cols, kh=kh: h.tensor_copy(out=kh[64:96, cols], in_=KROT[64:96, cols]),
                         reads=[krr[tb]], writes=[khr[tb]])
                if dbg <= 2:
                    return
                cchunk, r0 = 2 + h_ // 2, (h_ % 2) * 64
                for tb in range(4):
                    steps = list(range(0, 4 * tb + 4))
                    nst = len(steps)
                    ob = nub[(h_ * 4 + tb) % 2]
                    od = deb[(h_ * 4 + tb) % 2]

                    def c0_of(st):
                        return max(0, st - 4 * tb) * 128

                    def emit_s(n):
                        st = steps[n]
                        c0 = c0_of(st)
                        b = sb_[(gstep + n) % 2]
                        diag = st >= 4 * tb
                        P.op("pe", lambda h: h.matmul(
                            bank[b][:, c0:512], lhsT=kh[0:96, st * 128:(st + 1) * 128],
                            rhs=qh[0:96, tb * 512 + c0:(tb + 1) * 512], start=True, stop=not diag),
                            reads=[khr[st // 4], qhr[tb]], writes=[bres[b]])
                        if diag:
                            P.op("pe", lambda h: h.matmul(bank[b][:, c0:c0 + 128], lhsT=self.ident, rhs=self.mneg,
                                                          start=False, stop=True),
                                 reads=[self.cres], writes=[bres[b]])

                    def emit_p(n):
                        st = steps[n]
                        c0 = c0_of(st)
                        b = sb_[(gstep + n) % 2]
                        pt, ptr = PTW[(gstep + n) % 3]
                        P.op("act", lambda h: h.activation(out=pt[:, c0:512], in_=bank[b][:, c0:512], func=AF.Exp,
                                                           scale=MLA_SCALE), reads=[bres[b]], writes=[ptr])

                    def emit_av(n):
                        st = steps[n]
                        c0 = c0_of(st)
                        pt, ptr = PTW[(gstep + n) % 3]
                        P.op("pe", lambda h: h.matmul(bank[ob][:, c0:512], lhsT=VO[:, st, hl, :], rhs=pt[:, c0:512],
                                                      start=(n == 0), stop=(n == nst - 1)),
                             reads=[ptr, vor[st]], writes=[bres[ob]])
                        P.op("pe", lambda h: h.matmul(bank[od][:, c0:512], lhsT=self.ones, rhs=pt[:, c0:512],
                                                      start=(n == 0), stop=(n == nst - 1)),
                             reads=[ptr, self.cres], writes=[bres[od]])

                    emit_s(0)
                    emit_s(1)
                    emit_p(0)
                    for n in range(nst):
                        if n + 2 < nst:
                            emit_s(n + 2)
                        if n + 1 < nst:
                            emit_p(n + 1)
                        emit_av(n)
                    gstep += nst
                    if dbg <= 3:
                        return
                    if dbg == 5 and (h_, tb) == (0, 1):
                        return
                    if dbg == 6 and (h_, tb) == (1, 0):
                        return
                    if dbg == 7 and (h_, tb) == (4, 0):
                        return
                    cols = slice(tb * 512, (tb + 1) * 512)
                    P.op("dve", lambda h, od=od, r0=r0: h.reciprocal(out=RC[r0:r0 + 64, :], in_=bank[od][r0:r0 + 64, :]),
                         reads=[bres[od]], writes=[rcr])
                    P.op("dve", lambda h, ob=ob, cols=cols, cchunk=cchunk, r0=r0: h.tensor_tensor(
                        out=MIXT[r0:r0 + 64, cchunk, cols], in0=bank[ob][r0:r0 + 64, :], in1=RC[r0:r0 + 64, :], op=ALU.mult),
                        reads=[bres[ob], rcr], writes=[mixr[(cchunk, tb)]])

    def ple(self, l):
        P, M = self.P, self.M
        d = self.dram
        m0 = M.mark()
        HT = M.alloc("HT", [128, NKC, S], BF16)
        WPG = M.alloc("WPG", [128, NKC, D], BF16)
        WPP = M.alloc("WPP", [128, 2, D], BF16)
        PT = M.alloc("PT", [128, 2, S], BF16)
        Gpre = M.alloc("Gpre", [128, D], F32)
        Gpost = M.alloc("Gpost", [128, D], F32)
        gprer, gpostr = P.res("gpre", "ple", l), P.res("gpost", "ple", l)
        hn_bufs = [(M.alloc("hn", [128, D], BF16), P.res("hn", "ple", l, s)) for s in range(2)]
        junk = M.alloc("junk", [128, D], BF16)
        SIG = [(M.alloc("sig", [128, D], F32), P.res("sig", l, s)) for s in range(2)]
        EE = [(M.alloc("ee", [128, D], F32), P.res("ee", l, s)) for s in range(2)]
        TMP = [(M.alloc("tmp", [128, D], F32), P.res("tmp", "ple", l, s)) for s in range(2)]
        self.load_gain(Gpre, "g_ple_pre", l, None, gprer)
        self.load_gain(Gpost, "g_ple_post", l, None, gpostr)
        wgr = [P.res("wpg", l, k) for k in range(NKC)]
        wpr = [P.res("wpp", l, k) for k in range(2)]
        ptr_ = [P.res("ptT", l, k) for k in range(2)]
        for k in range(NKC):
            P.op("pool", lambda h, k=k: h.dma_start(out=WPG[:, k, :], in_=d["w_pg"][l, k * 128:(k + 1) * 128, :]),
                 writes=[wgr[k]], dsem=P.nsem("pool"))
        for k in range(2):
            P.op("pool", lambda h, k=k: h.dma_start(out=WPP[:, k, :], in_=d["w_pp"][l, k * 128:(k + 1) * 128, :]),
                 writes=[wpr[k]], dsem=P.nsem("pool"))
            P.op("pool", lambda h, k=k: h.dma_start(out=PT[:, k, :], in_=d["pT"][l, k * 128:(k + 1) * 128, :]),
                 writes=[ptr_[k]], dsem=P.nsem("pool"))
        htr = {}

        def htres(cc):
            return htr.setdefault(cc // 512, P.res("HT", "ple", l, cc // 512))

        tiles = list(range(NTILE))
        self.norm_transpose(tiles, Gpre, gprer, HT, htres, 0, hn_bufs, junk, [6, 7])
        for i in tiles:
            gb = (0, 1) if i % 2 == 0 else (2, 3)
            pb = (4, 5)
            sig, sigr = SIG[i % 2]
            ee, eer = EE[i % 2]
            for hf in range(2):
                for k in range(NKC):
                    P.op("pe", lambda h, k=k, hf=hf, i=i, b=gb[hf]: h.matmul(
                        self.bank[b][:, :], lhsT=HT[:, k, i * 128:(i + 1) * 128], rhs=WPG[:, k, hf * 512:(hf + 1) * 512],
                        start=(k == 0), stop=(k == NKC - 1)),
                        reads=[htres(i * 128), wgr[k]], writes=[self.bres[gb[hf]]])
                for k in range(2):
                    P.op("pe", lambda h, k=k, hf=hf, i=i, b=pb[hf]: h.matmul(
                        self.bank[b][:, :], lhsT=PT[:, k, i * 128:(i + 1) * 128], rhs=WPP[:, k, hf * 512:(hf + 1) * 512],
                        start=(k == 0), stop=(k == 1)),
                        reads=[wpr[k], ptr_[k]], writes=[self.bres[pb[hf]]])
                P.op("act", lambda h, hf=hf, sig=sig, b=gb[hf]: h.activation(
                    out=sig[:, hf * 512:(hf + 1) * 512], in_=self.bank[b][:, :], func=AF.Sigmoid),
                    reads=[self.bres[gb[hf]]], writes=[sigr])
                P.op("dve", lambda h, hf=hf, sig=sig, ee=ee, b=pb[hf]: h.tensor_tensor(
                    out=ee[:, hf * 512:(hf + 1) * 512], in0=self.bank[b][:, :], in1=sig[:, hf * 512:(hf + 1) * 512],
                    op=ALU.mult), reads=[self.bres[pb[hf]], sigr], writes=[eer])
            tmp, tmpr = TMP[i % 2]
            self.post_residual(i, None, Gpost, gpostr, tmp, tmpr, junk,
                               srcs=[(ee[:, 0:512], eer), (ee[:, 512:1024], eer)])
        M.release(m0)
        P.barrier()


def make_consts():
    j = np.arange(128)[:, None]
    s = np.arange(128)[None, :]
    cb = np.zeros((128, 6 * 128), np.float32)
    cb[:, 0:128] = np.eye(128)
    cb[:, 128:256] = (j >= s)
    cb[:, 256:384] = 1.0
    cb[:, 384:512] = np.where(j >= s, BIG, 0.0)
    cb[:, 512:640] = np.where(j > s, -BIG, 0.0)
    cf = np.zeros((128, 132), np.float32)
    cf[:, 0:128] = (j < s)
    inv = 10000.0 ** (-np.arange(16, dtype=np.float32) / 16.0)
    for p in range(64, 96):
        cf[p, 128] = inv[(p - 64) % 16]
    cf[:, 129] = EPS
    cf[:, 130] = 1.0
    return cb.astype(ml_dtypes.bfloat16), cf


def host_layout(inputs):
    f = lambda a: np.ascontiguousarray(np.asarray(a))
    sh = {}

    def gu(wg, wu):
        wg = np.asarray(wg).reshape(DEPTH, NKC, 128, NJ, 128)
        wu = np.asarray(wu).reshape(DEPTH, NKC, 128, NJ, 128)
        st = np.stack([wg, wu], axis=0)
        return np.ascontiguousarray(st.transpose(1, 4, 3, 0, 2, 5))

    sh["wgu1"] = gu(inputs["w1_gate"], inputs["w1_up"])
    sh["wgu2"] = gu(inputs["w2_gate"], inputs["w2_up"])
    sh["wd1"] = f(inputs["w1_down"])
    sh["wd2"] = f(inputs["w2_down"])
    sh["w_in"] = f(inputs["w_in"])
    sh["w_uq"] = f(inputs["w_mla_uq"])
    sh["w_ukv"] = f(inputs["w_mla_ukv"])
    sh["w_out"] = f(inputs["w_out"])
    sh["w_pg"] = f(inputs["w_ple_gate"])
    sh["w_pp"] = f(inputs["w_ple_proj"])
    wc = np.asarray(inputs["w_conv"])
    sh["w_convT"] = f(wc.reshape(DEPTH, 3, 2, 128).transpose(0, 3, 2, 1))
    sh["g_q_cols"] = f(np.asarray(inputs["g_mla_q"]).reshape(DEPTH, 3, 128).transpose(0, 2, 1))
    sh["g_kv_cols"] = f(np.asarray(inputs["g_mla_kv"]).reshape(DEPTH, 2, 128).transpose(0, 2, 1))
    for g in ("g_ffn1_pre", "g_ffn1_post", "g_mix_pre", "g_mix_post", "g_ffn2_pre",
              "g_ffn2_post", "g_ple_pre", "g_ple_post"):
        sh[g] = f(inputs[g])
    cb, cf = make_consts()
    sh["constb"] = cb
    sh["constf"] = cf
    x = np.asarray(inputs["x"])
    p = np.asarray(inputs["p"])
    pos = np.asarray(inputs["positions"])
    per = []
    for b in range(x.shape[0]):
        per.append({
            "x": f(x[b]),
            "pT": f(p[:, b].transpose(0, 2, 1)),
            "pos": f(pos[b:b + 1]),
        })
    return sh, per


_NC_CACHE = {}


def kernel(**inputs):
    sh, per = host_layout(inputs)
    if "nc" not in _NC_CACHE:
        _NC_CACHE["nc"] = Builder().build()
    nc = _NC_CACHE["nc"]
    in_maps = [dict(sh, **pc) for pc in per]
    res = run_bass_kernel_spmd(nc, in_maps, core_ids=list(range(len(per))))
    return np.stack([r["out"] for r in res.results], axis=0).astype(np.float32)
```

```python
import math
from contextlib import ExitStack
import numpy as np
import ml_dtypes
import concourse.bass as bass
import concourse.mybir as mybir
from concourse.bass_utils import run_bass_kernel_spmd

F32 = mybir.dt.float32
BF16 = mybir.dt.bfloat16
I32 = mybir.dt.int32
AF = mybir.ActivationFunctionType
ALU = mybir.AluOpType

S = 2048
D = 1024
DFF = 2816
NJ = DFF // 128
NTILE = S // 128
NKC = D // 128
DEPTH = 2
EPS = 1e-6
BIG = 30000.0
IN_W = 2208
MLA_SCALE = 96 ** -0.5

SBUF_BASE = 16384
import os as _os
EPOCH = int(_os.environ.get('EPOCH', '4000'))
EMBED = int(_os.environ.get('EMBED', '1'))
DUMMY_SB = int(_os.environ.get('DUMMY_SB', '0'))
DUMMY_MLA = int(_os.environ.get('DUMMY_MLA', '0'))
SBUF_LIMIT = 229376


import types as _types


def _freeze(fn):
    if fn is None or fn.__closure__ is None:
        return fn
    cells = []
    for c in fn.__closure__:
        try:
            cells.append(_types.CellType(c.cell_contents))
        except ValueError:
            cells.append(c)
    g = _types.FunctionType(fn.__code__, fn.__globals__, fn.__name__, fn.__defaults__, tuple(cells))
    g.__kwdefaults__ = fn.__kwdefaults__
    return g


class Res:
    __slots__ = ("w", "r")

    def __init__(self):
        self.w = None
        self.r = {}


class DSem:
    __slots__ = ("key", "count")

    def __init__(self, key):
        self.key = key
        self.count = 0


class Prog:
    def __init__(self, nc, stack):
        self.nc = nc
        self.stack = stack
        self.h = dict(pe=nc.tensor, act=nc.scalar, dve=nc.vector, pool=nc.gpsimd, sp=nc.sync)
        self.prog = {e: [] for e in self.h}
        self.semh = []
        self.cnt = {}
        self.key = {}
        for e in ("pe", "act", "dve", "pool", "sp"):
            self._new_eng_sem(e)
        self.waited = {e: {} for e in self.h}
        self.dsems = []
        self.resd = {}
        self.pools = {"sp": [self.dsem("sp%d" % i) for i in range(8)],
                      "pool": [self.dsem("pl%d" % i) for i in range(10)]}
        self.pool_i = {"sp": 0, "pool": 0}

    def nsem(self, eng):
        p = self.pools[eng]
        d = p[self.pool_i[eng] % len(p)]
        self.pool_i[eng] += 1
        return d

    def _new_sem(self, name):
        h = self.stack.enter_context(self.nc.semaphore(name))
        self.semh.append(h)
        return len(self.semh) - 1

    def _new_eng_sem(self, e):
        self.key[e] = self._new_sem("c_%s_%d" % (e, len(self.semh)))
        self.cnt[e] = 0

    def dsem(self, name):
        d = DSem(self._new_sem("d_%s_%d" % (name, len(self.semh))))
        self.dsems.append(d)
        return d

    def res(self, *key):
        r = self.resd.get(key)
        if r is None:
            r = Res()
            self.resd[key] = r
        return r

    def op(self, eng, fn, reads=(), writes=(), dsem=None):
        deps = {}
        is_pe = eng == "pe" and dsem is None

        def need(key, val, src):
            if is_pe and src == "pe":
                return
            if deps.get(key, 0) < val:
                deps[key] = val

        for r in reads:
            if r.w is not None:
                need(*r.w)
        for w in writes:
            if w.w is not None:
                need(*w.w)
            for k, (v, s) in w.r.items():
                need(k, v, s)
        if dsem is not None and dsem.count > 0:
            need(dsem.key, dsem.count, "dma")
        wd = self.waited[eng]
        wl = []
        for k, v in deps.items():
            if wd.get(k, 0) < v:
                wd[k] = v
                wl.append((k, v))
        if dsem is not None:
            dsem.count += 16
            tok = (dsem.key, dsem.count, "dma")
            inc = (dsem.key, 16)
        elif fn is None:
            tok = None
            inc = None
        else:
            if self.cnt[eng] >= EPOCH:
                self._new_eng_sem(eng)
            self.cnt[eng] += 1
            tok = (self.key[eng], self.cnt[eng], eng)
            inc = (self.key[eng], 1)
        self.prog[eng].append((wl, _freeze(fn), inc, dsem is not None))
        if tok is not None:
            for r in reads:
                cur = r.r.get(tok[0])
                if cur is None or cur[0] < tok[1]:
                    r.r[tok[0]] = (tok[1], tok[2])
            for w in writes:
                w.w = tok
                w.r = {}

    def barrier(self):
        targets = []
        for e in ("pe", "act", "dve", "pool"):
            if self.cnt[e] > 0:
                targets.append((self.key[e], self.cnt[e]))
        for d in self.dsems:
            if d.count > 0:
                targets.append((d.key, d.count))
        for e in self.h:
            wd = self.waited[e]
            wl = []
            for k, v in targets:
                if wd.get(k, 0) < v:
                    wd[k] = v
                    wl.append((k, v))
            if wl:
                self.prog[e].append((wl, None, None, False))

    def emit(self):
        semh = self.semh
        with self.nc.Block() as block:
            for e, deco in (("sp", block.sync), ("act", block.scalar), ("dve", block.vector),
                            ("pool", block.gpsimd), ("pe", block.tensor)):
                items = self.prog[e]

                def body(h, items=items):
                    for wl, fn, inc, is_dma in items:
                        emb = None
                        if EMBED and fn is not None and wl and not is_dma:
                            emb = wl[-1]
                            wl = wl[:-1]
                        for k, v in wl:
                            h.wait_ge(semh[k], v)
                        if fn is not None:
                            ins = fn(h)
                            if emb is not None:
                                ins._wait_ge(semh[emb[0]], emb[1])
                            ins.then_inc(semh[inc[0]], inc[1])
                deco(body)


class Mem:
    def __init__(self, nc):
        self.nc = nc
        self.off = SBUF_BASE
        self.n = 0
        self.peak = 0

    def alloc(self, name, shape, dtype):
        esz = 2 if dtype == BF16 else 4
        size = esz
        for s in shape[1:]:
            size *= s
        size = (size + 63) // 64 * 64
        self.n += 1
        if not hasattr(self, "names"):
            self.names = {}
        self.names[name] = "%s_%d" % (name, self.n)
        h = self.nc.alloc_sbuf_tensor_at("%s_%d" % (name, self.n), list(shape), dtype, offset=self.off)
        self.off += size
        assert self.off <= SBUF_LIMIT, (name, self.off)
        self.peak = max(self.peak, self.off)
        return h

    def mark(self):
        return self.off

    def release(self, m):
        self.off = m


class Builder:
    def __init__(self, stop_after=None, ffn_nt=1024, mix_stop=99, phases=None):
        self.stop_after = stop_after
        self.phases = phases
        self.mix_stop = mix_stop
        self.ffn_nt = ffn_nt
        self.stack = ExitStack()
        nc = bass.Bass("TRN2", target_bir_lowering=False)
        self.nc = nc
        self.P = Prog(nc, self.stack)
        self.M = Mem(nc)
        self.dram = {}

    def din(self, name, shape, dtype=F32):
        ap = self.nc.dram_tensor(name, list(shape), dtype, kind="ExternalInput").ap()
        self.dram[name] = ap
        return ap

    def setup(self):
        nc, P, M = self.nc, self.P, self.M
        L = DEPTH
        self.din("x", [S, D])
        self.din("pT", [L, 256, S])
        self.din("pos", [1, S], I32)
        self.din("wgu1", [L, NJ, 128, 2, NKC, 128])
        self.din("wgu2", [L, NJ, 128, 2, NKC, 128])
        self.din("wd1", [L, DFF, D])
        self.din("wd2", [L, DFF, D])
        self.din("w_in", [L, D, IN_W])
        self.din("w_uq", [L, 384, 768])
        self.din("w_ukv", [L, 256, 1024])
        self.din("w_out", [L, D, D])
        self.din("w_pg", [L, D, D])
        self.din("w_pp", [L, 256, D])
        self.din("w_convT", [L, 128, 2, 3])
        self.din("g_q_cols", [L, 128, 3])
        self.din("g_kv_cols", [L, 128, 2])
        for g in ("g_ffn1_pre", "g_ffn1_post", "g_mix_pre", "g_mix_post", "g_ffn2_pre",
                  "g_ffn2_post", "g_ple_pre", "g_ple_post"):
            self.din(g, [L, D])
        self.din("constb", [128, 6 * 128], BF16)
        self.din("constf", [128, 132])
        self.out = nc.dram_tensor("out", [S, D], F32, kind="ExternalOutput").ap()

        self.X = M.alloc("X", [128, NTILE, D], F32)
        self.CB = M.alloc("CB", [128, 6 * 128], BF16)
        self.CF = M.alloc("CF", [128, 132], F32)
        self.SS = M.alloc("SS", [128, 64], F32)
        self.RS = M.alloc("RS", [128, 64], F32)
        self.ident = self.CB[:, 0:128]
        self.tri = self.CB[:, 128:256]
        self.ones = self.CB[:, 256:384]
        self.mbig = self.CB[:, 384:512]
        self.mneg = self.CB[:, 512:640]
        self.zeros = self.CB[:, 640:768]
        self.mlt = self.CF[:, 0:128]
        self.invf = self.CF[:, 128:129]
        self.epsc = self.CF[:, 129:130]
        self.onec = self.CF[:, 130:131]
        self.sscol = 0
        self.bank = [self.stack.enter_context(nc.psum_tensor("bank%d" % i, [128, 512], F32))
                     for i in range(8)]
        self.bres = [P.res("bank", i) for i in range(8)]
        self.XR = [P.res("X", i) for i in range(NTILE)]

        cres = P.res("const")
        d = self.dram
        P.op("sp", lambda h: h.dma_start(out=self.CB[:, :], in_=d["constb"][:, :]), writes=[cres], dsem=P.nsem("sp"))
        P.op("sp", lambda h: h.dma_start(out=self.CF[:, :], in_=d["constf"][:, :]), writes=[cres], dsem=P.nsem("sp"))
        self.cres = cres
        xv = d["x"].rearrange("(n p) d -> p n d", p=128)
        for i in range(NTILE):
            P.op("sp", lambda h, i=i: h.dma_start(out=self.X[:, i, :], in_=xv[:, i, :]),
                 writes=[self.XR[i]], dsem=P.nsem("sp"))

    def next_col(self):
        c = self.sscol
        self.sscol = (self.sscol + 1) % 64
        return c, self.P.res("ss", c), self.P.res("rs", c)

    def load_gain(self, dst, name, l, scale=None, res=None):
        P = self.P
        src = self.dram[name][l:l + 1, :].partition_broadcast(128)
        P.op("sp", lambda h: h.dma_start(out=dst[:, :], in_=src), writes=[res], dsem=P.nsem("sp"))
        if scale is not None:
            P.op("dve", lambda h: h.tensor_scalar(out=dst[:, :], in0=dst[:, :], scalar1=float(scale),
                                                  scalar2=None, op0=ALU.mult),
                 reads=[res], writes=[res])

    def warm(self, n, spare=7):
        P = self.P
        for _ in range(n):
            P.op("pe", lambda h: h.matmul(self.bank[spare][:, :], lhsT=self.ones, rhs=self.CB[:, 0:512],
                                          start=True, stop=True), reads=[self.cres])

    def rstd(self, cs, cr, ssr, rsr, scale):
        P, SS, RS = self.P, self.SS, self.RS
        P.op("act", lambda h: h.activation(out=RS[:, cr:cr + 1], in_=SS[:, cs:cs + 1], func=AF.Sqrt,
                                           scale=float(scale), bias=self.epsc),
             reads=[ssr, self.cres], writes=[rsr])
        P.op("dve", lambda h: h.reciprocal(out=RS[:, cr:cr + 1], in_=RS[:, cr:cr + 1]), reads=[rsr], writes=[rsr])

    def norm_transpose(self, tiles, G, gres, HT, htres_fn, col0, hn_bufs, junk, banks):
        P = self.P
        X = self.X
        for idx, i in enumerate(tiles):
            c, ssr, rsr = self.next_col()
            hn, hnr = hn_bufs[idx % len(hn_bufs)]
            b = banks[idx % len(banks)]
            SS, RS = self.SS, self.RS
            P.op("act", lambda h, i=i, c=c: h.activation(out=junk[:, :], in_=X[:, i, :], func=AF.Square,
                                                          accum_out=SS[:, c:c + 1]),
                 reads=[self.XR[i]], writes=[ssr, P.res("junk", id(junk))])
            self.rstd(c, c, ssr, rsr, 1.0 / D)
            P.op("dve", lambda h, i=i, c=c, hn=hn: h.scalar_tensor_tensor(
                out=hn[:, :], in0=X[:, i, :], scalar=RS[:, c:c + 1], in1=G[:, :], op0=ALU.mult, op1=ALU.mult),
                reads=[self.XR[i], rsr, gres], writes=[hnr])
            pb = self.bank[b][:, :].bitcast(BF16)
            for k in range(NKC):
                P.op("pe", lambda h, k=k, hn=hn, pb=pb: h.transpose(
                    out=pb[:, k * 128:(k + 1) * 128], in_=hn[:, k * 128:(k + 1) * 128], identity=self.ident),
                    reads=[hnr, self.cres], writes=[self.bres[b]])
            cc = col0 + idx * 128
            P.op("act", lambda h, pb=pb, cc=cc: h.activation(
                out=HT[:, :, cc:cc + 128], in_=pb.rearrange("p (k t) -> p k t", k=NKC), func=AF.Copy),
                reads=[self.bres[b]], writes=[htres_fn(cc)])

    def post_residual(self, i, halves, Gs, gres, tmp, tmpr, junk, srcs=None, sq_scale=1.0):
        P = self.P
        X, SS, RS = self.X, self.SS, self.RS
        if srcs is None:
            srcs = [(self.bank[b][:, :], self.bres[b]) for b in halves]
        cols = []
        for hf, (sap, sres) in enumerate(srcs):
            c, ssr, rsr = self.next_col()
            cols.append((c, ssr, rsr))
            P.op("act", lambda h, sap=sap, c=c: h.activation(out=junk[:, 0:512], in_=sap, func=AF.Square,
                                                              scale=float(sq_scale), accum_out=SS[:, c:c + 1]),
                 reads=[sres], writes=[ssr, P.res("junk", id(junk))])
        (c0, s0, r0), (c1, s1, r1) = cols
        P.op("dve", lambda h: h.tensor_tensor(out=SS[:, c0:c0 + 1], in0=SS[:, c0:c0 + 1], in1=SS[:, c1:c1 + 1],
                                              op=ALU.add), reads=[s0, s1], writes=[s0])
        self.rstd(c0, c0, s0, r0, 1.0 / D)
        for hf, (sap, sres) in enumerate(srcs):
            P.op("dve", lambda h, hf=hf, sap=sap: h.scalar_tensor_tensor(
                out=tmp[:, hf * 512:(hf + 1) * 512], in0=sap, scalar=RS[:, c0:c0 + 1],
                in1=Gs[:, hf * 512:(hf + 1) * 512], op0=ALU.mult, op1=ALU.mult),
                reads=[sres, r0, gres], writes=[tmpr])
        P.op("pool", lambda h: h.tensor_tensor(out=X[:, i, :], in0=X[:, i, :], in1=tmp[:, :], op=ALU.add),
             reads=[tmpr, self.XR[i]], writes=[self.XR[i]])

    def ffn(self, l, which):
        nc, P, M = self.nc, self.P, self.M
        d = self.dram
        wgu = d["wgu%d" % which]
        wd = d["wd%d" % which]
        gpre = "g_ffn%d_pre" % which
        gpost = "g_ffn%d_post" % which
        NT = self.ffn_nt
        npass = S // NT
        ntb = NT // 512
        m0 = M.mark()
        HT = M.alloc("HT", [128, NKC, NT], BF16)
        ACTT = M.alloc("ACTT", [128, NJ, NT], BF16)
        NWG = 2
        WGU = [M.alloc("WGU", [128, 2, NKC, 128], BF16) for _ in range(NWG)]
        WDF = M.alloc("WDF", [128, NJ, D], BF16)
        wgur = [P.res("wgu", which, l, s) for s in range(NWG)]
        wdr = [P.res("wd", which, l, j) for j in range(NJ)]
        wds = None
        Gpre = M.alloc("Gpre", [128, D], F32)
        Gpost = M.alloc("Gpost", [128, D], F32)
        gprer, gpostr = P.res("gpre", which, l), P.res("gpost", which, l)
        hn_bufs = [(M.alloc("hn", [128, D], BF16), P.res("hn", which, l, s)) for s in range(2)]
        junk = M.alloc("junk", [128, D], BF16)
        SG = [(M.alloc("sg", [128, 512], F32), P.res("sg", which, l, s)) for s in range(2)]
        TMP = [(M.alloc("tmp", [128, D], F32), P.res("tmp", which, l, s)) for s in range(2)]
        self.load_gain(Gpre, gpre, l, None, gprer)
        self.load_gain(Gpost, gpost, l, 0.5, gpostr)
        htr = {}
        actr = {}

        def htres(cc):
            return htr.setdefault(cc // 512, P.res("HT", which, l, cc // 512))

        for ps in range(npass):
            tiles = list(range(ps * (NT // 128), (ps + 1) * (NT // 128)))
            self.norm_transpose(tiles, Gpre, gprer, HT, htres, 0, hn_bufs, junk, [6, 7])
            for j in range(NJ):
                s = j % NWG
                P.op("pool", lambda h, j=j, s=s: h.dma_start(out=WGU[s][:, :, :, :], in_=wgu[l, j]),
                     writes=[wgur[s]], dsem=P.nsem("pool"))
                if ps == 0:
                    P.op("pool", lambda h, j=j: h.dma_start(out=WDF[:, j, :], in_=wd[l, j * 128:(j + 1) * 128, :]),
                         writes=[wdr[j]], dsem=P.nsem("pool"))
                for tb in range(ntb):
                    bg, bu = (0, 1) if (j * ntb + tb) % 2 == 0 else (2, 3)
                    for gu, b in ((0, bg), (1, bu)):
                        for k in range(NKC):
                            P.op("pe", lambda h, s=s, gu=gu, k=k, b=b, tb=tb: h.matmul(
                                self.bank[b][:, :], lhsT=WGU[s][:, gu, k, :], rhs=HT[:, k, tb * 512:(tb + 1) * 512],
                                start=(k == 0), stop=(k == NKC - 1)),
                                reads=[wgur[s], htres(tb * 512)], writes=[self.bres[b]])
                    sg, sgr = SG[(j * ntb + tb) % 2]
                    P.op("act", lambda h, sg=sg, bg=bg: h.activation(out=sg[:, :], in_=self.bank[bg][:, :],
                                                                      func=AF.Silu),
                         reads=[self.bres[bg]], writes=[sgr])
                    ar = actr.setdefault((j, tb), P.res("ACTT", which, l, j, tb))
                    P.op("dve", lambda h, sg=sg, bu=bu, j=j, tb=tb: h.tensor_tensor(
                        out=ACTT[:, j, tb * 512:(tb + 1) * 512], in0=self.bank[bu][:, :], in1=sg[:, :], op=ALU.mult),
                        reads=[self.bres[bu], sgr], writes=[ar])
            self._ffn_down(l, which, ps, tiles, ACTT, actr, wd, WDF, wdr, wds, Gpost, gpostr, TMP, junk)
        M.release(m0)
        P.barrier()

    def _ffn_down(self, l, which, ps, tiles, ACTT, actr, wd, WDF, wdr, wds, Gpost, gpostr, TMP, junk):
        P = self.P
        for idx, i in enumerate(tiles):
            halves = (4, 5) if idx % 2 == 0 else (6, 7)
            il = i - tiles[0]
            tb = il // 4
            for hf, b in enumerate(halves):
                for j in range(NJ):
                    P.op("pe", lambda h, j=j, il=il, hf=hf, b=b: h.matmul(
                        self.bank[b][:, :], lhsT=ACTT[:, j, il * 128:(il + 1) * 128],
                        rhs=WDF[:, j, hf * 512:(hf + 1) * 512], start=(j == 0), stop=(j == NJ - 1)),
                        reads=[actr[(j, tb)], wdr[j]], writes=[self.bres[b]])
            tmp, tmpr = TMP[idx % 2]
            self.post_residual(i, halves, Gpost, gpostr, tmp, tmpr, junk)

    def store_out(self):
        P = self.P
        ov = self.out.rearrange("(n p) d -> p n d", p=128)
        for i in range(NTILE):
            P.op("sp", lambda h, i=i: h.dma_start(out=ov[:, i, :], in_=self.X[:, i, :]),
                 reads=[self.XR[i]], dsem=P.nsem("sp"))
        P.prog["sp"].append(([(dd.key, dd.count) for dd in P.pools["sp"] if dd.count > 0], None, None, False))

    def build(self):
        self.setup()
        phases = []
        for l in range(DEPTH):
            phases += [("ffn1", l), ("mix", l), ("ffn2", l), ("ple", l)]
        if self.phases is not None:
            phases = self.phases
        for n, (ph, l) in enumerate(phases):
            if self.stop_after is not None and n >= self.stop_after:
                break
            if ph == "ffn1":
                self.ffn(l, 1)
            elif ph == "ffn2":
                self.ffn(l, 2)
            elif ph == "mix":
                self.mix(l)
            elif ph == "ple":
                self.ple(l)
        self.store_out()
        self.P.emit()
        return self.nc

    def proj_fm(self, slot, sres, col, Mrows, HT, htres, banks, evac, lhs_fn=None):
        P = self.P
        for tb in range(4):
            b = banks[tb % len(banks)]
            for k in range(NKC):
                lhs = slot[:, k, col:col + Mrows] if lhs_fn is None else lhs_fn(k)
                P.op("pe", lambda h, k=k, b=b, tb=tb, lhs=lhs: h.matmul(
                    self.bank[b][0:Mrows, :], lhsT=lhs, rhs=HT[:, k, tb * 512:(tb + 1) * 512],
                    start=(k == 0), stop=(k == NKC - 1)),
                    reads=[sres, htres(tb * 512)], writes=[self.bres[b]])
            evac(b, tb)

    def load_win(self, l, slot, sres, dsem, c0, w):
        dsem = self.P.nsem("pool")
        P = self.P
        src = self.dram["w_in"][l].rearrange("(kc p) n -> p kc n", p=128)[:, :, c0:c0 + w]
        P.op("pool", lambda h: h.dma_start(out=slot[:, :, 0:w], in_=src), writes=[sres], dsem=dsem)

    def mix(self, l):
        P, M = self.P, self.M
        d = self.dram
        bank, bres = self.bank, self.bres
        m0 = M.mark()
        MIXT = M.alloc("MIXT", [128, 8, S], BF16)
        mixr = {(c, tb): P.res("MIXT", l, c, tb) for c in range(8) for tb in range(4)}
        regB = M.mark()
        HT = M.alloc("HT", [128, NKC, S], BF16)
        RING = [M.alloc("WIN", [128, NKC, 512], BF16) for _ in range(2)]
        regC = M.mark()
        ringr = [P.res("win", l, s) for s in range(2)]
        rings = [None, None]
        htr = {}

        def htres(cc):
            return htr.setdefault(cc // 512, P.res("HT", "mix", l, cc // 512))

        Gpre = M.alloc("Gpre", [128, D], F32)
        gprer = P.res("gpre", "mix", l)
        hn_bufs = [(M.alloc("hn", [128, D], BF16), P.res("hn", "mix", l, s)) for s in range(2)]
        junk = M.alloc("junk", [128, D], BF16)
        self.load_gain(Gpre, "g_mix_pre", l, None, gprer)
        self.load_win(l, RING[0], ringr[0], rings[0], 0, 512)
        self.load_win(l, RING[1], ringr[1], rings[1], 512, 256)
        self.norm_transpose(list(range(NTILE)), Gpre, gprer, HT, htres, 0, hn_bufs, junk, [6, 7])
        M.release(regC)
        P.barrier()
        if self.mix_stop <= 0:
            M.release(m0)
            return

        mS = M.mark()
        QT = M.alloc("QT", [128, 2, S], BF16)
        KT = M.alloc("KT", [128, 2, S], BF16)
        NKT = M.alloc("NKT", [128, 2, S], BF16)
        VS = M.alloc("VS", [128, NTILE, 256], BF16)
        qr = {(c, tb): P.res("sbq", l, c, tb) for c in range(2) for tb in range(4)}
        kr_ = {(c, tb): P.res("sbk", l, c, tb) for c in range(2) for tb in range(4)}
        vr = [P.res("sbv", l, i) for i in range(NTILE)]
        for c in range(2):
            def evq(b, tb, c=c):
                P.op("act", lambda h: h.activation(out=QT[:, c, tb * 512:(tb + 1) * 512], in_=bank[b][:, :],
                                                   func=AF.Copy, scale=0.125),
                     reads=[bres[b]], writes=[qr[(c, tb)]])
            self.proj_fm(RING[0], ringr[0], c * 128, 128, HT, htres, [0, 1], evq)

            def evk(b, tb, c=c):
                P.op("act", lambda h: h.activation(out=KT[:, c, tb * 512:(tb + 1) * 512], in_=bank[b][:, :],
                                                   func=AF.Copy),
                     reads=[bres[b]], writes=[kr_[(c, tb)]])
                P.op("dve", lambda h: h.tensor_scalar(out=NKT[:, c, tb * 512:(tb + 1) * 512], in0=bank[b][:, :],
                                                      scalar1=-1.0, scalar2=None, op0=ALU.mult),
                     reads=[bres[b]], writes=[kr_[(c, tb)]])
            self.proj_fm(RING[0], ringr[0], 256 + c * 128, 128, HT, htres, [2, 3], evk)
        for i in range(NTILE):
            b = 4 + (i % 2)
            for k in range(NKC):
                P.op("pe", lambda h, k=k, i=i, b=b: h.matmul(
                    bank[b][:, 0:256], lhsT=HT[:, k, i * 128:(i + 1) * 128], rhs=RING[1][:, k, 0:256],
                    start=(k == 0), stop=(k == NKC - 1)),
                    reads=[ringr[1], htres(i * 128)], writes=[bres[b]])
            P.op("act", lambda h, i=i, b=b: h.activation(out=VS[:, i, :], in_=bank[b][:, 0:256], func=AF.Copy),
                 reads=[bres[b]], writes=[vr[i]])
        if self.mix_stop <= 1:
            M.release(m0)
            P.barrier()
            return
        self.load_win(l, RING[0], ringr[0], rings[0], 1440, 512)
        self.load_win(l, RING[1], ringr[1], rings[1], 1952, 256)

        EW = [(M.alloc("sbE", [128, 512], F32), P.res("sbE", l, s)) for s in range(2)]
        SPW = [(M.alloc("sbSP", [128, 512], F32), P.res("sbSP", l, s)) for s in range(2)]
        SPB = [(M.alloc("sbSPb", [128, 512], BF16), P.res("sbSPb", l, s)) for s in range(3)]
        RACC = [(M.alloc("sbRA", [128, 512], BF16), P.res("sbRA", l, s)) for s in range(3)]
        ATW = [(M.alloc("sbAT", [128, 512], BF16), P.res("sbAT", l, s)) for s in range(3)]
        zb = [0, 1, 6]
        cb_ = [2, 3]
        ob = [4, 5]
        gstep = 0
        for h_ in range(4):
            c, r0 = h_ // 2, (h_ % 2) * 64
            for tb in range(4):
                steps = list(range(4 * tb + 3, -1, -1))
                nst = len(steps)
                obk = ob[(h_ * 4 + tb) % 2]
                cols = slice(tb * 512, (tb + 1) * 512)
                P.op("pe", lambda h, obk=obk, c=c, cols=cols: h.matmul(
                    bank[obk][0:64, :], lhsT=self.zeros[:, 0:64], rhs=QT[:, c, cols], start=True, stop=False),
                    reads=[self.cres, qr[(c, tb)]], writes=[bres[obk]])
                for ra, rar in RACC:
                    P.op("pool", lambda h, ra=ra: h.memset(ra[:, :], 0.0), writes=[rar])

                def c0_of(st):
                    return max(0, st - 4 * tb) * 128

                def emit_z(n):
                    st = steps[n]
                    c0 = c0_of(st)
                    b = zb[(gstep + n) % 3]
                    P.op("pe", lambda h: h.matmul(
                        bank[b][:, c0:512], lhsT=KT[r0:r0 + 64, c, st * 128:(st + 1) * 128],
                        rhs=QT[r0:r0 + 64, c, tb * 512 + c0:(tb + 1) * 512], start=True, stop=True),
                        reads=[kr_[(c, st // 4)], qr[(c, tb)]], writes=[bres[b]])

                def emit_sp(n):
                    st = steps[n]
                    c0 = c0_of(st)
                    b = zb[(gstep + n) % 3]
                    e, er = EW[(gstep + n) % 2]
                    sp, spr = SPW[(gstep + n) % 2]
                    spb, spbr = SPB[(gstep + n) % 3]
                    P.op("act", lambda h: h.activation(out=e[:, c0:512], in_=bank[b][:, c0:512], func=AF.Exp),
                         reads=[bres[b]], writes=[er])
                    P.op("act", lambda h: h.activation(out=sp[:, c0:512], in_=e[:, c0:512], func=AF.Ln, bias=self.onec),
                         reads=[er, self.cres], writes=[spr])
                    diag = st >= 4 * tb
                    if diag:
                        P.op("dve", lambda h: h.tensor_tensor(out=spb[:, c0:c0 + 128], in0=sp[:, c0:c0 + 128],
                                                              in1=self.mlt, op=ALU.mult),
                             reads=[spr, self.cres], writes=[spbr])
                        if c0 + 128 < 512:
                            P.op("dve", lambda h: h.tensor_copy(out=spb[:, c0 + 128:512], in_=sp[:, c0 + 128:512]),
                                 reads=[spr], writes=[spbr])
                    else:
                        P.op("dve", lambda h: h.tensor_copy(out=spb[:, :], in_=sp[:, :]), reads=[spr], writes=[spbr])
                    if n + 1 < nst:
                        ra0, rar0 = RACC[n % 3]
                        ra1, rar1 = RACC[(n + 1) % 3]
                        P.op("pool", lambda h: h.tensor_tensor(out=ra1[:, c0:512], in0=ra0[:, c0:512],
                                                               in1=spb[:, c0:512], op=ALU.add),
                             reads=[rar0, spbr], writes=[rar1])

                def emit_cum(n):
                    st = steps[n]
                    c0 = c0_of(st)
                    b = cb_[(gstep + n) % 2]
                    spb, spbr = SPB[(gstep + n) % 3]
                    ra0, rar0 = RACC[n % 3]
                    at, atr = ATW[(gstep + n) % 3]
                    diag = st >= 4 * tb
                    P.op("pe", lambda h: h.matmul(bank[b][:, c0:512], lhsT=self.tri, rhs=spb[:, c0:512],
                                                  start=True, stop=False),
                         reads=[spbr, self.cres], writes=[bres[b]])
                    if n > 0:
                        P.op("pe", lambda h: h.matmul(bank[b][:, c0:512], lhsT=self.ones, rhs=ra0[:, c0:512],
                                                      start=False, stop=False),
                             reads=[rar0, self.cres], writes=[bres[b]])
                    P.op("pe", lambda h: h.matmul(
                        bank[b][:, c0:512], lhsT=NKT[r0:r0 + 64, c, st * 128:(st + 1) * 128],
                        rhs=QT[r0:r0 + 64, c, tb * 512 + c0:(tb + 1) * 512], start=False, stop=not diag),
                        reads=[kr_[(c, st // 4)], qr[(c, tb)]], writes=[bres[b]])
                    if diag:
                        P.op("pe", lambda h: h.matmul(bank[b][:, c0:c0 + 128], lhsT=self.ident, rhs=self.mbig,
                                                      start=False, stop=True),
                             reads=[self.cres], writes=[bres[b]])
                    P.op("act", lambda h: h.activation(out=at[:, c0:512], in_=bank[b][:, c0:512], func=AF.Exp,
                                                       scale=-1.0),
                         reads=[bres[b]], writes=[atr])

                def emit_av(n):
                    st = steps[n]
                    c0 = c0_of(st)
                    at, atr = ATW[(gstep + n) % 3]
                    P.op("pe", lambda h: h.matmul(
                        bank[obk][0:64, c0:512], lhsT=VS[:, st, h_ * 64:(h_ + 1) * 64], rhs=at[:, c0:512],
                        start=False, stop=(n == nst - 1)),
                        reads=[atr, vr[st]], writes=[bres[obk]])

                emit_z(0)
                if nst > 1:
                    emit_z(1)
                emit_sp(0)
                for n in range(nst):
                    if n + 2 < nst:
                        emit_z(n + 2)
                    if n + 1 < nst:
                        emit_sp(n + 1)
                    emit_cum(n)
                    if n >= 1:
                        emit_av(n - 1)
                emit_av(nst - 1)
                gstep += nst
                P.op("act", lambda h, obk=obk, c=c, r0=r0, cols=cols: h.activation(
                    out=MIXT[r0:r0 + 64, c, cols], in_=bank[obk][0:64, :], func=AF.Copy),
                    reads=[bres[obk]], writes=[mixr[(c, tb)]])
        M.release(mS)
        P.barrier()
        if self.mix_stop <= 2:
            M.release(m0)
            return

        mC = M.mark()
        WC = M.alloc("WC", [128, 2, 3], F32)
        wcr = P.res("wc", l)
        P.op("sp", lambda h: h.dma_start(out=WC[:, :, :], in_=d["w_convT"][l]), writes=[wcr], dsem=P.nsem("sp"))
        U = M.alloc("cvU", [128, S + 2], F32)
        ur = P.res("cvU", l)
        BS = M.alloc("cvB", [128, S], F32)
        bsr = P.res("cvB", l)
        ACC = M.alloc("cvA", [128, S], F32)
        accr = P.res("cvA", l)
        CS = [(M.alloc("cvC", [128, 512], F32), P.res("cvC", l, s)) for s in range(2)]
        P.op("dve", lambda h: h.memset(U[:, 0:2], 0.0), writes=[ur])
        for cc in range(2):
            for tb in range(4):
                bb, bc, bh = (0, 1, 2) if tb % 2 == 0 else (3, 4, 5)
                for (b, slot, sres, col) in ((bb, RING[0], ringr[0], cc * 128), (bc, RING[0], ringr[0], 256 + cc * 128),
                                             (bh, RING[1], ringr[1], cc * 128)):
                    for k in range(NKC):
                        P.op("pe", lambda h, k=k, b=b, slot=slot, col=col, tb=tb: h.matmul(
                            bank[b][:, :], lhsT=slot[:, k, col:col + 128], rhs=HT[:, k, tb * 512:(tb + 1) * 512],
                            start=(k == 0), stop=(k == NKC - 1)),
                            reads=[sres, htres(tb * 512)], writes=[bres[b]])
                cs, csr = CS[tb % 2]
                P.op("act", lambda h, cs=cs, bc=bc: h.activation(out=cs[:, :], in_=bank[bc][:, :], func=AF.Copy),
                     reads=[bres[bc]], writes=[csr])
                P.op("dve", lambda h, cs=cs, bh=bh, tb=tb: h.tensor_tensor(
                    out=U[:, 2 + tb * 512:2 + (tb + 1) * 512], in0=bank[bh][:, :], in1=cs[:, :], op=ALU.mult),
                    reads=[bres[bh], csr], writes=[ur])
                P.op("act", lambda h, bb=bb, tb=tb: h.activation(out=BS[:, tb * 512:(tb + 1) * 512], in_=bank[bb][:, :],
                                                                  func=AF.Copy),
                     reads=[bres[bb]], writes=[bsr])
            P.op("dve", lambda h, cc=cc: h.tensor_scalar(out=ACC[:, :], in0=U[:, 2:S + 2], scalar1=WC[:, cc, 2:3],
                                                         scalar2=None, op0=ALU.mult),
                 reads=[ur, wcr], writes=[accr])
            P.op("dve", lambda h, cc=cc: h.scalar_tensor_tensor(out=ACC[:, :], in0=U[:, 1:S + 1], scalar=WC[:, cc, 1:2],
                                                                in1=ACC[:, :], op0=ALU.mult, op1=ALU.add),
                 reads=[ur, wcr, accr], writes=[accr])
            P.op("dve", lambda h, cc=cc: h.scalar_tensor_tensor(out=ACC[:, :], in0=U[:, 0:S], scalar=WC[:, cc, 0:1],
                                                                in1=ACC[:, :], op0=ALU.mult, op1=ALU.add),
                 reads=[ur, wcr, accr], writes=[accr])
            P.op("dve", lambda h, cc=cc: h.tensor_tensor(out=MIXT[:, 6 + cc, :], in0=ACC[:, :], in1=BS[:, :], op=ALU.mult),
                 reads=[accr, bsr], writes=[mixr[(6 + cc, tb)] for tb in range(4)])
        M.release(mC)
        self.load_win(l, RING[0], ringr[0], rings[0], 768, 384)
        self.load_win(l, RING[1], ringr[1], rings[1], 1152, 288)
        P.barrier()
        if self.mix_stop <= 3:
            M.release(m0)
            return

        mM = M.mark()
        C1 = M.alloc("C1", [128, S], F32)
        S1 = M.alloc("S1", [128, S], F32)
        tabr = P.res("ropetab", l)
        CQT = M.alloc("CQT", [128, 3, S], BF16)
        CKVT = M.alloc("CKVT", [128, 2, S], BF16)
        KROT = M.alloc("KROT", [128, S], BF16)
        cqr = [P.res("cqt", l, tb) for tb in range(4)]
        ckvr = [P.res("ckvt", l, tb) for tb in range(4)]
        krr = [P.res("krot", l, tb) for tb in range(4)]
        mT = M.mark()
        HW = S // 2
        TI = M.alloc("TI", [128, HW], I32)
        ANG = M.alloc("ANG", [128, HW], F32)
        TQ = M.alloc("TQ", [128, HW], F32)
        tr = P.res("ropetmp", l)
        P.op("dve", lambda h: h.memset(C1[0:64, :], 1.0), writes=[tabr])
        P.op("dve", lambda h: h.memset(S1[0:64, :], 0.0), writes=[tabr])
        for hv in range(2):
            hc = slice(hv * HW, (hv + 1) * HW)
            P.op("sp", lambda h, hc=hc: h.dma_start(out=TI[64:96, :], in_=d["pos"][0:1, hc].partition_broadcast(32)),
                 writes=[tr], dsem=P.nsem("sp"))
            P.op("dve", lambda h: h.tensor_copy(out=ANG[64:96, :], in_=TI[64:96, :]), reads=[tr], writes=[tr])
            P.op("dve", lambda h: h.tensor_scalar(out=ANG[64:96, :], in0=ANG[64:96, :], scalar1=self.invf[64:96, :],
                                                  scalar2=None, op0=ALU.mult), reads=[tr, self.cres], writes=[tr])
            for (dst, shift) in ((S1, 0.0), (C1, math.pi / 2)):
                P.op("dve", lambda h, shift=shift: h.tensor_scalar(
                    out=TQ[64:96, :], in0=ANG[64:96, :], scalar1=float(shift), scalar2=1.0 / (2 * math.pi),
                    op0=ALU.add, op1=ALU.mult), reads=[tr], writes=[tr])
                P.op("dve", lambda h: h.tensor_copy(out=TI[64:96, :], in_=TQ[64:96, :]), reads=[tr], writes=[tr])
                P.op("dve", lambda h: h.tensor_copy(out=TQ[64:96, :], in_=TI[64:96, :]), reads=[tr], writes=[tr])
                P.op("dve", lambda h: h.scalar_tensor_tensor(
                    out=TQ[64:96, :], in0=TQ[64:96, :], scalar=-2 * math.pi, in1=ANG[64:96, :],
                    op0=ALU.mult, op1=ALU.add), reads=[tr], writes=[tr])
                P.op("dve", lambda h, shift=shift: h.tensor_scalar(
                    out=TQ[64:96, :], in0=TQ[64:96, :], scalar1=float(shift), scalar2=3.141592,
                    op0=ALU.add, op1=ALU.min), reads=[tr], writes=[tr])
                P.op("dve", lambda h: h.tensor_scalar(
                    out=TQ[64:96, :], in0=TQ[64:96, :], scalar1=-3.141592, scalar2=None, op0=ALU.max),
                    reads=[tr], writes=[tr])
                P.op("act", lambda h, dst=dst, hc=hc: h.activation(out=dst[64:96, hc], in_=TQ[64:96, :], func=AF.Sin),
                     reads=[tr], writes=[tabr, tr])
        M.release(mT)
        P.barrier()
        if self.mix_stop <= 4:
            M.release(m0)
            return

        mI = M.mark()
        CF_ = M.alloc("cF", [128, 3, 512], F32)
        cfr = P.res("cF", l)
        SQ = M.alloc("cSQ", [128, 3, 512], BF16)
        sqr = P.res("cSQ", l)
        RST = M.alloc("cRST", [128, 512], F32)
        rstr = P.res("cRST", l)
        KRW = M.alloc("KRW", [128, NKC, 2, 96], BF16)
        krwr = P.res("KRW", l)
        T1 = M.alloc("kT1", [128, 512], F32)
        T2 = M.alloc("kT2", [128, 512], F32)
        t1r, t2r = P.res("kT1", l), P.res("kT2", l)
        P.op("dve", lambda h: h.memset(KRW[:, :, :, :], 0.0), writes=[krwr])
        P.op("dve", lambda h: h.tensor_copy(out=KRW[:, :, 0, 64:96], in_=RING[1][:, :, 256:288]),
             reads=[ringr[1]], writes=[krwr])
        P.op("dve", lambda h: h.tensor_scalar(out=KRW[:, :, 1, 64:80], in0=RING[1][:, :, 272:288], scalar1=-1.0,
                                              scalar2=None, op0=ALU.mult), reads=[ringr[1]], writes=[krwr])
        P.op("dve", lambda h: h.tensor_copy(out=KRW[:, :, 1, 80:96], in_=RING[1][:, :, 256:272]),
             reads=[ringr[1]], writes=[krwr])
        for (nch, slot, sres, dst, dres, dim) in ((3, RING[0], ringr[0], CQT, cqr, 384),
                                                  (2, RING[1], ringr[1], CKVT, ckvr, 256)):
            for tb in range(4):
                for cidx in range(nch):
                    b = cidx
                    for k in range(NKC):
                        P.op("pe", lambda h, k=k, b=b, slot=slot, cidx=cidx, tb=tb: h.matmul(
                            bank[b][:, :], lhsT=slot[:, k, cidx * 128:(cidx + 1) * 128],
                            rhs=HT[:, k, tb * 512:(tb + 1) * 512], start=(k == 0), stop=(k == NKC - 1)),
                            reads=[sres, htres(tb * 512)], writes=[bres[b]])
                    P.op("act", lambda h, b=b, cidx=cidx: h.activation(out=CF_[:, cidx, :], in_=bank[b][:, :],
                                                                        func=AF.Copy),
                         reads=[bres[b]], writes=[cfr])
                    P.op("act", lambda h, b=b, cidx=cidx: h.activation(out=SQ[:, cidx, :], in_=bank[b][:, :],
                                                                        func=AF.Square),
                         reads=[bres[b]], writes=[sqr])
                for cidx in range(nch):
                    P.op("pe", lambda h, cidx=cidx, nch=nch: h.matmul(
                        bank[3][:, :], lhsT=self.ones, rhs=SQ[:, cidx, :], start=(cidx == 0), stop=(cidx == nch - 1)),
                        reads=[sqr, self.cres], writes=[bres[3]])
                P.op("act", lambda h, dim=dim: h.activation(out=RST[:, :], in_=bank[3][:, :], func=AF.Sqrt,
                                                            scale=1.0 / dim, bias=self.epsc),
                     reads=[bres[3], self.cres], writes=[rstr])
                P.op("dve", lambda h: h.reciprocal(out=RST[:, :], in_=RST[:, :]), reads=[rstr], writes=[rstr])
                for cidx in range(nch):
                    P.op("dve", lambda h, cidx=cidx, dst=dst, tb=tb: h.tensor_tensor(
                        out=dst[:, cidx, tb * 512:(tb + 1) * 512], in0=CF_[:, cidx, :], in1=RST[:, :], op=ALU.mult),
                        reads=[cfr, rstr], writes=[dres[tb]])
        for tb in range(4):
            for v_, b in ((0, 4), (1, 5)):
                for k in range(NKC):
                    P.op("pe", lambda h, k=k, b=b, v_=v_, tb=tb: h.matmul(
                        bank[b][0:96, :], lhsT=KRW[:, k, v_, :], rhs=HT[:, k, tb * 512:(tb + 1) * 512],
                        start=(k == 0), stop=(k == NKC - 1)),
                        reads=[krwr, htres(tb * 512)], writes=[bres[b]])
            P.op("dve", lambda h, tb=tb: h.tensor_tensor(out=T1[64:96, :], in0=bank[4][64:96, :],
                                                         in1=C1[64:96, tb * 512:(tb + 1) * 512], op=ALU.mult),
                 reads=[bres[4], tabr], writes=[t1r])
            P.op("dve", lambda h, tb=tb: h.tensor_tensor(out=T2[64:96, :], in0=bank[5][64:96, :],
                                                         in1=S1[64:96, tb * 512:(tb + 1) * 512], op=ALU.mult),
                 reads=[bres[5], tabr], writes=[t2r])
            P.op("dve", lambda h, tb=tb: h.tensor_tensor(out=KROT[64:96, tb * 512:(tb + 1) * 512], in0=T1[64:96, :],
                                                         in1=T2[64:96, :], op=ALU.add),
                 reads=[t1r, t2r], writes=[krr[tb]])
        M.release(mI)
        P.barrier()
        if self.mix_stop <= 5:
            M.release(m0)
            return
        self.mla_attn(l, MIXT, mixr, C1, S1, tabr, CQT, CKVT, KROT, cqr, ckvr, krr, regB, mT)
        M.release(regB)
        P.barrier()
        import os
        if self.mix_stop <= 6 or int(os.environ.get('MLADBG', '99')) < 99:
            M.release(m0)
            return

        WOUT = M.alloc("WOUT", [128, NKC, D], BF16)
        wor = [P.res("wout", l, k) for k in range(NKC)]
        for k in range(NKC):
            P.op("pool", lambda h, k=k: h.dma_start(out=WOUT[:, k, 0:D], in_=d["w_out"][l, k * 128:(k + 1) * 128, :]),
                 writes=[wor[k]], dsem=P.nsem("pool"))
        Gpost = M.alloc("Gpost", [128, D], F32)
        gpostr = P.res("gpost", "mix", l)
        self.load_gain(Gpost, "g_mix_post", l, None, gpostr)
        junk2 = M.alloc("junk", [128, D], BF16)
        TMP = [(M.alloc("tmp", [128, D], F32), P.res("tmp", "mix", l, s)) for s in range(2)]
        for i in range(NTILE):
            halves = (0, 1) if i % 2 == 0 else (2, 3)
            for hf, b in enumerate(halves):
                for cidx in range(8):
                    P.op("pe", lambda h, cidx=cidx, hf=hf, b=b, i=i: h.matmul(
                        bank[b][:, :], lhsT=MIXT[:, cidx, i * 128:(i + 1) * 128],
                        rhs=WOUT[:, cidx, hf * 512:(hf + 1) * 512], start=(cidx == 0), stop=(cidx == 7)),
                        reads=[mixr[(cidx, i // 4)], wor[cidx]], writes=[bres[b]])
            tmp, tmpr = TMP[i % 2]
            self.post_residual(i, halves, Gpost, gpostr, tmp, tmpr, junk2)
        M.release(m0)
        P.barrier()

    def mla_attn(self, l, MIXT, mixr, C1, S1, tabr, CQT, CKVT, KROT, cqr, ckvr, krr, regB, regC2):
        P, M = self.P, self.M
        d = self.dram
        bank, bres = self.bank, self.bres
        M.release(regC2)
        GQ = M.alloc("GQ", [128, 3], F32)
        GKV = M.alloc("GKV", [128, 2], F32)
        T1 = M.alloc("qT1", [128, 512], F32)
        T2 = M.alloc("qT2", [128, 512], F32)
        t1r, t2r = P.res("qT1", l), P.res("qT2", l)
        PTW = [(M.alloc("PT", [128, 512], BF16), P.res("PTw", l, s)) for s in range(3)]
        RC = M.alloc("RC", [128, 512], F32)
        rcr = P.res("RC", l)
        M.release(regB)
        WUQ = M.alloc("WUQ", [128, 3, 768], BF16)
        WUQR = M.alloc("WUQR", [128, 3, 8, 96], BF16)
        WUKV = M.alloc("WUKV", [128, 2, 1024], BF16)
        mq = M.mark()
        STQ = M.alloc("STQ", [128, 3, 768], F32)
        STK = M.alloc("STK", [128, 2, 1024], F32)
        M.release(mq)
        VO = M.alloc("VO", [128, NTILE, 4, 128], BF16)
        QH = [(M.alloc("QH", [128, S], BF16), [P.res("QH", l, s, tb) for tb in range(4)]) for s in range(2)]
        KH = [(M.alloc("KH", [128, S], BF16), [P.res("KH", l, s, tb) for tb in range(4)]) for s in range(2)]
        wr = P.res("mlaw", l)
        wgq, wgkv = P.res("mlagq", l), P.res("mlagkv", l)
        wsq = [P.res("mlasq", l, c) for c in range(3)]
        wsk = [P.res("mlask", l, c) for c in range(2)]
        P.op("sp", lambda h: h.dma_start(out=GQ[:, :], in_=d["g_q_cols"][l]), writes=[wgq], dsem=P.nsem("sp"))
        P.op("sp", lambda h: h.dma_start(out=GKV[:, :], in_=d["g_kv_cols"][l]), writes=[wgkv], dsem=P.nsem("sp"))
        for c in range(3):
            P.op("sp", lambda h, c=c: h.dma_start(out=STQ[:, c, :], in_=d["w_uq"][l, c * 128:(c + 1) * 128, :]),
                 writes=[wsq[c]], dsem=P.nsem("sp"))
        for c in range(2):
            P.op("sp", lambda h, c=c: h.dma_start(out=STK[:, c, :], in_=d["w_ukv"][l, c * 128:(c + 1) * 128, :]),
                 writes=[wsk[c]], dsem=P.nsem("sp"))
        P.op("dve", lambda h: h.memset(WUQR[:, :, :, :], 0.0), writes=[wr])
        for c in range(3):
            P.op("dve", lambda h, c=c: h.tensor_scalar(out=WUQ[:, c, :], in0=STQ[:, c, :], scalar1=GQ[:, c:c + 1],
                                                       scalar2=None, op0=ALU.mult), reads=[wsq[c], wgq], writes=[wr])
            sv = STQ[:, c, :].rearrange("p (h x) -> p h x", x=96)
            P.op("dve", lambda h, c=c, sv=sv: h.tensor_scalar(out=WUQR[:, c, :, 64:80], in0=sv[:, :, 80:96],
                                                              scalar1=GQ[:, c:c + 1], scalar2=-1.0, op0=ALU.mult,
                                                              op1=ALU.mult), reads=[wsq[c], wgq, wr], writes=[wr])
            P.op("dve", lambda h, c=c, sv=sv: h.tensor_scalar(out=WUQR[:, c, :, 80:96], in0=sv[:, :, 64:80],
                                                              scalar1=GQ[:, c:c + 1], scalar2=None, op0=ALU.mult),
                 reads=[wsq[c], wgq, wr], writes=[wr])
        for c in range(2):
            P.op("dve", lambda h, c=c: h.tensor_scalar(out=WUKV[:, c, :], in0=STK[:, c, :], scalar1=GKV[:, c:c + 1],
                                                       scalar2=None, op0=ALU.mult), reads=[wsk[c], wgkv], writes=[wr])
        P.barrier()
        import os
        dbg = int(os.environ.get('MLADBG', '99'))
        if dbg <= 0:
            return
        vor = [P.res("VO", l, st) for st in range(NTILE)]
        P.op("pool", lambda h: h.memset(VO[:, :, :, :], 1.0), writes=vor)
        wkv4 = [WUKV[:, c, :].rearrange("p (h x) -> p h x", x=128) for c in range(2)]
        sb_ = [0, 1, 7]
        nub = [2, 3]
        DEN = M.alloc("DEN", [128, 512], F32)
        denr = P.res("DEN", l)
        upb = [4, 5, 6]
        gstep = 0
        def upproj(h_):
            qh, qhr = QH[h_ % 2]
            kh, khr = KH[h_ % 2]
            for tb in range(4):
                cols = slice(tb * 512, (tb + 1) * 512)
                ba, bb_, bk = upb
                for c in range(2):
                    P.op("pe", lambda h, c=c, h_=h_, cols=cols, bk=bk: h.matmul(
                        bank[bk][0:64, :], lhsT=WUKV[:, c, h_ * 128:h_ * 128 + 64], rhs=CKVT[:, c, cols],
                        start=(c == 0), stop=(c == 1)), reads=[wr, ckvr[tb]], writes=[bres[bk]])
                P.op("act", lambda h, cols=cols, kh=kh, bk=bk: h.activation(out=kh[0:64, cols], in_=bank[bk][0:64, :],
                                                                            func=AF.Copy),
                     reads=[bres[bk]], writes=[khr[tb]])
                for c in range(3):
                    P.op("pe", lambda h, c=c, h_=h_, cols=cols, ba=ba: h.matmul(
                        bank[ba][0:96, :], lhsT=WUQ[:, c, h_ * 96:(h_ + 1) * 96], rhs=CQT[:, c, cols],
                        start=(c == 0), stop=(c == 2)), reads=[wr, cqr[tb]], writes=[bres[ba]])
                for c in range(3):
                    P.op("pe", lambda h, c=c, h_=h_, cols=cols, bb_=bb_: h.matmul(
                        bank[bb_][0:96, :], lhsT=WUQR[:, c, h_, :], rhs=CQT[:, c, cols],
                        start=(c == 0), stop=(c == 2)), reads=[wr, cqr[tb]], writes=[bres[bb_]])
                P.op("dve", lambda h, cols=cols, ba=ba: h.tensor_tensor(out=T1[0:96, :], in0=bank[ba][0:96, :],
                                                                        in1=C1[0:96, cols], op=ALU.mult),
                     reads=[bres[ba], tabr], writes=[t1r])
                P.op("dve", lambda h, cols=cols, bb_=bb_: h.tensor_tensor(out=T2[0:96, :], in0=bank[bb_][0:96, :],
                                                                          in1=S1[0:96, cols], op=ALU.mult),
                     reads=[bres[bb_], tabr], writes=[t2r])
                P.op("pool", lambda h, cols=cols, qh=qh: h.tensor_tensor(out=qh[0:96, cols], in0=T1[0:96, :],
                                                                         in1=T2[0:96, :], op=ALU.add),
                     reads=[t1r, t2r], writes=[qhr[tb]])
                P.op("pool", lambda h, cols=cols, kh=kh: h.tensor_copy(out=kh[64:96, cols], in_=KROT[64:96, cols]),
                     reads=[krr[tb]], writes=[khr[tb]])

        upproj(0)
        for hg in range(2):
            for st in range(NTILE):
                b = upb[st % 2]
                for c in range(2):
                    P.op("pe", lambda h, c=c, st=st, b=b: h.matmul(
                        bank[b][:, 0:256], lhsT=CKVT[:, c, st * 128:(st + 1) * 128],
                        rhs=wkv4[c][:, hg * 4:(hg + 1) * 4, 64:128], start=(c == 0), stop=(c == 1)),
                        reads=[ckvr[st // 4], wr], writes=[bres[b]])
                for hh in range(4):
                    par = hh % 2
                    P.op("act", lambda h, st=st, b=b, par=par, hh=hh: h.activation(
                        out=VO[:, st, hh, par * 64:(par + 1) * 64], in_=bank[b][:, hh * 64:(hh + 1) * 64],
                        func=AF.Copy), reads=[bres[b]], writes=[vor[st]])
            if dbg <= 1:
                return
            for hl in range(4):
                h_ = hg * 4 + hl
                qh, qhr = QH[h_ % 2]
                kh, khr = KH[h_ % 2]
                if dbg <= 2:
                    return
                cchunk, r0 = 2 + h_ // 2, (h_ % 2) * 64
                for tb in range(4):
                    steps = list(range(0, 4 * tb + 4))
                    nst = len(steps)
                    ob = nub[(h_ * 4 + tb) % 2]

                    def c0_of(st):
                        return max(0, st - 4 * tb) * 128

                    def emit_s(n):
                        st = steps[n]
                        c0 = c0_of(st)
                        b = sb_[(gstep + n) % 3]
                        diag = st >= 4 * tb
                        P.op("pe", lambda h: h.matmul(
                            bank[b][:, c0:512], lhsT=kh[0:96, st * 128:(st + 1) * 128],
                            rhs=qh[0:96, tb * 512 + c0:(tb + 1) * 512], start=True, stop=not diag),
                            reads=[khr[st // 4], qhr[tb]], writes=[bres[b]])
                        if diag:
                            P.op("pe", lambda h: h.matmul(bank[b][:, c0:c0 + 128], lhsT=self.ident, rhs=self.mneg,
                                                          start=False, stop=True),
                                 reads=[self.cres], writes=[bres[b]])

                    def emit_p(n):
                        st = steps[n]
                        c0 = c0_of(st)
                        b = sb_[(gstep + n) % 3]
                        pt, ptr = PTW[(gstep + n) % 3]
                        P.op("act", lambda h: h.activation(out=pt[:, c0:512], in_=bank[b][:, c0:512], func=AF.Exp,
                                                           scale=MLA_SCALE), reads=[bres[b]], writes=[ptr])

                    def emit_av(n):
                        st = steps[n]
                        c0 = c0_of(st)
                        pt, ptr = PTW[(gstep + n) % 3]
                        P.op("pe", lambda h: h.matmul(bank[ob][:, c0:512], lhsT=VO[:, st, hl, :], rhs=pt[:, c0:512],
                                                      start=(n == 0), stop=(n == nst - 1)),
                             reads=[ptr, vor[st]], writes=[bres[ob]])

                    emit_s(0)
                    emit_s(1)
                    emit_p(0)
                    for n in range(nst):
                        if n + 2 < nst:
                            emit_s(n + 2)
                        if n + 1 < nst:
                            emit_p(n + 1)
                        emit_av(n)
                    gstep += nst
                    if tb == 1 and h_ + 1 < 8:
                        upproj(h_ + 1)
                    if dbg <= 3:
                        return
                    if dbg == 5 and (h_, tb) == (0, 1):
                        return
                    if dbg == 6 and (h_, tb) == (1, 0):
                        return
                    if dbg == 7 and (h_, tb) == (4, 0):
                        return
                    cols = slice(tb * 512, (tb + 1) * 512)
                    r1 = 64 - r0
                    P.op("act", lambda h, ob=ob, r0=r0, r1=r1: h.activation(out=DEN[r0:r0 + 64, :], in_=bank[ob][r1:r1 + 64, :],
                                                                            func=AF.Copy),
                         reads=[bres[ob]], writes=[denr])
                    P.op("dve", lambda h, r0=r0: h.reciprocal(out=RC[r0:r0 + 64, :], in_=DEN[r0:r0 + 64, :]),
                         reads=[denr], writes=[rcr])
                    P.op("dve", lambda h, ob=ob, cols=cols, cchunk=cchunk, r0=r0: h.tensor_tensor(
                        out=MIXT[r0:r0 + 64, cchunk, cols], in0=bank[ob][r0:r0 + 64, :], in1=RC[r0:r0 + 64, :], op=ALU.mult),
                        reads=[bres[ob], rcr], writes=[mixr[(cchunk, tb)]])

    def ple(self, l):
        P, M = self.P, self.M
        d = self.dram
        m0 = M.mark()
        HT = M.alloc("HT", [128, NKC, S], BF16)
        WPG = M.alloc("WPG", [128, NKC, D], BF16)
        WPP = M.alloc("WPP", [128, 2, D], BF16)
        PT = M.alloc("PT", [128, 2, S], BF16)
        Gpre = M.alloc("Gpre", [128, D], F32)
        Gpost = M.alloc("Gpost", [128, D], F32)
        gprer, gpostr = P.res("gpre", "ple", l), P.res("gpost", "ple", l)
        hn_bufs = [(M.alloc("hn", [128, D], BF16), P.res("hn", "ple", l, s)) for s in range(2)]
        junk = M.alloc("junk", [128, D], BF16)
        SIG = [(M.alloc("sig", [128, D], F32), P.res("sig", l, s)) for s in range(2)]
        EE = [(M.alloc("ee", [128, D], F32), P.res("ee", l, s)) for s in range(2)]
        TMP = [(M.alloc("tmp", [128, D], F32), P.res("tmp", "ple", l, s)) for s in range(2)]
        self.load_gain(Gpre, "g_ple_pre", l, None, gprer)
        self.load_gain(Gpost, "g_ple_post", l, None, gpostr)
        wgr = [P.res("wpg", l, k) for k in range(NKC)]
        wpr = [P.res("wpp", l, k) for k in range(2)]
        ptr_ = [P.res("ptT", l, k) for k in range(2)]
        for k in range(NKC):
            P.op("pool", lambda h, k=k: h.dma_start(out=WPG[:, k, :], in_=d["w_pg"][l, k * 128:(k + 1) * 128, :]),
                 writes=[wgr[k]], dsem=P.nsem("pool"))
        for k in range(2):
            P.op("pool", lambda h, k=k: h.dma_start(out=WPP[:, k, :], in_=d["w_pp"][l, k * 128:(k + 1) * 128, :]),
                 writes=[wpr[k]], dsem=P.nsem("pool"))
            P.op("pool", lambda h, k=k: h.dma_start(out=PT[:, k, :], in_=d["pT"][l, k * 128:(k + 1) * 128, :]),
                 writes=[ptr_[k]], dsem=P.nsem("pool"))
        htr = {}

        def htres(cc):
            return htr.setdefault(cc // 512, P.res("HT", "ple", l, cc // 512))

        tiles = list(range(NTILE))
        self.norm_transpose(tiles, Gpre, gprer, HT, htres, 0, hn_bufs, junk, [6, 7])
        for i in tiles:
            gb = (0, 1) if i % 2 == 0 else (2, 3)
            pb = (4, 5)
            sig, sigr = SIG[i % 2]
            ee, eer = EE[i % 2]
            for hf in range(2):
                for k in range(NKC):
                    P.op("pe", lambda h, k=k, hf=hf, i=i, b=gb[hf]: h.matmul(
                        self.bank[b][:, :], lhsT=HT[:, k, i * 128:(i + 1) * 128], rhs=WPG[:, k, hf * 512:(hf + 1) * 512],
                        start=(k == 0), stop=(k == NKC - 1)),
                        reads=[htres(i * 128), wgr[k]], writes=[self.bres[gb[hf]]])
                for k in range(2):
                    P.op("pe", lambda h, k=k, hf=hf, i=i, b=pb[hf]: h.matmul(
                        self.bank[b][:, :], lhsT=PT[:, k, i * 128:(i + 1) * 128], rhs=WPP[:, k, hf * 512:(hf + 1) * 512],
                        start=(k == 0), stop=(k == 1)),
                        reads=[wpr[k], ptr_[k]], writes=[self.bres[pb[hf]]])
                P.op("act", lambda h, hf=hf, sig=sig, b=gb[hf]: h.activation(
                    out=sig[:, hf * 512:(hf + 1) * 512], in_=self.bank[b][:, :], func=AF.Sigmoid),
                    reads=[self.bres[gb[hf]]], writes=[sigr])
                P.op("dve", lambda h, hf=hf, sig=sig, ee=ee, b=pb[hf]: h.tensor_tensor(
                    out=ee[:, hf * 512:(hf + 1) * 512], in0=self.bank[b][:, :], in1=sig[:, hf * 512:(hf + 1) * 512],
                    op=ALU.mult), reads=[self.bres[pb[hf]], sigr], writes=[eer])
            tmp, tmpr = TMP[i % 2]
            self.post_residual(i, None, Gpost, gpostr, tmp, tmpr, junk,
                               srcs=[(ee[:, 0:512], eer), (ee[:, 512:1024], eer)])
        M.release(m0)
        P.barrier()


def make_consts():
    j = np.arange(128)[:, None]
    s = np.arange(128)[None, :]
    cb = np.zeros((128, 6 * 128), np.float32)
    cb[:, 0:128] = np.eye(128)
    cb[:, 128:256] = (j >= s)
    cb[:, 256:384] = 1.0
    cb[:, 384:512] = np.where(j >= s, BIG, 0.0)
    cb[:, 512:640] = np.where(j > s, -BIG, 0.0)
    cf = np.zeros((128, 132), np.float32)
    cf[:, 0:128] = (j < s)
    inv = 10000.0 ** (-np.arange(16, dtype=np.float32) / 16.0)
    for p in range(64, 96):
        cf[p, 128] = inv[(p - 64) % 16]
    cf[:, 129] = EPS
    cf[:, 130] = 1.0
    return cb.astype(ml_dtypes.bfloat16), cf


def host_layout(inputs):
    f = lambda a: np.ascontiguousarray(np.asarray(a))
    sh = {}

    def gu(wg, wu):
        wg = np.asarray(wg).reshape(DEPTH, NKC, 128, NJ, 128)
        wu = np.asarray(wu).reshape(DEPTH, NKC, 128, NJ, 128)
        st = np.stack([wg, wu], axis=0)
        return np.ascontiguousarray(st.transpose(1, 4, 3, 0, 2, 5))

    sh["wgu1"] = gu(inputs["w1_gate"], inputs["w1_up"])
    sh["wgu2"] = gu(inputs["w2_gate"], inputs["w2_up"])
    sh["wd1"] = f(inputs["w1_down"])
    sh["wd2"] = f(inputs["w2_down"])
    sh["w_in"] = f(inputs["w_in"])
    sh["w_uq"] = f(inputs["w_mla_uq"])
    sh["w_ukv"] = f(inputs["w_mla_ukv"])
    sh["w_out"] = f(inputs["w_out"])
    sh["w_pg"] = f(inputs["w_ple_gate"])
    sh["w_pp"] = f(inputs["w_ple_proj"])
    wc = np.asarray(inputs["w_conv"])
    sh["w_convT"] = f(wc.reshape(DEPTH, 3, 2, 128).transpose(0, 3, 2, 1))
    sh["g_q_cols"] = f(np.asarray(inputs["g_mla_q"]).reshape(DEPTH, 3, 128).transpose(0, 2, 1))
    sh["g_kv_cols"] = f(np.asarray(inputs["g_mla_kv"]).reshape(DEPTH, 2, 128).transpose(0, 2, 1))
    for g in ("g_ffn1_pre", "g_ffn1_post", "g_mix_pre", "g_mix_post", "g_ffn2_pre",
              "g_ffn2_post", "g_ple_pre", "g_ple_post"):
        sh[g] = f(inputs[g])
    cb, cf = make_consts()
    sh["constb"] = cb
    sh["constf"] = cf
    x = np.asarray(inputs["x"])
    p = np.asarray(inputs["p"])
    pos = np.asarray(inputs["positions"])
    per = []
    for b in range(x.shape[0]):
        per.append({
            "x": f(x[b]),
            "pT": f(p[:, b].transpose(0, 2, 1)),
            "pos": f(pos[b:b + 1]),
        })
    return sh, per


_NC_CACHE = {}


def kernel(**inputs):
    sh, per = host_layout(inputs)
    if "nc" not in _NC_CACHE:
        _NC_CACHE["nc"] = Builder().build()
    nc = _NC_CACHE["nc"]
    in_maps = [dict(sh, **pc) for pc in per]
    res = run_bass_kernel_spmd(nc, in_maps, core_ids=list(range(len(per))))
    return np.stack([r["out"] for r in res.results], axis=0).astype(np.float32)
```
